# Optimizing a Trainium2 kernel written in Bass

```python
import jax, jax.numpy as jnp
from jax import lax
import numpy as np

D_MODEL = 2048
BATCH = 4
SEQ = 2048
DEPTH = 1
DEC_BATCH = 128
DEC_SEQ = 4
PAST_LEN = 2048
PAGE_SIZE = 128

HEAD_DIM = 128
SB_HEADS = D_MODEL // (2 * HEAD_DIM)
SB_WIDTH = SB_HEADS * HEAD_DIM
SB_LOGIT_OFFSET = -6.0
MEM_HEADS = 4
MEM_WIDTH = MEM_HEADS * HEAD_DIM
MEM_LEN = 256
CONV_WIDTH = D_MODEL // 4
CONV_TAPS = 31
CONV_HIST = CONV_TAPS - 1
N_BRANCH = 3
D_FF = 4 * D_MODEL
Q_BLOCK = 128
EPS = 1e-6
IN_COLS = 2 * CONV_WIDTH + 3 * SB_WIDTH + MEM_WIDTH + N_BRANCH * D_MODEL
SPLIT_POINTS = (CONV_WIDTH, 2 * CONV_WIDTH, 2 * CONV_WIDTH + SB_WIDTH, 2 * CONV_WIDTH + 2 * SB_WIDTH,
                2 * CONV_WIDTH + 3 * SB_WIDTH, 2 * CONV_WIDTH + 3 * SB_WIDTH + MEM_WIDTH)

kernel_name = 'gated_conformer_stickbreak_memory_decoder_step'


def rms_norm(x, g):
    xf = x.astype(jnp.float32)
    y = xf * lax.rsqrt(jnp.mean(xf * xf, axis=-1, keepdims=True) + EPS)
    return (y * g.astype(jnp.float32)).astype(x.dtype)


def layer_norm(x, g, b):
    xf = x.astype(jnp.float32)
    xc = xf - jnp.mean(xf, axis=-1, keepdims=True)
    y = xc * lax.rsqrt(jnp.mean(xc * xc, axis=-1, keepdims=True) + EPS)
    return (y * g.astype(jnp.float32) + b.astype(jnp.float32)).astype(x.dtype)


def depthwise_causal_conv(u_ext, w, b):
    y = lax.conv_general_dilated(u_ext, w[:, None, :].astype(u_ext.dtype), window_strides=(1,), padding='VALID',
                                 dimension_numbers=('NWC', 'WIO', 'NWC'), feature_group_count=u_ext.shape[-1])
    return y + b


def stick_breaking_block(q, k, v, q_pos, k_pos, b_sb):
    z = jnp.einsum('bqhd,bkhd->bhqk', q, k).astype(jnp.float32) * (HEAD_DIM ** -0.5)
    z = z + b_sb.astype(jnp.float32)[None, :, None, None]
    causal = k_pos[None, :] < q_pos[:, None]
    log_keep = jnp.where(causal, jax.nn.log_sigmoid(-z), 0.0)
    after = lax.cumsum(log_keep, axis=3, reverse=True) - log_keep
    a = jnp.where(causal, jnp.exp(jax.nn.log_sigmoid(z) + after), 0.0)
    return jnp.einsum('bhqk,bkhd->bqhd', a.astype(v.dtype), v)


def stick_breaking_sweep(q, k_all, v_all, past_len, b_sb):
    t = q.shape[1]
    outs = []
    for qs in range(0, t, Q_BLOCK):
        qe = min(t, qs + Q_BLOCK)
        kend = past_len + qe
        q_pos = past_len + jnp.arange(qs, qe)
        k_pos = jnp.arange(kend)
        outs.append(stick_breaking_block(q[:, qs:qe], k_all[:, :kend], v_all[:, :kend], q_pos, k_pos, b_sb))
    return jnp.concatenate(outs, axis=1)


def memory_kv(mem, g_mem, w_mem_kv, g_k_mem):
    b, m, _ = mem.shape
    kv = (rms_norm(mem, g_mem) @ w_mem_kv).reshape(b, m, 2, MEM_HEADS, HEAD_DIM)
    return rms_norm(kv[:, :, 0], g_k_mem), kv[:, :, 1]


def memory_attend(q, k, v):
    s = jnp.einsum('bqhd,bmhd->bhqm', q, k).astype(jnp.float32) * (HEAD_DIM ** -0.5)
    p = jax.nn.softmax(s, axis=-1)
    return jnp.einsum('bhqm,bmhd->bqhd', p.astype(v.dtype), v)


def trunk_layer(x, conv_hist, past_k, past_v, mem_k, mem_v, p):
    b, t, _ = x.shape
    past_len = past_k.shape[1]
    h = rms_norm(x, p['g_mix'])
    proj = h @ p['w_in']
    u_lin, u_gate, q_sb, k_sb, v_sb, q_mem, gate_logits = jnp.split(proj, SPLIT_POINTS, axis=-1)
    u = u_lin * jax.nn.sigmoid(u_gate)
    u_ext = jnp.concatenate([conv_hist, u], axis=1)
    c = depthwise_causal_conv(u_ext, p['w_dw'], p['b_dw'])
    c = jax.nn.silu(layer_norm(c, p['g_conv_ln'], p['b_conv_ln']))
    y_conv = c @ p['w_conv_out']
    q = rms_norm(q_sb.reshape(b, t, SB_HEADS, HEAD_DIM), p['g_q_sb'])
    k = rms_norm(k_sb.reshape(b, t, SB_HEADS, HEAD_DIM), p['g_k_sb'])
    v = v_sb.reshape(b, t, SB_HEADS, HEAD_DIM)
    k_all = jnp.concatenate([past_k, k], axis=1)
    v_all = jnp.concatenate([past_v, v], axis=1)
    o = stick_breaking_sweep(q, k_all, v_all, past_len, p['b_sb'])
    y_sb = o.reshape(b, t, SB_WIDTH) @ p['w_sb_out']
    qm = rms_norm(q_mem.reshape(b, t, MEM_HEADS, HEAD_DIM), p['g_q_mem'])
    y_mem = memory_attend(qm, mem_k, mem_v).reshape(b, t, MEM_WIDTH) @ p['w_mem_out']
    gates = jax.nn.sigmoid((gate_logits + p['b_gate']).astype(jnp.float32)).astype(x.dtype)
    gates = gates.reshape(b, t, N_BRANCH, D_MODEL)
    merged = gates[:, :, 0] * y_conv + gates[:, :, 1] * y_sb + gates[:, :, 2] * y_mem
    x = x + merged @ p['w_o']
    h2 = rms_norm(x, p['g_mlp'])
    x = x + jnp.square(jax.nn.relu(h2 @ p['w_up'])) @ p['w_down']
    return x, u_ext[:, -CONV_HIST:], k, v


def setup_inputs(seed: int = 0) -> dict:
    key = jax.random.key(seed)
    ks = jax.random.split(key, 40)

    def nrm(i, shape, scale=1.0):
        return scale * jax.random.normal(ks[i], shape, jnp.float32)

    def gain(i, shape):
        return 1.0 + nrm(i, shape, 0.02)

    n_pages = PAST_LEN // PAGE_SIZE
    n_used = DEC_BATCH * n_pages
    n_pool = n_used + (n_used + 3) // 4
    page_table = jax.random.permutation(ks[0], n_pool)[:n_used].reshape(DEC_BATCH, n_pages).astype(jnp.int32)
    return {
        'x_prompt': nrm(1, (BATCH, SEQ, D_MODEL)),
        'x_sample': nrm(2, (DEC_BATCH, DEC_SEQ, D_MODEL)),
        'mem_prompt': nrm(3, (BATCH, MEM_LEN, D_MODEL)),
        'cache_sb_k': nrm(4, (DEPTH, n_pool, PAGE_SIZE, SB_HEADS, HEAD_DIM)),
        'cache_sb_v': nrm(5, (DEPTH, n_pool, PAGE_SIZE, SB_HEADS, HEAD_DIM)),
        'page_table': page_table,
        'state_conv': nrm(6, (DEPTH, DEC_BATCH, CONV_HIST, CONV_WIDTH), 0.5),
        'cache_mem_k': nrm(7, (DEPTH, DEC_BATCH, MEM_LEN, MEM_HEADS, HEAD_DIM)),
        'cache_mem_v': nrm(8, (DEPTH, DEC_BATCH, MEM_LEN, MEM_HEADS, HEAD_DIM)),
        'g_mix': gain(9, (DEPTH, D_MODEL)),
        'w_in': nrm(10, (DEPTH, D_MODEL, IN_COLS), D_MODEL ** -0.5),
        'b_gate': nrm(11, (DEPTH, N_BRANCH * D_MODEL), 0.01),
        'w_dw': nrm(12, (DEPTH, CONV_TAPS, CONV_WIDTH), CONV_TAPS ** -0.5),
        'b_dw': nrm(13, (DEPTH, CONV_WIDTH), 0.01),
        'g_conv_ln': gain(14, (DEPTH, CONV_WIDTH)),
        'b_conv_ln': nrm(15, (DEPTH, CONV_WIDTH), 0.01),
        'w_conv_out': nrm(16, (DEPTH, CONV_WIDTH, D_MODEL), CONV_WIDTH ** -0.5),
        'g_q_sb': gain(17, (DEPTH, HEAD_DIM)),
        'g_k_sb': gain(18, (DEPTH, HEAD_DIM)),
        'b_sb': SB_LOGIT_OFFSET + nrm(29, (DEPTH, SB_HEADS), 0.1),
        'w_sb_out': nrm(19, (DEPTH, SB_WIDTH, D_MODEL), SB_WIDTH ** -0.5),
        'g_mem': gain(20, (DEPTH, D_MODEL)),
        'w_mem_kv': nrm(21, (DEPTH, D_MODEL, 2 * MEM_WIDTH), D_MODEL ** -0.5),
        'g_q_mem': gain(22, (DEPTH, HEAD_DIM)),
        'g_k_mem': gain(23, (DEPTH, HEAD_DIM)),
        'w_mem_out': nrm(24, (DEPTH, MEM_WIDTH, D_MODEL), MEM_WIDTH ** -0.5),
        'w_o': nrm(25, (DEPTH, D_MODEL, D_MODEL), D_MODEL ** -0.5),
        'g_mlp': gain(26, (DEPTH, D_MODEL)),
        'w_up': nrm(27, (DEPTH, D_MODEL, D_FF), D_MODEL ** -0.5),
        'w_down': nrm(28, (DEPTH, D_FF, D_MODEL), D_FF ** -0.5),
    }


def reference(x_prompt, x_sample, mem_prompt, cache_sb_k, cache_sb_v, page_table, state_conv, cache_mem_k,
              cache_mem_v, g_mix, w_in, b_gate, w_dw, b_dw, g_conv_ln, b_conv_ln, w_conv_out, g_q_sb, g_k_sb,
              b_sb, w_sb_out, g_mem, w_mem_kv, g_q_mem, g_k_mem, w_mem_out, w_o, g_mlp, w_up, w_down):
    n_pages = PAST_LEN // PAGE_SIZE
    past_len = n_pages * PAGE_SIZE
    bp = x_prompt.shape[0]
    bs = x_sample.shape[0]
    zero_hist = jnp.zeros((bp, CONV_HIST, CONV_WIDTH), x_prompt.dtype)
    no_past = jnp.zeros((bp, 0, SB_HEADS, HEAD_DIM), x_prompt.dtype)
    yp, ys = x_prompt, x_sample
    kp_l, vp_l, ks_l, vs_l, cp_l, cs_l, mk_l, mv_l = [], [], [], [], [], [], [], []
    for l in range(DEPTH):
        p = {'g_mix': g_mix[l], 'w_in': w_in[l], 'b_gate': b_gate[l], 'w_dw': w_dw[l], 'b_dw': b_dw[l],
             'g_conv_ln': g_conv_ln[l], 'b_conv_ln': b_conv_ln[l], 'w_conv_out': w_conv_out[l],
             'g_q_sb': g_q_sb[l], 'g_k_sb': g_k_sb[l], 'b_sb': b_sb[l], 'w_sb_out': w_sb_out[l],
             'g_q_mem': g_q_mem[l], 'w_mem_out': w_mem_out[l], 'w_o': w_o[l], 'g_mlp': g_mlp[l],
             'w_up': w_up[l], 'w_down': w_down[l]}
        mk, mv = memory_kv(mem_prompt, g_mem[l], w_mem_kv[l], g_k_mem[l])
        yp, cp, kp, vp = trunk_layer(yp, zero_hist, no_past, no_past, mk, mv, p)
        past_k = cache_sb_k[l][page_table].reshape(bs, past_len, SB_HEADS, HEAD_DIM)
        past_v = cache_sb_v[l][page_table].reshape(bs, past_len, SB_HEADS, HEAD_DIM)
        ys, cs, ks_new, vs_new = trunk_layer(ys, state_conv[l], past_k, past_v, cache_mem_k[l], cache_mem_v[l], p)
        kp_l.append(kp); vp_l.append(vp); ks_l.append(ks_new); vs_l.append(vs_new)
        cp_l.append(cp); cs_l.append(cs); mk_l.append(mk); mv_l.append(mv)
    return (yp, ys, jnp.stack(kp_l), jnp.stack(vp_l), jnp.stack(ks_l), jnp.stack(vs_l),
            jnp.stack(cp_l), jnp.stack(cs_l), jnp.stack(mk_l), jnp.stack(mv_l))
```

```python
import numpy as np
from contextlib import ExitStack
import concourse.bass as bass
import concourse.mybir as mybir
from concourse.bass_utils import run_bass_kernel_spmd

F32 = mybir.dt.float32
BF16 = mybir.dt.bfloat16
I32 = mybir.dt.int32
AF = mybir.ActivationFunctionType
ALU = mybir.AluOpType

D = 2048
KD = 16
NH = 8
HD = 128
CW = 512
MH = 4
ML = 256
DFF = 8192
TP = 1024
TS = 64
NT = TP + TS
NTP = TP + 128
NSAMP = 16
NPG = 16
INC = 10752
SCALE = HD ** -0.5
EPS = 1e-6
GROUPS = [(0, 512), (512, 512), (1024, 64)]
C_IDENT, C_TINC, C_LSTR, C_ONES, C_MASKD, C_M64, C_END = 0, 128, 256, 384, 512, 2560, 3072
V_BGATE, V_WDW, V_BDW, V_GCLN, V_BCLN, V_GQSB, V_GKSB, V_GQM, V_GKM, V_ROWID, V_PM, V_END = \
    0, 48, 172, 176, 180, 184, 185, 186, 187, 188, 189, 190


class Buf:
    __slots__ = ("name", "w", "r", "xr")

    def __init__(self, name, fence=None, xr=False):
        self.name = name
        self.w = None
        self.r = dict(fence) if fence else {}
        self.xr = xr


class Eng:
    def __init__(self, name, eng, sem, self_sync):
        self.name, self.eng, self.sem, self.self_sync = name, eng, sem, self_sync
        self.cnt = 0
        self.known = {}


class TB:
    __slots__ = ("t", "b")

    def __init__(self, t, b):
        self.t, self.b = t, b


class Trk:
    NDMA = 16

    def __init__(self, nc, stack):
        self.nc = nc
        mk = lambda n: stack.enter_context(nc.semaphore(n))
        self.pe = Eng("pe", nc.tensor, mk("s_pe"), False)
        self.dve = Eng("dve", nc.vector, mk("s_dve"), True)
        self.act = Eng("act", nc.scalar, mk("s_act"), True)
        self.pool = Eng("pool", nc.gpsimd, mk("s_pool"), True)
        self.sp = Eng("sp", nc.sync, mk("s_sp"), False)
        self.dsem = {q: [[mk(f"d_{q}{i}"), 0] for i in range(self.NDMA)] for q in ("sp", "pool")}
        self.dnext = {"sp": 0, "pool": 0}
        self.fence = {}

    def newbuf(self, name, xr=False):
        return Buf(name, self.fence, xr)

    def release(self, bufs):
        for b in bufs:
            evs = list(b.r.values())
            if b.w is not None:
                evs.append(b.w)
            for sem, val in evs:
                k = id(sem)
                if k not in self.fence or self.fence[k][1] < val:
                    self.fence[k] = (sem, val)

    def _wait(self, E, evs):
        best = {}
        for ev in evs:
            if ev is None:
                continue
            sem, val = ev
            k = id(sem)
            if k not in best or best[k][1] < val:
                best[k] = (sem, val)
        for k, (sem, val) in best.items():
            if sem is E.sem and not E.self_sync:
                continue
            if E.known.get(k, 0) >= val:
                continue
            E.eng.wait_ge(sem, val)
            E.known[k] = val

    @staticmethod
    def _deps(reads, writes, own=None):
        evs = []
        for b in reads:
            evs.append(b.w)
            if b.xr:
                evs.extend(e for e in b.r.values() if e[0] is not own)
        for b in writes:
            evs.append(b.w)
            evs.extend(b.r.values())
        return evs

    @staticmethod
    def _mark(ev, reads, writes):
        k = id(ev[0])
        for b in reads:
            o = b.r.get(k)
            if o is None or o[1] < ev[1]:
                b.r[k] = ev
        for b in writes:
            b.w = ev
            b.r = {}

    def op(self, E, fn, reads=(), writes=()):
        self._wait(E, self._deps(reads, writes, E.sem))
        inst = fn()
        E.cnt += 1
        inst.then_inc(E.sem, 1)
        self._mark((E.sem, E.cnt), reads, writes)
        return inst

    def dma(self, E, fn, reads=(), writes=()):
        slots = self.dsem[E.name]
        i = self.dnext[E.name]
        self.dnext[E.name] = (i + 1) % len(slots)
        sem, n = slots[i]
        evs = self._deps(reads, writes)
        if n > 0:
            evs.append((sem, 16 * n))
        self._wait(E, evs)
        inst = fn()
        inst.then_inc(sem, 16)
        slots[i][1] = n + 1
        self._mark((sem, 16 * (n + 1)), reads, writes)

    def finish(self):
        evs = []
        for q in self.dsem:
            for sem, n in self.dsem[q]:
                if n > 0:
                    evs.append((sem, 16 * n))
        self._wait(self.sp, evs)


class Arena:
    def __init__(self, nc, stack, nbytes):
        self.t = stack.enter_context(nc.sbuf_tensor("arena", [128, nbytes // 4], F32))
        self.free_list = [(0, nbytes)]
        self.used = 0
        self.peak = 0

    def alloc(self, nbytes, name):
        nbytes = (nbytes + 31) // 32 * 32
        for i, (off, sz) in enumerate(self.free_list):
            if sz >= nbytes:
                if sz == nbytes:
                    self.free_list.pop(i)
                else:
                    self.free_list[i] = (off + nbytes, sz - nbytes)
                self.used += nbytes
                self.peak = max(self.peak, self.used)
                return off, nbytes
        raise MemoryError(f"arena: cannot fit {name} ({nbytes} B/partition); used={self.used} free={self.free_list}")

    def free(self, off, nbytes):
        self.used -= nbytes
        fl = sorted(self.free_list + [(off, nbytes)])
        out = []
        for o, z in fl:
            if out and out[-1][0] + out[-1][1] == o:
                out[-1] = (out[-1][0], out[-1][1] + z)
            else:
                out.append((o, z))
        self.free_list = out

    def view(self, off, shape, dt):
        n = 1
        for d in shape[1:]:
            n *= d
        esz = 2 if dt == BF16 else 4
        words = (n * esz + 3) // 4
        ap = self.t[:, off // 4:off // 4 + words]
        if dt != F32:
            ap = ap.bitcast(dt)
        ap = ap[:, 0:n]
        fr = shape[1:]
        if len(fr) == 2:
            ap = ap.rearrange("p (a b) -> p a b", a=fr[0])
        elif len(fr) == 3:
            ap = ap.rearrange("p (a b c) -> p a b c", a=fr[0], b=fr[1])
        return ap


class Phase:
    def __init__(self, A, T):
        self.A, self.T = A, T
        self.bufs = []
        self.blocks = []

    def buf(self, name):
        b = self.T.newbuf(name)
        self.bufs.append(b)
        return b

    def sb(self, name, shape, dt):
        n = 1
        for d in shape[1:]:
            n *= d
        off, nb = self.A.alloc(n * (2 if dt == BF16 else 4), name)
        self.blocks.append((off, nb))
        return TB(self.A.view(off, shape, dt), self.buf(name))

    def close(self):
        self.T.release(self.bufs)
        for off, nb in self.blocks:
            self.A.free(off, nb)
        self.bufs, self.blocks = [], []


class _Stop(Exception):
    pass


def build_nc(npool, dbg=False, stop_after=None):
    nc = bass.Bass("TRN2", target_bir_lowering=False)
    din = lambda n, s, d=F32: nc.dram_tensor(n, s, d, kind="ExternalInput").ap()
    dout = lambda n, s: nc.dram_tensor(n, s, F32, kind="ExternalOutput").ap()
    x_prev, x_own, x_s, mem = din("x_prev", [TP, D]), din("x_own", [TP, D]), din("x_s", [TS, D]), din("mem", [ML, D])
    poolk, poolv = din("poolk", [npool * 128, NH * HD]), din("poolv", [npool * 128, NH * HD])
    pt_d = din("pt", [1, NSAMP * NPG], I32)
    sconv = din("sconv", [NSAMP * 30, CW])
    cmk, cmv = din("cmk", [NSAMP * ML, MH * HD]), din("cmv", [NSAMP * ML, MH * HD])
    w_in, w_co, w_so, w_mo = din("w_in", [D, INC]), din("w_co", [CW, D]), din("w_so", [NH * HD, D]), din("w_mo", [MH * HD, D])
    w_o, w_up, w_dn, w_mkv = din("w_o", [D, D]), din("w_up", [D, DFF]), din("w_dn", [DFF, D]), din("w_mkv", [D, 2 * MH * HD])
    grow = din("grow", [3, D])
    vecs_d = din("vecs", [128, V_END])
    bsb_d = din("bsb", [1, NH])
    cst_d = din("cst", [128, C_END])
    y_own, y_s = dout("y_own", [TP, D]), dout("y_s", [TS, D])
    kp_o, vp_o = dout("kp", [TP, NH * HD]), dout("vp", [TP, NH * HD])
    ks_o, vs_o = dout("ks", [TS, NH * HD]), dout("vs", [TS, NH * HD])
    convp_o, convs_o = dout("convp", [30, CW]), dout("convs", [NSAMP, 30, CW])
    memk_o, memv_o = dout("memk", [ML, MH * HD]), dout("memv", [ML, MH * HD])
    hT_scr = nc.dram_tensor("hT_scr", [128, KD * NTP], BF16, kind="Internal").ap()

    def dump(T, name, tb, flat, dt=BF16, reads=None):
        if not dbg:
            return
        n = 1
        for d_ in tb.t.shape[1:]:
            n *= d_
        o = nc.dram_tensor("dbg_" + name, [128, n], dt, kind="ExternalOutput").ap()
        T.dma(T.sp, lambda: nc.sync.dma_start(out=o[:, :], in_=tb.t[:].rearrange(flat)), reads=reads or [tb.b])

    try:
        with ExitStack() as st:
            T = Trk(nc, st)
            PE, DVE, ACT, POOL, SP = T.pe, T.dve, T.act, T.pool, T.sp

            def checkpoint(name):
                if dbg or stop_after == "print":
                    print(f"[ckpt] {name}: pe={PE.cnt} dve={DVE.cnt} act={ACT.cnt} arena_used={A.used / 1024:.1f} peak={A.peak / 1024:.1f}")
                    A.peak = A.used
                if stop_after == name:
                    T.finish()
                    raise _Stop()
            A = Arena(nc, st, 206 * 1024)
            G = Phase(A, T)

            banks = [TB(st.enter_context(nc.psum_tensor(f"pb{i}", [128, 512], F32)), T.newbuf(f"pb{i}", xr=True)) for i in range(8)]

            def ld(dst, src, reads=(), q=None):
                E = q or SP
                T.dma(E, lambda: E.eng.dma_start(out=dst_ap(dst), in_=src), reads=list(reads), writes=[dst.b])

            def dst_ap(x):
                return x.t[:] if isinstance(x, TB) else x

            cst = G.sb("cst", [128, C_END], F32)
            vecs = G.sb("vecs", [128, V_END], F32)
            bsb = G.sb("bsb", [128, NH], F32)
            T.dma(SP, lambda: nc.sync.dma_start(out=cst.t[:], in_=cst_d[:, :]), writes=[cst.b])
            T.dma(SP, lambda: nc.sync.dma_start(out=vecs.t[:], in_=vecs_d[:, :]), writes=[vecs.b])
            T.dma(SP, lambda: nc.sync.dma_start(out=bsb.t[:], in_=bsb_d.partition_broadcast(128)), writes=[bsb.b])
            identf = cst.t[:, C_IDENT:C_IDENT + 128]
            tinc = cst.t[:, C_TINC:C_TINC + 128]
            lstr = cst.t[:, C_LSTR:C_LSTR + 128]
            onesf = cst.t[:, C_ONES:C_ONES + 128]
            cb = G.sb("cstb", [128, 256], BF16)
            T.op(DVE, lambda: nc.vector.tensor_copy(out=cb.t[:, 0:128], in_=identf), reads=[cst.b], writes=[cb.b])
            T.op(DVE, lambda: nc.vector.tensor_copy(out=cb.t[:, 128:256], in_=onesf), reads=[cst.b], writes=[cb.b])
            identb = cb.t[:, 0:128]
            onesb = cb.t[:, 128:256]
            pm = vecs.t[:, V_PM:V_PM + 1]
            sm = G.sb("small", [128, 16], F32)
            T.op(DVE, lambda: nc.vector.tensor_scalar(out=sm.t[:, 0:1], in0=pm, scalar1=SCALE, scalar2=None, op0=ALU.mult),
                 reads=[vecs.b], writes=[sm.b])
            T.op(DVE, lambda: nc.vector.tensor_scalar(out=sm.t[:, 1:2], in0=pm, scalar1=-1.0, scalar2=30.0, op0=ALU.add, op1=ALU.mult),
                 reads=[vecs.b], writes=[sm.b])
            T.op(DVE, lambda: nc.vector.tensor_scalar(out=sm.t[:, 2:10], in0=bsb.t[:], scalar1=pm, scalar2=sm.t[:, 1:2],
                                                      op0=ALU.mult, op1=ALU.add), reads=[vecs.b, bsb.b, sm.b], writes=[sm.b])
            ptb = G.sb("ptb", [128, NSAMP * NPG], I32)
            idx = G.sb("idx", [128, NSAMP * NPG], I32)
            T.dma(SP, lambda: nc.sync.dma_start(out=ptb.t[:], in_=pt_d.partition_broadcast(128)), writes=[ptb.b])
            T.op(DVE, lambda: nc.vector.tensor_scalar(out=idx.t[:], in0=ptb.t[:], scalar1=128.0, scalar2=vecs.t[:, V_ROWID:V_ROWID + 1],
                                                      op0=ALU.mult, op1=ALU.add), reads=[ptb.b, vecs.b], writes=[idx.b])
            dd = T.newbuf("convs_rows")
            T.dma(SP, lambda: nc.sync.dma_start(out=convs_o[:, 0:26, :], in_=sconv.rearrange("(b r) c -> b r c", r=30)[:, 4:30, :]),
                  writes=[dd])

            class WStream:
                def __init__(self, ph, nslots, kc):
                    self.slots = [ph.sb(f"w{i}", [128, kc, 512], BF16) for i in range(nslots)]
                    self.i = 0
                    self.plan = []
                    self.issued = 0
                    self.taken = 0

                def add(self, W, r0, kc, c0):
                    self.plan.append((W, r0, kc, c0))

                def _issue(self):
                    W, r0, kc, c0 = self.plan[self.issued]
                    s = self.slots[self.issued % len(self.slots)]
                    T.dma(POOL, lambda: nc.gpsimd.dma_start(
                        out=s.t[:, 0:kc, :], in_=W[r0:r0 + kc * 128, c0:c0 + 512].rearrange("(k p) c -> p k c", p=128)),
                        writes=[s.b])
                    self.issued += 1

                def get(self, hold=0):
                    while self.issued < len(self.plan) and self.issued < self.taken + len(self.slots) - hold:
                        self._issue()
                    s = self.slots[self.taken % len(self.slots)]
                    self.taken += 1
                    return s

                def prefetch(self):
                    while self.issued < len(self.plan) and self.issued < self.taken + len(self.slots):
                        self._issue()

            def mm_fm(bank, n, col0, wt, kc, mcol, src, tok0):
                for k in range(kc):
                    T.op(PE, lambda: nc.tensor.matmul(bank.t[:, col0:col0 + n], lhsT=wt.t[:, k, mcol * 128:(mcol + 1) * 128],
                                                      rhs=src.t[:, k, tok0:tok0 + n], start=(k == 0), stop=(k == kc - 1)),
                         reads=[wt.b, src.b], writes=[bank.b])

            def mm_tm(bank, rows, wt, kc, src, tok0, koff=0):
                for k in range(kc):
                    T.op(PE, lambda: nc.tensor.matmul(bank.t[:, :], lhsT=src.t[:, koff + k, tok0:tok0 + 128], rhs=wt.t[:, k, :],
                                                      start=(k == 0), stop=(k == kc - 1)), reads=[wt.b, src.b], writes=[bank.b])

            def rstd_from(bank, n, inv, out):
                T.op(ACT, lambda: nc.scalar.activation(out=out.t[:, 0:n], in_=bank.t[:, 0:n], func=AF.Ln, scale=inv, bias=EPS),
                     reads=[bank.b], writes=[out.b])
                T.op(ACT, lambda: nc.scalar.activation(out=out.t[:, 0:n], in_=out.t[:, 0:n], func=AF.Exp, scale=-0.5),
                     reads=[out.b], writes=[out.b])

            def headnorm(bank, n, gcol, sq, rb, bank2, out_ap, out_b):
                T.op(ACT, lambda: nc.scalar.activation(out=sq.t[:, 0:n], in_=bank.t[:, 0:n], func=AF.Square), reads=[bank.b], writes=[sq.b])
                T.op(PE, lambda: nc.tensor.matmul(bank2.t[:, 0:n], lhsT=onesf, rhs=sq.t[:, 0:n], start=True, stop=True),
                     reads=[cst.b, sq.b], writes=[bank2.b])
                rstd_from(bank2, n, 1.0 / HD, rb)
                T.op(DVE, lambda: nc.vector.scalar_tensor_tensor(out=out_ap, in0=bank.t[:, 0:n], scalar=vecs.t[:, gcol:gcol + 1],
                                                                 in1=rb.t[:, 0:n], op0=ALU.mult, op1=ALU.mult),
                     reads=[bank.b, vecs.b, rb.b], writes=[out_b])

            def norm_tiles(ph, xd, ntok, grow_i, dst, dtok0, tag):
                gbc = ph.sb(f"gbc{tag}", [128, D], F32)
                T.dma(SP, lambda: nc.sync.dma_start(out=gbc.t[:], in_=grow[grow_i:grow_i + 1, :].partition_broadcast(128)), writes=[gbc.b])
                xs = [ph.sb(f"xt{tag}{i}", [128, D], F32) for i in range(2)]
                xn = [ph.sb(f"xn{tag}{i}", [128, D], BF16) for i in range(2)]
                junk = ph.sb(f"junk{tag}", [128, D], BF16)
                ssr = [ph.sb(f"ss{tag}{i}", [128, 2], F32) for i in range(2)]
                ntile = (ntok + 127) // 128
                for i in range(ntile):
                    rows = min(128, ntok - i * 128)
                    x_, n_, s_ = xs[i % 2], xn[i % 2], ssr[i % 2]
                    T.dma(SP, lambda: nc.sync.dma_start(out=x_.t[0:rows, :], in_=xd[i * 128:i * 128 + rows, :]), writes=[x_.b])
                    T.op(ACT, lambda: nc.scalar.activation(out=junk.t[0:rows, :], in_=x_.t[0:rows, :], func=AF.Square, accum_out=s_.t[0:rows, 0:1]),
                         reads=[x_.b], writes=[junk.b, s_.b])
                    T.op(ACT, lambda: nc.scalar.activation(out=s_.t[0:rows, 1:2], in_=s_.t[0:rows, 0:1], func=AF.Ln, scale=1.0 / D, bias=EPS),
                         reads=[s_.b], writes=[s_.b])
                    T.op(ACT, lambda: nc.scalar.activation(out=s_.t[0:rows, 1:2], in_=s_.t[0:rows, 1:2], func=AF.Exp, scale=-0.5),
                         reads=[s_.b], writes=[s_.b])
                    T.op(DVE, lambda: nc.vector.scalar_tensor_tensor(out=n_.t[0:rows, :], in0=x_.t[0:rows, :], scalar=s_.t[0:rows, 1:2],
                                                                     in1=gbc.t[0:rows, :], op0=ALU.mult, op1=ALU.mult),
                         reads=[x_.b, s_.b, gbc.b], writes=[n_.b])
                    for half in range(2):
                        bk = banks[(2 * i + half) % 4]
                        bkb = bk.t[:].bitcast(BF16)
                        for kk in range(8):
                            k = half * 8 + kk
                            T.op(PE, lambda: nc.tensor.transpose(out=bkb[:, kk * 128:kk * 128 + rows], in_=n_.t[0:rows, k * 128:(k + 1) * 128],
                                                                 identity=identb[0:rows, 0:rows]), reads=[n_.b, cb.b], writes=[bk.b])
                        src = bkb.rearrange("p (k t) -> p k t", k=8)[:, :, 0:rows]
                        d_ap = dst.t[:, half * 8:half * 8 + 8, dtok0 + i * 128:dtok0 + i * 128 + rows]
                        if half == 0:
                            T.op(ACT, lambda: nc.scalar.copy(out=d_ap, in_=src), reads=[bk.b], writes=[dst.b])
                        else:
                            T.op(DVE, lambda: nc.vector.tensor_copy(out=d_ap, in_=src), reads=[bk.b], writes=[dst.b])


            L_hT = Phase(A, T)
            hT = L_hT.sb("hT", [128, KD, NTP], BF16)
            T.op(DVE, lambda: nc.vector.memset(hT.t[:, :, NT:NTP], 0.0), writes=[hT.b])
            L_cT = Phase(A, T)
            cT = L_cT.sb("cT", [128, 4, NT], BF16)
            P2 = Phase(A, T)
            norm_tiles(P2, x_own, TP, 0, hT, 0, "o")
            norm_tiles(P2, x_s, TS, 0, hT, TP, "s")
            P2.close()

            checkpoint("P_A")
            P3 = Phase(A, T)
            hTpl = P3.sb("hTpl", [128, KD, 128], BF16)
            PN = Phase(A, T)
            norm_tiles(PN, x_prev[TP - 128:TP, :], 128, 0, hTpl, 0, "l")
            PN.close()
            WS = WStream(P3, 2, KD)
            for c0 in (0, 512):
                WS.add(w_in, 0, KD, c0)
            uxp = P3.sb("uxp", [128, 4, 30 + TP], F32)
            uxs = P3.sb("uxs", [128, 4, NSAMP, 34], F32)
            acc = P3.sb("cacc", [128, 4, NT], F32)
            tmp = [P3.sb(f"ctmp{i}", [128, 512], F32) for i in range(2)]
            sct = [P3.sb(f"sct{i}", [128, CW], F32) for i in range(2)]
            for i in range(4):
                s_ = sct[i % 2]
                T.dma(SP, lambda: nc.sync.dma_start(out=s_.t[0:120, :], in_=sconv[i * 120:(i + 1) * 120, :]), writes=[s_.b])
                bk = banks[i % 2]
                for m in range(4):
                    T.op(PE, lambda: nc.tensor.transpose(out=bk.t[:, m * 128:m * 128 + 120], in_=s_.t[0:120, m * 128:(m + 1) * 128],
                                                         identity=identf[0:120, 0:120]), reads=[s_.b, cst.b], writes=[bk.b])
                T.op(ACT, lambda: nc.scalar.copy(out=uxs.t[:, :, 4 * i:4 * i + 4, 0:30],
                                                 in_=bk.t[:].rearrange("p (m x) -> p m x", m=4)[:, :, 0:120].rearrange("p m (b r) -> p m b r", b=4)),
                     reads=[bk.b], writes=[uxs.b])

            CG = [(hTpl, 0, 128)] + [(hT, t0, n) for (t0, n) in GROUPS]

            def u_dst(m, gi):
                if gi == 0:
                    return uxp.t[:, m, 0:30], uxp.b
                t0, n = GROUPS[gi - 1]
                if gi < 3:
                    return uxp.t[:, m, 30 + t0:30 + t0 + n], uxp.b
                return uxs.t[:, m, :, 30:34], uxs.b

            def u_src(t_ap, gi, n):
                if gi == 0:
                    return t_ap[:, 98:128]
                if gi < 3:
                    return t_ap[:, 0:n]
                return t_ap[:, 0:n].rearrange("p (b t) -> p b t", t=4)

            wt = WS.get()
            it = 0
            for m in range(4):
                for gi, (src_h, t0, n) in enumerate(CG):
                    bk = banks[it % 4]
                    mm_fm(bk, n, 0, wt, KD, m, src_h, t0)
                    d_ap, d_b = u_dst(m, gi)
                    T.op(ACT, lambda: nc.scalar.copy(out=d_ap, in_=u_src(bk.t, gi, n)), reads=[bk.b], writes=[d_b])
                    it += 1
            wt = WS.get()
            for m in range(4):
                for gi, (src_h, t0, n) in enumerate(CG):
                    bk = banks[it % 4]
                    mm_fm(bk, n, 0, wt, KD, m, src_h, t0)
                    t_ = tmp[it % 2]
                    T.op(ACT, lambda: nc.scalar.activation(out=t_.t[:, 0:n], in_=bk.t[:, 0:n], func=AF.Sigmoid), reads=[bk.b], writes=[t_.b])
                    d_ap, d_b = u_dst(m, gi)
                    if gi == 0:
                        T.op(DVE, lambda: nc.vector.scalar_tensor_tensor(out=d_ap, in0=d_ap, scalar=pm, in1=u_src(t_.t, gi, n), op0=ALU.mult, op1=ALU.mult),
                             reads=[d_b, vecs.b, t_.b], writes=[d_b])
                    else:
                        T.op(DVE, lambda: nc.vector.tensor_tensor(out=d_ap, in0=d_ap, in1=u_src(t_.t, gi, n), op=ALU.mult), reads=[d_b, t_.b], writes=[d_b])
                    it += 1
            cst_o = P3.sb("cst_o", [128, CW], F32)
            bk = banks[4]
            for m in range(4):
                T.op(PE, lambda: nc.tensor.transpose(out=bk.t[0:30, m * 128:(m + 1) * 128], in_=uxp.t[:, m, TP:TP + 30], identity=identf),
                     reads=[uxp.b, cst.b], writes=[bk.b])
            T.op(ACT, lambda: nc.scalar.copy(out=cst_o.t[0:30, :], in_=bk.t[0:30, :]), reads=[bk.b], writes=[cst_o.b])
            T.dma(SP, lambda: nc.sync.dma_start(out=convp_o[:, :], in_=cst_o.t[0:30, :]), reads=[cst_o.b])
            cst_s = P3.sb("cst_s", [128, CW], F32)
            ucp = P3.sb("ucp", [128, 4, TS], F32)
            T.op(DVE, lambda: nc.vector.tensor_copy(out=ucp.t[:].rearrange("p m (b t) -> p m b t", t=4), in_=uxs.t[:, :, :, 30:34]),
                 reads=[uxs.b], writes=[ucp.b])
            bk = banks[5]
            for m in range(4):
                T.op(PE, lambda: nc.tensor.transpose(out=bk.t[0:TS, m * 128:(m + 1) * 128], in_=ucp.t[:, m, :], identity=identf),
                     reads=[ucp.b, cst.b], writes=[bk.b])
            T.op(ACT, lambda: nc.scalar.copy(out=cst_s.t[0:TS, :], in_=bk.t[0:TS, :]), reads=[bk.b], writes=[cst_s.b])
            for b in range(NSAMP):
                T.dma(SP, lambda: nc.sync.dma_start(out=convs_o[b, 26:30, :], in_=cst_s.t[4 * b:4 * b + 4, :]), reads=[cst_s.b])
            wdw = lambda m, j: vecs.t[:, V_WDW + m * 31 + j:V_WDW + m * 31 + j + 1]
            for m in range(4):
                a_p = acc.t[:, m, 0:TP]
                a_s = acc.t[:, m, TP:NT].rearrange("p (b t) -> p b t", t=4)
                T.op(DVE, lambda: nc.vector.tensor_scalar(out=a_p, in0=uxp.t[:, m, 0:TP], scalar1=wdw(m, 0), scalar2=vecs.t[:, V_BDW + m:V_BDW + m + 1],
                                                          op0=ALU.mult, op1=ALU.add), reads=[uxp.b, vecs.b], writes=[acc.b])
                T.op(DVE, lambda: nc.vector.tensor_scalar(out=a_s, in0=uxs.t[:, m, :, 0:4], scalar1=wdw(m, 0), scalar2=vecs.t[:, V_BDW + m:V_BDW + m + 1],
                                                          op0=ALU.mult, op1=ALU.add), reads=[uxs.b, vecs.b], writes=[acc.b])
                for j in range(1, 31):
                    T.op(DVE, lambda: nc.vector.scalar_tensor_tensor(out=a_p, in0=uxp.t[:, m, j:j + TP], scalar=wdw(m, j), in1=a_p,
                                                                     op0=ALU.mult, op1=ALU.add), reads=[uxp.b, vecs.b, acc.b], writes=[acc.b])
                    T.op(DVE, lambda: nc.vector.scalar_tensor_tensor(out=a_s, in0=uxs.t[:, m, :, j:j + 4], scalar=wdw(m, j), in1=a_s,
                                                                     op0=ALU.mult, op1=ALU.add), reads=[uxs.b, vecs.b, acc.b], writes=[acc.b])
            rbc = P3.sb("rbc", [128, 512], F32)
            for gi, (t0, n) in enumerate(GROUPS):
                b1, b2 = banks[6], banks[7]
                for m in range(4):
                    T.op(PE, lambda: nc.tensor.matmul(b1.t[:, 0:n], lhsT=onesf, rhs=acc.t[:, m, t0:t0 + n], start=(m == 0), stop=(m == 3)),
                         reads=[cst.b, acc.b], writes=[b1.b])
                for m in range(4):
                    a_ = acc.t[:, m, t0:t0 + n]
                    T.op(DVE, lambda: nc.vector.scalar_tensor_tensor(out=a_, in0=b1.t[:, 0:n], scalar=-1.0 / CW, in1=a_, op0=ALU.mult, op1=ALU.add),
                         reads=[b1.b, acc.b], writes=[acc.b])
                    t_ = tmp[m % 2]
                    T.op(ACT, lambda: nc.scalar.activation(out=t_.t[:, 0:n], in_=a_, func=AF.Square), reads=[acc.b], writes=[t_.b])
                    T.op(PE, lambda: nc.tensor.matmul(b2.t[:, 0:n], lhsT=onesf, rhs=t_.t[:, 0:n], start=(m == 0), stop=(m == 3)),
                         reads=[cst.b, t_.b], writes=[b2.b])
                rstd_from(b2, n, 1.0 / CW, rbc)
                for m in range(4):
                    a_ = acc.t[:, m, t0:t0 + n]
                    T.op(DVE, lambda: nc.vector.tensor_tensor(out=a_, in0=a_, in1=rbc.t[:, 0:n], op=ALU.mult), reads=[acc.b, rbc.b], writes=[acc.b])
                    T.op(ACT, lambda: nc.scalar.activation(out=cT.t[:, m, t0:t0 + n], in_=a_, func=AF.Silu,
                                                           scale=vecs.t[:, V_GCLN + m:V_GCLN + m + 1], bias=vecs.t[:, V_BCLN + m:V_BCLN + m + 1]),
                         reads=[acc.b, vecs.b], writes=[cT.b])
            P3.close()
            dump(T, "cT", cT, "p k t -> p (k t)")

            checkpoint("P_B")
            L_kv = Phase(A, T)
            kT = L_kv.sb("kT", [128, NH, 2 * TP], BF16)
            Vtok = L_kv.sb("Vtok", [128, 16, NH * HD], BF16)
            L_q = Phase(A, T)
            kTs = L_q.sb("kTs", [128, NH, TS], BF16)
            Vs = L_q.sb("Vs", [128, NH * HD], BF16)
            P1 = Phase(A, T)
            hTp = P1.sb("hTp", [128, KD, TP], BF16)
            PN = Phase(A, T)
            norm_tiles(PN, x_prev, TP, 0, hTp, 0, "p")
            PN.close()
            checkpoint("PC0")
            WS = WStream(P1, 2, KD)
            for c0 in (2048, 2560, 3072, 3584):
                WS.add(w_in, 0, KD, c0)
            wk_sq = [P1.sb(f"sq4{i}", [128, 512], F32) for i in range(2)]
            wk_rb = [P1.sb(f"rb4{i}", [128, 512], F32) for i in range(2)]
            kn = [P1.sb(f"kn{i}", [128, 512], F32) for i in range(2)]
            kst = [P1.sb(f"kst{i}", [128, 512], F32) for i in range(1)]
            vst = kst
            it = 0
            for blk in range(2):
                wt = WS.get()
                for hh in range(4):
                    h = blk * 4 + hh
                    for g in range(2):
                        bk, bk2 = banks[it % 2], banks[2 + it % 2]
                        mm_fm(bk, 512, 0, wt, KD, hh, hTp, g * 512)
                        headnorm(bk, 512, V_GKSB, wk_sq[it % 2], wk_rb[it % 2], bk2, kT.t[:, h, g * 512:(g + 1) * 512], kT.b)
                        it += 1
                    for gi, (t0, n) in enumerate(GROUPS):
                        bk, bk2, bk3 = banks[it % 2], banks[2 + it % 2], banks[4 + it % 2]
                        k_ = kn[it % 2]
                        mm_fm(bk, n, 0, wt, KD, hh, hT, t0)
                        headnorm(bk, n, V_GKSB, wk_sq[it % 2], wk_rb[it % 2], bk2, k_.t[:, 0:n], k_.b)
                        if gi < 2:
                            T.op(ACT, lambda: nc.scalar.copy(out=kT.t[:, h, TP + t0:TP + t0 + n], in_=k_.t[:, 0:n]), reads=[k_.b], writes=[kT.b])
                        else:
                            T.op(ACT, lambda: nc.scalar.copy(out=kTs.t[:, h, :], in_=k_.t[:, 0:n]), reads=[k_.b], writes=[kTs.b])
                        s_ = kst[0]
                        if gi < 2:
                            for j in range(4):
                                T.op(PE, lambda: nc.tensor.transpose(out=bk3.t[:, j * 128:(j + 1) * 128], in_=k_.t[:, j * 128:(j + 1) * 128], identity=identf),
                                     reads=[k_.b, cst.b], writes=[bk3.b])
                            T.op(DVE, lambda: nc.vector.tensor_copy(out=s_.t[:], in_=bk3.t[:]), reads=[bk3.b], writes=[s_.b])
                            T.dma(SP, lambda: nc.sync.dma_start(out=kp_o[t0:t0 + n, h * 128:(h + 1) * 128].rearrange("(j p) d -> p j d", p=128),
                                                                in_=s_.t[:].rearrange("p (j d) -> p j d", j=4)), reads=[s_.b])
                        else:
                            T.op(PE, lambda: nc.tensor.transpose(out=bk3.t[0:TS, 0:128], in_=k_.t[:, 0:TS], identity=identf),
                                 reads=[k_.b, cst.b], writes=[bk3.b])
                            T.op(DVE, lambda: nc.vector.tensor_copy(out=s_.t[0:TS, 0:128], in_=bk3.t[0:TS, 0:128]), reads=[bk3.b], writes=[s_.b])
                            T.dma(SP, lambda: nc.sync.dma_start(out=ks_o[:, h * 128:(h + 1) * 128], in_=s_.t[0:TS, 0:128]), reads=[s_.b])
                        it += 1
            checkpoint("PC1")
            for blk in range(2):
                wt = WS.get()
                for tile in range(8):
                    bk = banks[it % 4]
                    mm_tm(bk, 128, wt, KD, hTp, tile * 128)
                    o_ap = Vtok.t[:, tile, blk * 512:(blk + 1) * 512]
                    if it % 2 == 0:
                        T.op(ACT, lambda: nc.scalar.copy(out=o_ap, in_=bk.t[:, :]), reads=[bk.b], writes=[Vtok.b])
                    else:
                        T.op(DVE, lambda: nc.vector.tensor_copy(out=o_ap, in_=bk.t[:, :]), reads=[bk.b], writes=[Vtok.b])
                    it += 1
                if blk == 0:
                    checkpoint("PC2")
                for tile in range(9):
                    if blk == 0 and tile == 8:
                        checkpoint("PC3")
                    rows = 128 if tile < 8 else TS
                    bk = banks[it % 4]
                    s_ = vst[0]
                    mm_tm(bk, rows, wt, KD, hT, tile * 128)
                    T.op(ACT, lambda: nc.scalar.copy(out=s_.t[0:rows, :], in_=bk.t[0:rows, :]), reads=[bk.b], writes=[s_.b])
                    if tile < 8:
                        T.op(DVE, lambda: nc.vector.tensor_copy(out=Vtok.t[:, 8 + tile, blk * 512:(blk + 1) * 512], in_=s_.t[:, :]),
                             reads=[s_.b], writes=[Vtok.b])
                        T.dma(SP, lambda: nc.sync.dma_start(out=vp_o[tile * 128:(tile + 1) * 128, blk * 512:(blk + 1) * 512], in_=s_.t[:, :]), reads=[s_.b])
                    else:
                        T.op(DVE, lambda: nc.vector.tensor_copy(out=Vs.t[0:TS, blk * 512:(blk + 1) * 512], in_=s_.t[0:TS, :]),
                             reads=[s_.b], writes=[Vs.b])
                        T.dma(SP, lambda: nc.sync.dma_start(out=vs_o[:, blk * 512:(blk + 1) * 512], in_=s_.t[0:TS, :]), reads=[s_.b])
                    it += 1
            P1.close()

            checkpoint("P_C")
            L_qm = Phase(A, T)
            qT = L_q.sb("qT", [128, NH, NT], BF16)
            qmT = L_qm.sb("qmT", [128, MH, NT], BF16)
            P4 = Phase(A, T)
            WS = WStream(P4, 2, KD)
            for c0 in (1024, 1536, 4096):
                WS.add(w_in, 0, KD, c0)
            wk_sq = [P4.sb(f"sqd{i}", [128, 512], F32) for i in range(2)]
            wk_rb = [P4.sb(f"rbd{i}", [128, 512], F32) for i in range(2)]
            it = 0
            for blk in range(2):
                wt = WS.get()
                for hh in range(4):
                    h = blk * 4 + hh
                    for gi, (t0, n) in enumerate(GROUPS):
                        bk, bk2 = banks[it % 2], banks[2 + it % 2]
                        mm_fm(bk, n, 0, wt, KD, hh, hT, t0)
                        headnorm(bk, n, V_GQSB, wk_sq[it % 2], wk_rb[it % 2], bk2, qT.t[:, h, t0:t0 + n], qT.b)
                        it += 1
            wt = WS.get()
            for h in range(MH):
                for gi, (t0, n) in enumerate(GROUPS):
                    bk, bk2 = banks[it % 2], banks[2 + it % 2]
                    mm_fm(bk, n, 0, wt, KD, h, hT, t0)
                    headnorm(bk, n, V_GQM, wk_sq[it % 2], wk_rb[it % 2], bk2, qmT.t[:, h, t0:t0 + n], qmT.b)
                    it += 1
            P4.close()
            dump(T, "qT", qT, "p k t -> p (k t)")
            dump(T, "qmT", qmT, "p k t -> p (k t)")
            dump(T, "kT", kT, "p k t -> p (k t)")
            hT_buf = T.newbuf("hT_dram")
            T.dma(SP, lambda: nc.sync.dma_start(out=hT_scr[:, :], in_=hT.t[:].rearrange("p k t -> p (k t)")), reads=[hT.b], writes=[hT_buf])
            L_hT.close()

            L_o = Phase(A, T)
            oT_sb = L_o.sb("oT_sb", [128, NH, NT], BF16)
            oT_mem = L_o.sb("oT_mem", [128, MH, NT], BF16)

            checkpoint("P_D")
            P5 = Phase(A, T)
            NE = 4
            Eb = [P5.sb(f"E{i}", [128, 512], F32) for i in range(NE)]
            SPb = [P5.sb(f"SP{i}", [128, 512], F32) for i in range(NE)]
            Xb = [P5.sb(f"X{i}", [128, 512], F32) for i in range(2)]
            ab = [P5.sb(f"a{i}", [128, 512], BF16) for i in range(2)]
            zbanks = banks[0:3]
            tiles = []
            for hp in range(0, NH, 2):
                for qg in range(2):
                    lists = []
                    for s in range(2):
                        kbs = [8 + i for i in range(4 * qg + 3, -1, -1)] + list(range(7, -1, -1))
                        lists.append([(hp + s, qg, kb, s) for kb in kbs])
                    for a_, b_ in zip(*lists):
                        tiles.append(a_)
                        tiles.append(b_)
            state = {}
            for ti, (h, qg, kb, s) in enumerate(tiles):
                first = kb == 8 + 4 * qg + 3
                state[ti] = dict(h=h, qg=qg, kb=kb, s=s, first=first, last=(kb == 0), z=zbanks[ti % 3], E=Eb[ti % NE], SP=SPb[ti % NE],
                                 X=Xb[ti % 2], a=ab[ti % 2], C=banks[3 + s], O=banks[5 + s], prevSP=(SPb[(ti - 2) % NE] if not first else None))

            def stA(d):
                T.op(PE, lambda: nc.tensor.matmul(d["z"].t[:], lhsT=kT.t[:, d["h"], d["kb"] * 128:(d["kb"] + 1) * 128],
                                                  rhs=qT.t[:, d["h"], d["qg"] * 512:(d["qg"] + 1) * 512], start=True, stop=True),
                     reads=[kT.b, qT.b], writes=[d["z"].b])

            def stB(d):
                h, kb, qg = d["h"], d["kb"], d["qg"]
                if kb >= 8:
                    T.op(ACT, lambda: nc.scalar.activation(out=d["E"].t[:], in_=d["z"].t[:], func=AF.Exp, scale=SCALE, bias=bsb.t[:, h:h + 1]),
                         reads=[d["z"].b, bsb.b], writes=[d["E"].b])
                    r = kb - 8 - 4 * qg
                    if r >= 0:
                        T.op(DVE, lambda: nc.vector.tensor_tensor(out=d["E"].t[:], in0=d["E"].t[:], in1=cst.t[:, C_MASKD + r * 512:C_MASKD + (r + 1) * 512],
                                                                  op=ALU.mult), reads=[d["E"].b, cst.b], writes=[d["E"].b])
                else:
                    T.op(ACT, lambda: nc.scalar.activation(out=d["E"].t[:], in_=d["z"].t[:], func=AF.Exp, scale=sm.t[:, 0:1], bias=sm.t[:, 2 + h:3 + h]),
                         reads=[d["z"].b, sm.b], writes=[d["E"].b])
                T.op(ACT, lambda: nc.scalar.activation(out=d["SP"].t[:], in_=d["E"].t[:], func=AF.Ln, bias=1.0), reads=[d["E"].b], writes=[d["SP"].b])
                if not d["first"]:
                    T.op(PE, lambda: nc.tensor.matmul(d["C"].t[:], lhsT=lstr, rhs=d["prevSP"].t[:], start=False, stop=False, skip_group_check=True),
                         reads=[cst.b, d["prevSP"].b], writes=[d["C"].b])
                T.op(PE, lambda: nc.tensor.matmul(d["C"].t[:], lhsT=tinc, rhs=d["SP"].t[:], start=d["first"], stop=True, skip_group_check=True),
                     reads=[cst.b, d["SP"].b], writes=[d["C"].b])

            def stC(d):
                h, kb, qg = d["h"], d["kb"], d["qg"]
                T.op(ACT, lambda: nc.scalar.activation(out=d["X"].t[:], in_=d["C"].t[:], func=AF.Exp, scale=-1.0), reads=[d["C"].b], writes=[d["X"].b])
                T.op(DVE, lambda: nc.vector.tensor_tensor(out=d["a"].t[:], in0=d["E"].t[:], in1=d["X"].t[:], op=ALU.mult),
                     reads=[d["E"].b, d["X"].b], writes=[d["a"].b])
                T.op(PE, lambda: nc.tensor.matmul(d["O"].t[:], lhsT=Vtok.t[:, kb, h * 128:(h + 1) * 128], rhs=d["a"].t[:],
                                                  start=d["first"], stop=d["last"], skip_group_check=True),
                     reads=[Vtok.b, d["a"].b], writes=[d["O"].b])
                if d["last"]:
                    T.op(ACT, lambda: nc.scalar.copy(out=oT_sb.t[:, h, qg * 512:(qg + 1) * 512], in_=d["O"].t[:]), reads=[d["O"].b], writes=[oT_sb.b])

            n_t = len(tiles)
            for step in range(n_t + 2):
                if step < n_t:
                    stA(state[step])
                if 0 <= step - 1 < n_t:
                    stB(state[step - 1])
                if 0 <= step - 2 < n_t:
                    stC(state[step - 2])
            P5.close()
            L_kv.close()

            checkpoint("A1")
            P6 = Phase(A, T)
            Vb = P6.sb("Vb", [128, NPG, NH * HD], BF16)
            KT = P6.sb("KT", [128, NPG, NH, 128], BF16)
            m64 = cst.t[0:TS, C_M64:C_M64 + 512]
            biasN = P6.sb("biasN", [128, 512], F32)
            T.op(DVE, lambda: nc.vector.tensor_copy(out=biasN.t[:].rearrange("p (h x) -> p h x", h=NH),
                                                    in_=bsb.t[:].unsqueeze(2).to_broadcast([128, NH, 64])), reads=[bsb.b], writes=[biasN.b])
            biasP = P6.sb("biasP", [128, 512], F32)
            T.op(DVE, lambda: nc.vector.tensor_copy(out=biasP.t[:].rearrange("p (g h t) -> p g h t", g=NPG, h=NH),
                                                    in_=bsb.t[:].unsqueeze(1).unsqueeze(3).to_broadcast([128, NPG, NH, 4])),
                 reads=[bsb.b], writes=[biasP.b])
            P6n = Phase(A, T)
            En = P6n.sb("En", [128, 512], F32)
            SPn = P6n.sb("SPn", [128, 512], F32)
            Xn = P6n.sb("Xn", [128, 512], F32)
            an = P6n.sb("an", [128, 512], BF16)
            XN = P6.sb("XN", [128, 512], F32)
            Onew = P6.sb("Onew", [128, 512], F32)
            bz = banks[0]
            for h in range(NH):
                T.op(PE, lambda: nc.tensor.matmul(bz.t[0:TS, h * 64:(h + 1) * 64], lhsT=kTs.t[:, h, :], rhs=qT.t[:, h, TP:NT], start=True, stop=True),
                     reads=[kTs.b, qT.b], writes=[bz.b])
            T.op(DVE, lambda: nc.vector.scalar_tensor_tensor(out=En.t[0:TS, :], in0=bz.t[0:TS, :], scalar=SCALE, in1=biasN.t[0:TS, :],
                                                             op0=ALU.mult, op1=ALU.add), reads=[bz.b, biasN.b], writes=[En.b])
            T.op(ACT, lambda: nc.scalar.activation(out=En.t[0:TS, :], in_=En.t[0:TS, :], func=AF.Exp), reads=[En.b], writes=[En.b])
            T.op(DVE, lambda: nc.vector.tensor_tensor(out=En.t[0:TS, :], in0=En.t[0:TS, :], in1=m64, op=ALU.mult), reads=[En.b, cst.b], writes=[En.b])
            T.op(ACT, lambda: nc.scalar.activation(out=SPn.t[0:TS, :], in_=En.t[0:TS, :], func=AF.Ln, bias=1.0), reads=[En.b], writes=[SPn.b])
            b1, b2, b3 = banks[1], banks[2], banks[3]
            T.op(PE, lambda: nc.tensor.matmul(b1.t[:, :], lhsT=onesf[0:TS, :], rhs=SPn.t[0:TS, :], start=True, stop=True),
                 reads=[cst.b, SPn.b], writes=[b1.b])
            T.op(ACT, lambda: nc.scalar.activation(out=XN.t[:], in_=b1.t[:], func=AF.Exp, scale=-1.0), reads=[b1.b], writes=[XN.b])
            T.op(PE, lambda: nc.tensor.matmul(b2.t[0:TS, :], lhsT=tinc[0:TS, 0:TS], rhs=SPn.t[0:TS, :], start=True, stop=True),
                 reads=[cst.b, SPn.b], writes=[b2.b])
            T.op(ACT, lambda: nc.scalar.activation(out=Xn.t[0:TS, :], in_=b2.t[0:TS, :], func=AF.Exp, scale=-1.0), reads=[b2.b], writes=[Xn.b])
            T.op(DVE, lambda: nc.vector.tensor_tensor(out=an.t[0:TS, :], in0=En.t[0:TS, :], in1=Xn.t[0:TS, :], op=ALU.mult),
                 reads=[En.b, Xn.b], writes=[an.b])
            for h in range(NH):
                T.op(PE, lambda: nc.tensor.matmul(b3.t[:, h * 64:(h + 1) * 64], lhsT=Vs.t[0:TS, h * 128:(h + 1) * 128], rhs=an.t[0:TS, h * 64:(h + 1) * 64],
                                                  start=True, stop=True), reads=[Vs.b, an.b], writes=[b3.b])
            T.op(DVE, lambda: nc.vector.tensor_copy(out=Onew.t[:], in_=b3.t[:]), reads=[b3.b], writes=[Onew.b])
            P6n.close()
            NKS, NVS = 6, 4
            Kst = [P6.sb(f"Kst{i}", [128, NH * HD], F32) for i in range(NKS)]
            Vst = [P6.sb(f"Vst{i}", [128, NH * HD], F32) for i in range(NVS)]
            zb_ = P6.sb("zb", [128, 512], F32)
            Es = P6.sb("Es", [128, 512], F32)
            SPs = P6.sb("SPs", [128, 512], F32)
            Xs = P6.sb("Xs", [128, 512], F32)
            a1 = P6.sb("a1", [128, 512], F32)
            as_ = P6.sb("as", [128, 512], BF16)
            kcnt = vcnt = 0
            ev = 0
            for b in range(NSAMP):
                for pg in range(NPG):
                    col = b * NPG + pg
                    ks_ = Kst[kcnt % NKS]
                    kcnt += 1
                    T.dma(POOL, lambda: nc.gpsimd.indirect_dma_start(out=ks_.t[:, :], out_offset=None, in_=poolk[:, :],
                                                                     in_offset=bass.IndirectOffsetOnAxis(ap=idx.t[:, col:col + 1], axis=0)),
                          reads=[idx.b], writes=[ks_.b])
                    vs_ = Vst[vcnt % NVS]
                    vcnt += 1
                    T.dma(POOL, lambda: nc.gpsimd.indirect_dma_start(out=vs_.t[:, :], out_offset=None, in_=poolv[:, :],
                                                                     in_offset=bass.IndirectOffsetOnAxis(ap=idx.t[:, col:col + 1], axis=0)),
                          reads=[idx.b], writes=[vs_.b])
                    for hb in range(2):
                        bk = banks[4 + ev % 4]
                        for hh in range(4):
                            h = hb * 4 + hh
                            T.op(PE, lambda: nc.tensor.transpose(out=bk.t[:, hh * 128:(hh + 1) * 128], in_=ks_.t[:, h * 128:(h + 1) * 128], identity=identf),
                                 reads=[ks_.b, cst.b], writes=[bk.b])
                        o_ap = KT.t[:, pg, hb * 4:hb * 4 + 4, :]
                        i_ap = bk.t[:, :].rearrange("p (h s) -> p h s", h=4)
                        if ev % 2 == 0:
                            T.op(ACT, lambda: nc.scalar.copy(out=o_ap, in_=i_ap), reads=[bk.b], writes=[KT.b])
                        else:
                            T.op(DVE, lambda: nc.vector.tensor_copy(out=o_ap, in_=i_ap), reads=[bk.b], writes=[KT.b])
                        ev += 1
                    if pg % 2 == 0:
                        T.op(DVE, lambda: nc.vector.tensor_copy(out=Vb.t[:, pg, :], in_=vs_.t[:, :]), reads=[vs_.b], writes=[Vb.b])
                    else:
                        T.op(ACT, lambda: nc.scalar.copy(out=Vb.t[:, pg, :], in_=vs_.t[:, :]), reads=[vs_.b], writes=[Vb.b])
                bz, bc, bo = banks[0], banks[1], banks[2 + b % 2]
                for pg in range(NPG):
                    for h in range(NH):
                        c0 = (pg * NH + h) * 4
                        T.op(PE, lambda: nc.tensor.matmul(bz.t[:, c0:c0 + 4], lhsT=KT.t[:, pg, h, :], rhs=qT.t[:, h, TP + 4 * b:TP + 4 * b + 4],
                                                          start=True, stop=True), reads=[KT.b, qT.b], writes=[bz.b])
                T.op(DVE, lambda: nc.vector.scalar_tensor_tensor(out=zb_.t[:], in0=bz.t[:], scalar=SCALE, in1=biasP.t[:], op0=ALU.mult, op1=ALU.add),
                     reads=[bz.b, biasP.b], writes=[zb_.b])
                T.op(ACT, lambda: nc.scalar.activation(out=Es.t[:], in_=zb_.t[:], func=AF.Exp), reads=[zb_.b], writes=[Es.b])
                T.op(ACT, lambda: nc.scalar.activation(out=SPs.t[:], in_=Es.t[:], func=AF.Ln, bias=1.0), reads=[Es.b], writes=[SPs.b])
                T.op(PE, lambda: nc.tensor.matmul(bc.t[:], lhsT=tinc, rhs=SPs.t[:], start=True, stop=False, skip_group_check=True),
                     reads=[cst.b, SPs.b], writes=[bc.b])
                for k in range(1, NPG):
                    w_ = 32 * (NPG - k)
                    T.op(PE, lambda: nc.tensor.matmul(bc.t[:, 0:w_], lhsT=onesf, rhs=SPs.t[:, 32 * k:512], start=False, stop=(k == NPG - 1),
                                                      skip_group_check=True), reads=[cst.b, SPs.b], writes=[bc.b])
                T.op(ACT, lambda: nc.scalar.activation(out=Xs.t[:], in_=bc.t[:], func=AF.Exp, scale=-1.0), reads=[bc.b], writes=[Xs.b])
                T.op(DVE, lambda: nc.vector.tensor_tensor(out=a1.t[:], in0=Es.t[:], in1=Xs.t[:], op=ALU.mult), reads=[Es.b, Xs.b], writes=[a1.b])
                xn_b = XN.t[:].rearrange("p (h b t) -> p h b t", h=NH, b=NSAMP)[:, :, b, :].unsqueeze(1).to_broadcast([128, NPG, NH, 4])
                T.op(DVE, lambda: nc.vector.tensor_tensor(out=as_.t[:].rearrange("p (g h t) -> p g h t", g=NPG, h=NH),
                                                          in0=a1.t[:].rearrange("p (g h t) -> p g h t", g=NPG, h=NH), in1=xn_b, op=ALU.mult),
                     reads=[a1.b, XN.b], writes=[as_.b])
                for h in range(NH):
                    for pg in range(NPG):
                        c0 = (pg * NH + h) * 4
                        T.op(PE, lambda: nc.tensor.matmul(bo.t[:, h * 4:h * 4 + 4], lhsT=Vb.t[:, pg, h * 128:(h + 1) * 128], rhs=as_.t[:, c0:c0 + 4],
                                                          start=(pg == 0), stop=(pg == NPG - 1), skip_group_check=True),
                             reads=[Vb.b, as_.b], writes=[bo.b])
                T.op(DVE, lambda: nc.vector.tensor_tensor(out=oT_sb.t[:, :, TP + 4 * b:TP + 4 * b + 4],
                                                          in0=bo.t[:, 0:32].rearrange("p (h t) -> p h t", h=NH),
                                                          in1=Onew.t[:].rearrange("p (h b t) -> p h b t", h=NH, b=NSAMP)[:, :, b, :], op=ALU.add),
                     reads=[bo.b, Onew.b], writes=[oT_sb.b])
            P6.close()
            L_q.close()

            checkpoint("A2")
            P7 = Phase(A, T)
            hTm = P7.sb("hTm", [128, KD, ML], BF16)
            PN = Phase(A, T)
            norm_tiles(PN, mem, ML, 2, hTm, 0, "m")
            PN.close()
            WS = WStream(P7, 2, KD)
            WS.add(w_mkv, 0, KD, 0)
            WS.add(w_mkv, 0, KD, 512)
            mkT = P7.sb("mkT", [128, MH, ML], BF16)
            mvt = P7.sb("mvt", [128, 2, MH * HD], BF16)
            wk_sq = [P7.sb(f"sq7{i}", [128, 512], F32) for i in range(2)]
            wk_rb = [P7.sb(f"rb7{i}", [128, 512], F32) for i in range(2)]
            kn = [P7.sb(f"kn7{i}", [128, ML], F32) for i in range(2)]
            mst = [P7.sb(f"mst{i}", [128, 512], F32) for i in range(2)]
            wt = WS.get()
            it = 0
            for h in range(MH):
                bk, bk2, bk3 = banks[it % 2], banks[2 + it % 2], banks[4 + it % 2]
                k_ = kn[it % 2]
                mm_fm(bk, ML, 0, wt, KD, h, hTm, 0)
                headnorm(bk, ML, V_GKM, wk_sq[it % 2], wk_rb[it % 2], bk2, k_.t[:, :], k_.b)
                T.op(ACT, lambda: nc.scalar.copy(out=mkT.t[:, h, :], in_=k_.t[:, :]), reads=[k_.b], writes=[mkT.b])
                s_ = mst[it % 2]
                for j in range(2):
                    T.op(PE, lambda: nc.tensor.transpose(out=bk3.t[:, j * 128:(j + 1) * 128], in_=k_.t[:, j * 128:(j + 1) * 128], identity=identf),
                         reads=[k_.b, cst.b], writes=[bk3.b])
                T.op(DVE, lambda: nc.vector.tensor_copy(out=s_.t[:, 0:256], in_=bk3.t[:, 0:256]), reads=[bk3.b], writes=[s_.b])
                T.dma(SP, lambda: nc.sync.dma_start(out=memk_o[:, h * 128:(h + 1) * 128].rearrange("(j p) d -> p j d", p=128),
                                                    in_=s_.t[:, 0:256].rearrange("p (j d) -> p j d", j=2)), reads=[s_.b])
                it += 1
            wt = WS.get()
            for tile in range(2):
                bk = banks[it % 2]
                s_ = mst[it % 2]
                mm_tm(bk, 128, wt, KD, hTm, tile * 128)
                T.op(ACT, lambda: nc.scalar.copy(out=s_.t[:, :], in_=bk.t[:, :]), reads=[bk.b], writes=[s_.b])
                T.op(DVE, lambda: nc.vector.tensor_copy(out=mvt.t[:, tile, :], in_=bk.t[:, :]), reads=[bk.b], writes=[mvt.b])
                T.dma(SP, lambda: nc.sync.dma_start(out=memv_o[tile * 128:(tile + 1) * 128, :], in_=s_.t[:, :]), reads=[s_.b])
                it += 1
            Pb = [P7.sb(f"Pm{i}", [128, 512], BF16) for i in range(4)]
            rd = P7.sb("rden", [128, 512], F32)
            it = 0
            for h in range(MH):
                for qg in range(2):
                    bd, bo = banks[4 + it % 2], banks[6 + it % 2]
                    for mb in range(2):
                        bs = banks[it % 4]
                        p_ = Pb[it % 4]
                        T.op(PE, lambda: nc.tensor.matmul(bs.t[:], lhsT=mkT.t[:, h, mb * 128:(mb + 1) * 128], rhs=qmT.t[:, h, qg * 512:(qg + 1) * 512],
                                                          start=True, stop=True), reads=[mkT.b, qmT.b], writes=[bs.b])
                        T.op(ACT, lambda: nc.scalar.activation(out=p_.t[:], in_=bs.t[:], func=AF.Exp, scale=SCALE), reads=[bs.b], writes=[p_.b])
                        T.op(PE, lambda: nc.tensor.matmul(bd.t[:], lhsT=onesb, rhs=p_.t[:], start=(mb == 0), stop=(mb == 1)),
                             reads=[cb.b, p_.b], writes=[bd.b])
                        T.op(PE, lambda: nc.tensor.matmul(bo.t[:], lhsT=mvt.t[:, mb, h * 128:(h + 1) * 128], rhs=p_.t[:], start=(mb == 0), stop=(mb == 1)),
                             reads=[mvt.b, p_.b], writes=[bo.b])
                        it += 1
                    T.op(DVE, lambda: nc.vector.reciprocal(out=rd.t[:], in_=bd.t[:]), reads=[bd.b], writes=[rd.b])
                    T.op(DVE, lambda: nc.vector.tensor_tensor(out=oT_mem.t[:, h, qg * 512:(qg + 1) * 512], in0=bo.t[:], in1=rd.t[:], op=ALU.mult),
                         reads=[bo.b, rd.b], writes=[oT_mem.b])
            HS = NSAMP // 2
            cmkb = P7.sb("cmkb", [128, HS * 2, MH * HD], BF16)
            cmvb = P7.sb("cmvb", [128, HS * 2, MH * HD], BF16)
            mkTs = P7.sb("mkTs", [128, HS * 2, MH, 128], BF16)
            Ps = P7.sb("Ps", [128, 256], BF16)
            rds = P7.sb("rds", [128, 128], F32)
            ev = 0
            for hf in range(2):
                r0 = hf * HS * ML
                T.dma(POOL, lambda: nc.gpsimd.dma_start(out=cmkb.t[:], in_=cmk[r0:r0 + HS * ML, :].rearrange("(x p) c -> p x c", p=128)), writes=[cmkb.b])
                T.dma(POOL, lambda: nc.gpsimd.dma_start(out=cmvb.t[:], in_=cmv[r0:r0 + HS * ML, :].rearrange("(x p) c -> p x c", p=128)), writes=[cmvb.b])
                for x in range(HS * 2):
                    bk = banks[ev % 2]
                    bkb = bk.t[:].bitcast(BF16)
                    for h in range(MH):
                        T.op(PE, lambda: nc.tensor.transpose(out=bkb[:, h * 128:(h + 1) * 128], in_=cmkb.t[:, x, h * 128:(h + 1) * 128], identity=identb),
                             reads=[cmkb.b, cb.b], writes=[bk.b])
                    i_ap = bkb[:, 0:512].rearrange("p (h s) -> p h s", h=4)
                    if ev % 2 == 0:
                        T.op(ACT, lambda: nc.scalar.copy(out=mkTs.t[:, x, :, :], in_=i_ap), reads=[bk.b], writes=[mkTs.b])
                    else:
                        T.op(DVE, lambda: nc.vector.tensor_copy(out=mkTs.t[:, x, :, :], in_=i_ap), reads=[bk.b], writes=[mkTs.b])
                    ev += 1
                bs, bd, bo = banks[2], banks[3], banks[4]
                for bb in range(HS):
                    b = hf * HS + bb
                    for mb in range(2):
                        for h in range(MH):
                            c0 = ((bb * 2 + mb) * MH + h) * 4
                            T.op(PE, lambda: nc.tensor.matmul(bs.t[:, c0:c0 + 4], lhsT=mkTs.t[:, bb * 2 + mb, h, :], rhs=qmT.t[:, h, TP + 4 * b:TP + 4 * b + 4],
                                                              start=True, stop=True), reads=[mkTs.b, qmT.b], writes=[bs.b])
                T.op(ACT, lambda: nc.scalar.activation(out=Ps.t[:], in_=bs.t[:, 0:256], func=AF.Exp, scale=SCALE), reads=[bs.b], writes=[Ps.b])
                p4 = Ps.t[:].rearrange("p (b m x) -> p b m x", b=HS, m=2)
                for mb in range(2):
                    T.op(PE, lambda: nc.tensor.matmul(bd.t[:, 0:128].rearrange("p (b x) -> p b x", b=HS), lhsT=onesb, rhs=p4[:, :, mb, :],
                                                      start=(mb == 0), stop=(mb == 1)), reads=[cb.b, Ps.b], writes=[bd.b])
                for bb in range(HS):
                    for h in range(MH):
                        for mb in range(2):
                            c0 = ((bb * 2 + mb) * MH + h) * 4
                            o0 = (bb * MH + h) * 4
                            T.op(PE, lambda: nc.tensor.matmul(bo.t[:, o0:o0 + 4], lhsT=cmvb.t[:, bb * 2 + mb, h * 128:(h + 1) * 128], rhs=Ps.t[:, c0:c0 + 4],
                                                              start=(mb == 0), stop=(mb == 1), skip_group_check=True),
                                 reads=[cmvb.b, Ps.b], writes=[bo.b])
                T.op(DVE, lambda: nc.vector.reciprocal(out=rds.t[:], in_=bd.t[:, 0:128]), reads=[bd.b], writes=[rds.b])
                T.op(DVE, lambda: nc.vector.tensor_tensor(
                    out=oT_mem.t[:, :, TP + hf * HS * 4:TP + (hf + 1) * HS * 4].rearrange("p h (b t) -> p h b t", t=4),
                    in0=bo.t[:, 0:128].rearrange("p (b h t) -> p h b t", b=HS, h=MH),
                    in1=rds.t[:].rearrange("p (b h t) -> p h b t", b=HS, h=MH), op=ALU.mult), reads=[bo.b, rds.b], writes=[oT_mem.b])
            P7.close()
            L_qm.close()
            dump(T, "oT_sb", oT_sb, "p k t -> p (k t)")
            dump(T, "oT_mem", oT_mem, "p k t -> p (k t)")

            checkpoint("A3")
            L_hT = Phase(A, T)
            hT = L_hT.sb("hT", [128, KD, NTP], BF16)
            T.dma(SP, lambda: nc.sync.dma_start(out=hT.t[:].rearrange("p k t -> p (k t)"), in_=hT_scr[:, :]), reads=[hT_buf], writes=[hT.b])
            L_mrg = Phase(A, T)
            mrg = L_mrg.sb("merged", [128, KD, NTP], BF16)
            T.op(DVE, lambda: nc.vector.memset(mrg.t[:, :, NT:NTP], 0.0), writes=[mrg.b])
            P8 = Phase(A, T)
            WS = WStream(P8, 3, KD)
            for cg in range(4):
                WS.add(w_co, 0, 4, cg * 512)
                WS.add(w_in, 0, KD, 4608 + cg * 512)
                WS.add(w_so, 0, 8, cg * 512)
                WS.add(w_in, 0, KD, 4608 + 2048 + cg * 512)
                WS.add(w_mo, 0, 4, cg * 512)
                WS.add(w_in, 0, KD, 4608 + 4096 + cg * 512)
            sgb = [P8.sb(f"sg{i}", [128, 512], F32) for i in range(2)]
            macc = [[P8.sb(f"macc{mm}_{gi}", [128, 512], F32) for gi in range(3)] for mm in range(4)]
            it = 0
            for cg in range(4):
                for br, (src, kc, boff) in enumerate(((cT, 4, 0), (oT_sb, 8, 16), (oT_mem, 4, 32))):
                    wy = WS.get()
                    wg = WS.get(hold=1)
                    for mm in range(4):
                        m = cg * 4 + mm
                        for gi, (t0, n) in enumerate(GROUPS):
                            by, bg = banks[(2 * it) % 8], banks[(2 * it + 1) % 8]
                            s_ = sgb[it % 2]
                            mm_fm(by, n, 0, wy, kc, mm, src, t0)
                            mm_fm(bg, n, 0, wg, KD, mm, hT, t0)
                            T.op(ACT, lambda: nc.scalar.activation(out=s_.t[:, 0:n], in_=bg.t[:, 0:n], func=AF.Sigmoid,
                                                                   bias=vecs.t[:, V_BGATE + boff + m:V_BGATE + boff + m + 1]),
                                 reads=[bg.b, vecs.b], writes=[s_.b])
                            a_ = macc[mm][gi]
                            if br == 0:
                                T.op(DVE, lambda: nc.vector.tensor_tensor(out=a_.t[:, 0:n], in0=by.t[:, 0:n], in1=s_.t[:, 0:n], op=ALU.mult),
                                     reads=[by.b, s_.b], writes=[a_.b])
                            else:
                                T.op(DVE, lambda: nc.vector.tensor_tensor(out=s_.t[:, 0:n], in0=by.t[:, 0:n], in1=s_.t[:, 0:n], op=ALU.mult),
                                     reads=[by.b, s_.b], writes=[s_.b])
                                if br == 1:
                                    T.op(DVE, lambda: nc.vector.tensor_tensor(out=a_.t[:, 0:n], in0=a_.t[:, 0:n], in1=s_.t[:, 0:n], op=ALU.add),
                                         reads=[a_.b, s_.b], writes=[a_.b])
                                else:
                                    T.op(DVE, lambda: nc.vector.tensor_tensor(out=mrg.t[:, m, t0:t0 + n], in0=a_.t[:, 0:n], in1=s_.t[:, 0:n], op=ALU.add),
                                         reads=[a_.b, s_.b], writes=[mrg.b])
                            it += 1
            P8.close()
            dump(T, "mrg", mrg, "p k t -> p (k t)")
            L_hT.close()
            L_cT.close()
            L_o.close()

            checkpoint("M")
            L_x1 = Phase(A, T)
            x1 = L_x1.sb("x1", [128, 9, D], F32)
            x1b = [L_x1.buf(f"x1_{i}") for i in range(9)]
            P9 = Phase(A, T)
            WS = WStream(P9, 2, KD)
            for cg in range(4):
                WS.add(w_o, 0, KD, cg * 512)
            xr = [P9.sb(f"xr{i}", [128, 512], F32) for i in range(3)]
            it = 0
            for cg in range(4):
                wt = WS.get()
                for tile in range(9):
                    rows = 128 if tile < 8 else TS
                    bk = banks[it % 4]
                    x_ = xr[it % 3]
                    xsrc = x_own[tile * 128:(tile + 1) * 128, cg * 512:(cg + 1) * 512] if tile < 8 else x_s[:, cg * 512:(cg + 1) * 512]
                    T.dma(SP, lambda: nc.sync.dma_start(out=x_.t[0:rows, :], in_=xsrc), writes=[x_.b])
                    mm_tm(bk, rows, wt, KD, mrg, tile * 128)
                    T.op(DVE, lambda: nc.vector.tensor_tensor(out=x1.t[0:rows, tile, cg * 512:(cg + 1) * 512], in0=bk.t[0:rows, :], in1=x_.t[0:rows, :], op=ALU.add),
                         reads=[bk.b, x_.b], writes=[x1b[tile]])
                    it += 1
            P9.close()
            dump(T, "x1", x1, "p k t -> p (k t)", F32, reads=x1b)
            L_mrg.close()

            checkpoint("O")
            P10 = Phase(A, T)
            h2T = P10.sb("h2T", [128, KD, NT], BF16)
            PN = Phase(A, T)
            gbc = PN.sb("gbc2", [128, D], F32)
            T.dma(SP, lambda: nc.sync.dma_start(out=gbc.t[:], in_=grow[1:2, :].partition_broadcast(128)), writes=[gbc.b])
            xn2 = [PN.sb(f"xn2{i}", [128, D], BF16) for i in range(2)]
            junk = PN.sb("junk2", [128, D], BF16)
            ssr = [PN.sb(f"ss2{i}", [128, 2], F32) for i in range(2)]
            for tile in range(9):
                rows = 128 if tile < 8 else TS
                n_, s_ = xn2[tile % 2], ssr[tile % 2]
                T.op(ACT, lambda: nc.scalar.activation(out=junk.t[0:rows, :], in_=x1.t[0:rows, tile, :], func=AF.Square, accum_out=s_.t[0:rows, 0:1]),
                     reads=[x1b[tile]], writes=[junk.b, s_.b])
                T.op(ACT, lambda: nc.scalar.activation(out=s_.t[0:rows, 1:2], in_=s_.t[0:rows, 0:1], func=AF.Ln, scale=1.0 / D, bias=EPS), reads=[s_.b], writes=[s_.b])
                T.op(ACT, lambda: nc.scalar.activation(out=s_.t[0:rows, 1:2], in_=s_.t[0:rows, 1:2], func=AF.Exp, scale=-0.5), reads=[s_.b], writes=[s_.b])
                T.op(DVE, lambda: nc.vector.scalar_tensor_tensor(out=n_.t[0:rows, :], in0=x1.t[0:rows, tile, :], scalar=s_.t[0:rows, 1:2], in1=gbc.t[0:rows, :],
                                                                 op0=ALU.mult, op1=ALU.mult), reads=[x1b[tile], s_.b, gbc.b], writes=[n_.b])
                for half in range(2):
                    bk = banks[(2 * tile + half) % 4]
                    bkb = bk.t[:].bitcast(BF16)
                    for kk in range(8):
                        k = half * 8 + kk
                        T.op(PE, lambda: nc.tensor.transpose(out=bkb[:, kk * 128:kk * 128 + rows], in_=n_.t[0:rows, k * 128:(k + 1) * 128],
                                                             identity=identb[0:rows, 0:rows]), reads=[n_.b, cb.b], writes=[bk.b])
                    src = bkb.rearrange("p (k t) -> p k t", k=8)[:, :, 0:rows]
                    d_ap = h2T.t[:, half * 8:half * 8 + 8, tile * 128:tile * 128 + rows]
                    if half == 0:
                        T.op(ACT, lambda: nc.scalar.copy(out=d_ap, in_=src), reads=[bk.b], writes=[h2T.b])
                    else:
                        T.op(DVE, lambda: nc.vector.tensor_copy(out=d_ap, in_=src), reads=[bk.b], writes=[h2T.b])
            PN.close()
            WS = WStream(P10, 3, KD)
            for e in range(8):
                WS.add(w_up, 0, KD, e * 1024)
                WS.add(w_up, 0, KD, e * 1024 + 512)
                for cg in range(4):
                    WS.add(w_dn, e * 1024, 8, cg * 512)
            ffq = P10.sb("ffq", [128, 8, NTP], BF16)
            T.op(DVE, lambda: nc.vector.memset(ffq.t[:, :, NT:NTP], 0.0), writes=[ffq.b])
            sqf = [P10.sb(f"sqf{i}", [128, 512], F32) for i in range(2)]
            it = 0
            for e in range(8):
                for ub in range(2):
                    wt = WS.get()
                    for mm in range(4):
                        for gi, (t0, n) in enumerate(GROUPS):
                            bk = banks[it % 4]
                            s_ = sqf[it % 2]
                            mm_fm(bk, n, 0, wt, KD, mm, h2T, t0)
                            T.op(ACT, lambda: nc.scalar.activation(out=s_.t[:, 0:n], in_=bk.t[:, 0:n], func=AF.Square), reads=[bk.b], writes=[s_.b])
                            T.op(DVE, lambda: nc.vector.scalar_tensor_tensor(out=ffq.t[:, ub * 4 + mm, t0:t0 + n], in0=bk.t[:, 0:n], scalar=0.0, in1=s_.t[:, 0:n],
                                                                             op0=ALU.is_gt, op1=ALU.mult), reads=[bk.b, s_.b], writes=[ffq.b])
                            it += 1
                for cg in range(4):
                    wt = WS.get()
                    for tile in range(9):
                        rows = 128 if tile < 8 else TS
                        bk = banks[4 + it % 4]
                        mm_tm(bk, rows, wt, 8, ffq, tile * 128)
                        xs_ = x1.t[0:rows, tile, cg * 512:(cg + 1) * 512]
                        T.op(DVE, lambda: nc.vector.tensor_tensor(out=xs_, in0=bk.t[0:rows, :], in1=xs_, op=ALU.add), reads=[bk.b, x1b[tile]], writes=[x1b[tile]])
                        it += 1
            for tile in range(9):
                if tile < 8:
                    T.dma(SP, lambda: nc.sync.dma_start(out=y_own[tile * 128:(tile + 1) * 128, :], in_=x1.t[:, tile, :]), reads=[x1b[tile]])
                else:
                    T.dma(SP, lambda: nc.sync.dma_start(out=y_s[:, :], in_=x1.t[0:TS, tile, :]), reads=[x1b[tile]])
            P10.close()
            T.finish()
            print(f"[kernel] arena peak {A.peak / 1024:.1f} KiB/partition; instr pe={PE.cnt} dve={DVE.cnt} act={ACT.cnt}")
    except _Stop:
        pass
    return nc


def _consts():
    c = np.zeros((128, C_END), np.float32)
    j = np.arange(128)[:, None]
    s = np.arange(128)[None, :]
    c[:, C_IDENT:C_IDENT + 128] = np.eye(128, dtype=np.float32)
    c[:, C_TINC:C_TINC + 128] = (j >= s)
    c[:, C_LSTR:C_LSTR + 128] = (j < s)
    c[:, C_ONES:C_ONES + 128] = 1.0
    n = np.arange(512)[None, :]
    for r in range(4):
        c[:, C_MASKD + r * 512:C_MASKD + (r + 1) * 512] = (128 * r + j < n)
    rows = np.arange(64)[:, None]
    cols = np.arange(512)[None, :]
    bt = cols % 64
    c[0:64, C_M64:C_M64 + 512] = ((rows // 4) == (bt // 4)) & ((rows % 4) < (bt % 4))
    return c


def _fm(v, k):
    return np.ascontiguousarray(np.asarray(v, np.float32).reshape(k, 128).T)


def make_in_maps(inp, npool=None, core_list=range(8)):
    f = lambda a: np.ascontiguousarray(np.asarray(a, np.float32))
    cst = _consts()
    x_prompt, x_sample, mem_prompt = f(inp["x_prompt"]), f(inp["x_sample"]), f(inp["mem_prompt"])
    poolk = f(inp["cache_sb_k"])[0].reshape(-1, NH * HD)
    poolv = f(inp["cache_sb_v"])[0].reshape(-1, NH * HD)
    pt = np.asarray(inp["page_table"], np.int32)
    sconv = f(inp["state_conv"])[0]
    cmk, cmv = f(inp["cache_mem_k"])[0], f(inp["cache_mem_v"])[0]
    shared = {
        "poolk": poolk, "poolv": poolv,
        "w_in": f(inp["w_in"])[0], "w_co": f(inp["w_conv_out"])[0], "w_so": f(inp["w_sb_out"])[0], "w_mo": f(inp["w_mem_out"])[0],
        "w_o": f(inp["w_o"])[0], "w_up": f(inp["w_up"])[0], "w_dn": f(inp["w_down"])[0], "w_mkv": f(inp["w_mem_kv"])[0],
        "grow": np.ascontiguousarray(np.stack([f(inp["g_mix"])[0], f(inp["g_mlp"])[0], f(inp["g_mem"])[0]])),
        "bsb": f(inp["b_sb"]).reshape(1, NH), "cst": cst,
    }
    vbase = np.zeros((128, V_END), np.float32)
    vbase[:, V_BGATE:V_BGATE + 48] = _fm(f(inp["b_gate"])[0], 48)
    wdw = f(inp["w_dw"])[0]
    vbase[:, V_WDW:V_WDW + 124] = wdw.T.reshape(4, 128, 31).transpose(1, 0, 2).reshape(128, 124)
    vbase[:, V_BDW:V_BDW + 4] = _fm(f(inp["b_dw"])[0], 4)
    vbase[:, V_GCLN:V_GCLN + 4] = _fm(f(inp["g_conv_ln"])[0], 4)
    vbase[:, V_BCLN:V_BCLN + 4] = _fm(f(inp["b_conv_ln"])[0], 4)
    vbase[:, V_GQSB] = f(inp["g_q_sb"])[0]
    vbase[:, V_GKSB] = f(inp["g_k_sb"])[0]
    vbase[:, V_GQM] = f(inp["g_q_mem"])[0]
    vbase[:, V_GKM] = f(inp["g_k_mem"])[0]
    vbase[:, V_ROWID] = np.arange(128, dtype=np.float32)
    maps = []
    for c in core_list:
        b, half = c // 2, c % 2
        v = vbase.copy()
        v[:, V_PM] = float(half)
        m = dict(shared)
        m.update({
            "x_prev": x_prompt[b, 0:TP], "x_own": x_prompt[b, half * TP:(half + 1) * TP],
            "x_s": x_sample[c * NSAMP:(c + 1) * NSAMP].reshape(TS, D), "mem": mem_prompt[b],
            "pt": np.ascontiguousarray(pt[c * NSAMP:(c + 1) * NSAMP].reshape(1, NSAMP * NPG)),
            "sconv": sconv[c * NSAMP:(c + 1) * NSAMP].reshape(NSAMP * 30, CW),
            "cmk": cmk[c * NSAMP:(c + 1) * NSAMP].reshape(NSAMP * ML, MH * HD),
            "cmv": cmv[c * NSAMP:(c + 1) * NSAMP].reshape(NSAMP * ML, MH * HD),
            "vecs": v,
        })
        maps.append(m)
    return maps


def assemble(res):
    B = 4
    yp = np.zeros((B, 2 * TP, D), np.float32)
    ys = np.zeros((128, 4, D), np.float32)
    kp = np.zeros((1, B, 2 * TP, NH, HD), np.float32)
    vp = np.zeros_like(kp)
    ks = np.zeros((1, 128, 4, NH, HD), np.float32)
    vs = np.zeros_like(ks)
    cp = np.zeros((1, B, 30, CW), np.float32)
    cs = np.zeros((1, 128, 30, CW), np.float32)
    mk = np.zeros((1, B, ML, MH, HD), np.float32)
    mv = np.zeros_like(mk)
    for c, r in enumerate(res):
        b, half = c // 2, c % 2
        sl = slice(half * TP, (half + 1) * TP)
        ss = slice(c * NSAMP, (c + 1) * NSAMP)
        yp[b, sl] = r["y_own"]
        ys[ss] = r["y_s"].reshape(NSAMP, 4, D)
        kp[0, b, sl] = r["kp"].reshape(TP, NH, HD)
        vp[0, b, sl] = r["vp"].reshape(TP, NH, HD)
        ks[0, ss] = r["ks"].reshape(NSAMP, 4, NH, HD)
        vs[0, ss] = r["vs"].reshape(NSAMP, 4, NH, HD)
        cs[0, ss] = r["convs"]
        if half == 1:
            cp[0, b] = r["convp"]
        else:
            mk[0, b] = r["memk"].reshape(ML, MH, HD)
            mv[0, b] = r["memv"].reshape(ML, MH, HD)
    return (yp, ys, kp, vp, ks, vs, cp, cs, mk, mv)


def kernel(**inputs):
    npool = int(np.asarray(inputs["cache_sb_k"]).shape[1])
    nc = build_nc(npool)
    in_maps = make_in_maps(inputs)
    res = run_bass_kernel_spmd(nc, in_maps, core_ids=list(range(8)))
    return assemble(res.results)
```

```python
import numpy as np
from contextlib import ExitStack
import concourse.bass as bass
import concourse.mybir as mybir
from concourse.bass_utils import run_bass_kernel_spmd

F32 = mybir.dt.float32
BF16 = mybir.dt.bfloat16
I32 = mybir.dt.int32
AF = mybir.ActivationFunctionType
ALU = mybir.AluOpType

D = 2048
KD = 16
NH = 8
HD = 128
CW = 512
MH = 4
ML = 256
DFF = 8192
TP = 1024
TS = 64
NT = TP + TS
NTP = TP + 128
NSAMP = 16
NPG = 16
INC = 10752
SCALE = HD ** -0.5
EPS = 1e-6
GROUPS = [(0, 512), (512, 512), (1024, 64)]
C_IDENT, C_TINC, C_LSTR, C_ONES, C_MASKD, C_M64, C_END = 0, 128, 256, 384, 512, 2560, 3072
V_BGATE, V_WDW, V_BDW, V_GCLN, V_BCLN, V_GQSB, V_GKSB, V_GQM, V_GKM, V_ROWID, V_PM, V_END = \
    0, 48, 172, 176, 180, 184, 185, 186, 187, 188, 189, 190


class Buf:
    __slots__ = ("name", "w", "r", "xr")

    def __init__(self, name, fence=None, xr=False):
        self.name = name
        self.w = None
        self.r = dict(fence) if fence else {}
        self.xr = xr


class Eng:
    def __init__(self, name, eng, sem, self_sync):
        self.name, self.eng, self.sem, self.self_sync = name, eng, sem, self_sync
        self.cnt = 0
        self.known = {}


class TB:
    __slots__ = ("t", "b")

    def __init__(self, t, b):
        self.t, self.b = t, b


class Trk:
    NDMA = 16

    def __init__(self, nc, stack):
        self.nc = nc
        mk = lambda n: stack.enter_context(nc.semaphore(n))
        self.pe = Eng("pe", nc.tensor, mk("s_pe"), False)
        self.dve = Eng("dve", nc.vector, mk("s_dve"), True)
        self.act = Eng("act", nc.scalar, mk("s_act"), True)
        self.pool = Eng("pool", nc.gpsimd, mk("s_pool"), True)
        self.sp = Eng("sp", nc.sync, mk("s_sp"), False)
        self.dsem = {q: [[mk(f"d_{q}{i}"), 0] for i in range(self.NDMA)] for q in ("sp", "pool")}
        self.dnext = {"sp": 0, "pool": 0}
        self.fence = {}

    def newbuf(self, name, xr=False):
        return Buf(name, self.fence, xr)

    def release(self, bufs):
        for b in bufs:
            evs = list(b.r.values())
            if b.w is not None:
                evs.append(b.w)
            for sem, val in evs:
                k = id(sem)
                if k not in self.fence or self.fence[k][1] < val:
                    self.fence[k] = (sem, val)

    def _wait(self, E, evs):
        best = {}
        for ev in evs:
            if ev is None:
                continue
            sem, val = ev
            k = id(sem)
            if k not in best or best[k][1] < val:
                best[k] = (sem, val)
        for k, (sem, val) in best.items():
            if sem is E.sem and not E.self_sync:
                continue
            if E.known.get(k, 0) >= val:
                continue
            E.eng.wait_ge(sem, val)
            E.known[k] = val

    @staticmethod
    def _deps(reads, writes, own=None):
        evs = []
        for b in reads:
            evs.append(b.w)
            if b.xr:
                evs.extend(e for e in b.r.values() if e[0] is not own)
        for b in writes:
            evs.append(b.w)
            evs.extend(b.r.values())
        return evs

    @staticmethod
    def _mark(ev, reads, writes):
        k = id(ev[0])
        for b in reads:
            o = b.r.get(k)
            if o is None or o[1] < ev[1]:
                b.r[k] = ev
        for b in writes:
            b.w = ev
            b.r = {}

    def op(self, E, fn, reads=(), writes=()):
        self._wait(E, self._deps(reads, writes, E.sem))
        inst = fn()
        E.cnt += 1
        inst.then_inc(E.sem, 1)
        self._mark((E.sem, E.cnt), reads, writes)
        return inst

    def dma(self, E, fn, reads=(), writes=()):
        slots = self.dsem[E.name]
        i = self.dnext[E.name]
        self.dnext[E.name] = (i + 1) % len(slots)
        sem, n = slots[i]
        evs = self._deps(reads, writes)
        if n > 0:
            evs.append((sem, 16 * n))
        self._wait(E, evs)
        inst = fn()
        inst.then_inc(sem, 16)
        slots[i][1] = n + 1
        self._mark((sem, 16 * (n + 1)), reads, writes)

    def finish(self):
        evs = []
        for q in self.dsem:
            for sem, n in self.dsem[q]:
                if n > 0:
                    evs.append((sem, 16 * n))
        self._wait(self.sp, evs)


class Arena:
    def __init__(self, nc, stack, nbytes):
        self.t = stack.enter_context(nc.sbuf_tensor("arena", [128, nbytes // 4], F32))
        self.free_list = [(0, nbytes)]
        self.used = 0
        self.peak = 0

    def alloc(self, nbytes, name):
        nbytes = (nbytes + 31) // 32 * 32
        for i, (off, sz) in enumerate(self.free_list):
            if sz >= nbytes:
                if sz == nbytes:
                    self.free_list.pop(i)
                else:
                    self.free_list[i] = (off + nbytes, sz - nbytes)
                self.used += nbytes
                self.peak = max(self.peak, self.used)
                return off, nbytes
        raise MemoryError(f"arena: cannot fit {name} ({nbytes} B/partition); used={self.used} free={self.free_list}")

    def free(self, off, nbytes):
        self.used -= nbytes
        fl = sorted(self.free_list + [(off, nbytes)])
        out = []
        for o, z in fl:
            if out and out[-1][0] + out[-1][1] == o:
                out[-1] = (out[-1][0], out[-1][1] + z)
            else:
                out.append((o, z))
        self.free_list = out

    def view(self, off, shape, dt):
        n = 1
        for d in shape[1:]:
            n *= d
        esz = 2 if dt == BF16 else 4
        words = (n * esz + 3) // 4
        ap = self.t[:, off // 4:off // 4 + words]
        if dt != F32:
            ap = ap.bitcast(dt)
        ap = ap[:, 0:n]
        fr = shape[1:]
        if len(fr) == 2:
            ap = ap.rearrange("p (a b) -> p a b", a=fr[0])
        elif len(fr) == 3:
            ap = ap.rearrange("p (a b c) -> p a b c", a=fr[0], b=fr[1])
        return ap


class Phase:
    def __init__(self, A, T):
        self.A, self.T = A, T
        self.bufs = []
        self.blocks = []

    def buf(self, name):
        b = self.T.newbuf(name)
        self.bufs.append(b)
        return b

    def sb(self, name, shape, dt):
        n = 1
        for d in shape[1:]:
            n *= d
        off, nb = self.A.alloc(n * (2 if dt == BF16 else 4), name)
        self.blocks.append((off, nb))
        return TB(self.A.view(off, shape, dt), self.buf(name))

    def close(self):
        self.T.release(self.bufs)
        for off, nb in self.blocks:
            self.A.free(off, nb)
        self.bufs, self.blocks = [], []


class _Stop(Exception):
    pass


def build_nc(npool, dbg=False, stop_after=None):
    nc = bass.Bass("TRN2", target_bir_lowering=False)
    din = lambda n, s, d=F32: nc.dram_tensor(n, s, d, kind="ExternalInput").ap()
    dout = lambda n, s: nc.dram_tensor(n, s, F32, kind="ExternalOutput").ap()
    x_prev, x_own, x_s, mem = din("x_prev", [TP, D]), din("x_own", [TP, D]), din("x_s", [TS, D]), din("mem", [ML, D])
    poolk, poolv = din("poolk", [npool * 128, NH * HD]), din("poolv", [npool * 128, NH * HD])
    pt_d = din("pt", [1, NSAMP * NPG], I32)
    sconv = din("sconv", [NSAMP * 30, CW])
    cmk, cmv = din("cmk", [NSAMP * ML, MH * HD]), din("cmv", [NSAMP * ML, MH * HD])
    w_in, w_co, w_so, w_mo = din("w_in", [D, INC]), din("w_co", [CW, D]), din("w_so", [NH * HD, D]), din("w_mo", [MH * HD, D])
    w_o, w_up, w_dn, w_mkv = din("w_o", [D, D]), din("w_up", [D, DFF]), din("w_dn", [DFF, D]), din("w_mkv", [D, 2 * MH * HD])
    grow = din("grow", [3, D])
    vecs_d = din("vecs", [128, V_END])
    bsb_d = din("bsb", [1, NH])
    cst_d = din("cst", [128, C_END])
    y_own, y_s = dout("y_own", [TP, D]), dout("y_s", [TS, D])
    kp_o, vp_o = dout("kp", [TP, NH * HD]), dout("vp", [TP, NH * HD])
    ks_o, vs_o = dout("ks", [TS, NH * HD]), dout("vs", [TS, NH * HD])
    convp_o, convs_o = dout("convp", [30, CW]), dout("convs", [NSAMP, 30, CW])
    memk_o, memv_o = dout("memk", [ML, MH * HD]), dout("memv", [ML, MH * HD])
    hT_scr = nc.dram_tensor("hT_scr", [128, KD * NTP], BF16, kind="Internal").ap()
    u_scr_p = nc.dram_tensor("u_scr_p", [128, 4 * (30 + TP)], F32, kind="Internal").ap()
    u_scr_s = nc.dram_tensor("u_scr_s", [128, 4 * NSAMP * 34], F32, kind="Internal").ap()

    def dump(T, name, tb, flat, dt=BF16, reads=None):
        if not dbg:
            return
        n = 1
        for d_ in tb.t.shape[1:]:
            n *= d_
        o = nc.dram_tensor("dbg_" + name, [128, n], dt, kind="ExternalOutput").ap()
        T.dma(T.sp, lambda: nc.sync.dma_start(out=o[:, :], in_=tb.t[:].rearrange(flat)), reads=reads or [tb.b])

    try:
        with ExitStack() as st:
            T = Trk(nc, st)
            PE, DVE, ACT, POOL, SP = T.pe, T.dve, T.act, T.pool, T.sp

            def checkpoint(name):
                if dbg or stop_after == "print":
                    print(f"[ckpt] {name}: pe={PE.cnt} dve={DVE.cnt} act={ACT.cnt} arena_used={A.used / 1024:.1f} peak={A.peak / 1024:.1f}")
                    A.peak = A.used
                if stop_after == name:
                    T.finish()
                    raise _Stop()
            A = Arena(nc, st, 206 * 1024)
            G = Phase(A, T)

            banks = [TB(st.enter_context(nc.psum_tensor(f"pb{i}", [128, 512], F32)), T.newbuf(f"pb{i}", xr=True)) for i in range(8)]

            def ld(dst, src, reads=(), q=None):
                E = q or SP
                T.dma(E, lambda: E.eng.dma_start(out=dst_ap(dst), in_=src), reads=list(reads), writes=[dst.b])

            def dst_ap(x):
                return x.t[:] if isinstance(x, TB) else x

            cst = G.sb("cst", [128, C_MASKD], F32)
            vecs = G.sb("vecs", [128, V_END], F32)
            bsb = G.sb("bsb", [128, NH], F32)
            T.dma(SP, lambda: nc.sync.dma_start(out=cst.t[:], in_=cst_d[:, 0:C_MASKD]), writes=[cst.b])
            T.dma(SP, lambda: nc.sync.dma_start(out=vecs.t[:], in_=vecs_d[:, :]), writes=[vecs.b])
            T.dma(SP, lambda: nc.sync.dma_start(out=bsb.t[:], in_=bsb_d.partition_broadcast(128)), writes=[bsb.b])
            identf = cst.t[:, C_IDENT:C_IDENT + 128]
            tinc = cst.t[:, C_TINC:C_TINC + 128]
            lstr = cst.t[:, C_LSTR:C_LSTR + 128]
            onesf = cst.t[:, C_ONES:C_ONES + 128]
            cb = G.sb("cstb", [128, 256], BF16)
            T.op(DVE, lambda: nc.vector.tensor_copy(out=cb.t[:, 0:128], in_=identf), reads=[cst.b], writes=[cb.b])
            T.op(DVE, lambda: nc.vector.tensor_copy(out=cb.t[:, 128:256], in_=onesf), reads=[cst.b], writes=[cb.b])
            identb = cb.t[:, 0:128]
            onesb = cb.t[:, 128:256]
            pm = vecs.t[:, V_PM:V_PM + 1]
            sm = G.sb("small", [128, 16], F32)
            T.op(DVE, lambda: nc.vector.tensor_scalar(out=sm.t[:, 0:1], in0=pm, scalar1=SCALE, scalar2=None, op0=ALU.mult),
                 reads=[vecs.b], writes=[sm.b])
            T.op(DVE, lambda: nc.vector.tensor_scalar(out=sm.t[:, 1:2], in0=pm, scalar1=-1.0, scalar2=30.0, op0=ALU.add, op1=ALU.mult),
                 reads=[vecs.b], writes=[sm.b])
            T.op(DVE, lambda: nc.vector.tensor_scalar(out=sm.t[:, 2:10], in0=bsb.t[:], scalar1=pm, scalar2=sm.t[:, 1:2],
                                                      op0=ALU.mult, op1=ALU.add), reads=[vecs.b, bsb.b, sm.b], writes=[sm.b])
            ptb = G.sb("ptb", [128, NSAMP * NPG], I32)
            idx = G.sb("idx", [128, NSAMP * NPG], I32)
            T.dma(SP, lambda: nc.sync.dma_start(out=ptb.t[:], in_=pt_d.partition_broadcast(128)), writes=[ptb.b])
            T.op(DVE, lambda: nc.vector.tensor_scalar(out=idx.t[:], in0=ptb.t[:], scalar1=128.0, scalar2=vecs.t[:, V_ROWID:V_ROWID + 1],
                                                      op0=ALU.mult, op1=ALU.add), reads=[ptb.b, vecs.b], writes=[idx.b])
            dd = T.newbuf("convs_rows")
            T.dma(SP, lambda: nc.sync.dma_start(out=convs_o[:, 0:26, :], in_=sconv.rearrange("(b r) c -> b r c", r=30)[:, 4:30, :]),
                  writes=[dd])

            class WStream:
                def __init__(self, ph, nslots, kc):
                    self.slots = [ph.sb(f"w{i}", [128, kc, 512], BF16) for i in range(nslots)]
                    self.i = 0
                    self.plan = []
                    self.issued = 0
                    self.taken = 0

                def add(self, W, r0, kc, c0):
                    self.plan.append((W, r0, kc, c0))

                def _issue(self):
                    W, r0, kc, c0 = self.plan[self.issued]
                    s = self.slots[self.issued % len(self.slots)]
                    T.dma(POOL, lambda: nc.gpsimd.dma_start(
                        out=s.t[:, 0:kc, :], in_=W[r0:r0 + kc * 128, c0:c0 + 512].rearrange("(k p) c -> p k c", p=128)),
                        writes=[s.b])
                    self.issued += 1

                def get(self, hold=0):
                    while self.issued < len(self.plan) and self.issued < self.taken + len(self.slots) - hold:
                        self._issue()
                    s = self.slots[self.taken % len(self.slots)]
                    self.taken += 1
                    return s

                def prefetch(self):
                    while self.issued < len(self.plan) and self.issued < self.taken + len(self.slots):
                        self._issue()

            def mm_fm(bank, n, col0, wt, kc, mcol, src, tok0):
                for k in range(kc):
                    T.op(PE, lambda: nc.tensor.matmul(bank.t[:, col0:col0 + n], lhsT=wt.t[:, k, mcol * 128:(mcol + 1) * 128],
                                                      rhs=src.t[:, k, tok0:tok0 + n], start=(k == 0), stop=(k == kc - 1)),
                         reads=[wt.b, src.b], writes=[bank.b])

            def mm_tm(bank, rows, wt, kc, src, tok0, koff=0):
                for k in range(kc):
                    T.op(PE, lambda: nc.tensor.matmul(bank.t[:, :], lhsT=src.t[:, koff + k, tok0:tok0 + 128], rhs=wt.t[:, k, :],
                                                      start=(k == 0), stop=(k == kc - 1)), reads=[wt.b, src.b], writes=[bank.b])

            def rstd_from(bank, n, inv, out):
                T.op(ACT, lambda: nc.scalar.activation(out=out.t[:, 0:n], in_=bank.t[:, 0:n], func=AF.Ln, scale=inv, bias=EPS),
                     reads=[bank.b], writes=[out.b])
                T.op(ACT, lambda: nc.scalar.activation(out=out.t[:, 0:n], in_=out.t[:, 0:n], func=AF.Exp, scale=-0.5),
                     reads=[out.b], writes=[out.b])

            def headnorm(bank, n, gcol, sq, rb, bank2, out_ap, out_b):
                T.op(ACT, lambda: nc.scalar.activation(out=sq.t[:, 0:n], in_=bank.t[:, 0:n], func=AF.Square), reads=[bank.b], writes=[sq.b])
                T.op(PE, lambda: nc.tensor.matmul(bank2.t[:, 0:n], lhsT=onesf, rhs=sq.t[:, 0:n], start=True, stop=True),
                     reads=[cst.b, sq.b], writes=[bank2.b])
                rstd_from(bank2, n, 1.0 / HD, rb)
                T.op(DVE, lambda: nc.vector.scalar_tensor_tensor(out=out_ap, in0=bank.t[:, 0:n], scalar=vecs.t[:, gcol:gcol + 1],
                                                                 in1=rb.t[:, 0:n], op0=ALU.mult, op1=ALU.mult),
                     reads=[bank.b, vecs.b, rb.b], writes=[out_b])

            def norm_tiles(ph, xd, ntok, grow_i, dst, dtok0, tag):
                gbc = ph.sb(f"gbc{tag}", [128, D], F32)
                T.dma(SP, lambda: nc.sync.dma_start(out=gbc.t[:], in_=grow[grow_i:grow_i + 1, :].partition_broadcast(128)), writes=[gbc.b])
                xs = [ph.sb(f"xt{tag}{i}", [128, D], F32) for i in range(2)]
                xn = [ph.sb(f"xn{tag}{i}", [128, D], BF16) for i in range(2)]
                junk = ph.sb(f"junk{tag}", [128, D], BF16)
                ssr = [ph.sb(f"ss{tag}{i}", [128, 2], F32) for i in range(2)]
                ntile = (ntok + 127) // 128
                for i in range(ntile):
                    rows = min(128, ntok - i * 128)
                    x_, n_, s_ = xs[i % 2], xn[i % 2], ssr[i % 2]
                    T.dma(SP, lambda: nc.sync.dma_start(out=x_.t[0:rows, :], in_=xd[i * 128:i * 128 + rows, :]), writes=[x_.b])
                    T.op(ACT, lambda: nc.scalar.activation(out=junk.t[0:rows, :], in_=x_.t[0:rows, :], func=AF.Square, accum_out=s_.t[0:rows, 0:1]),
                         reads=[x_.b], writes=[junk.b, s_.b])
                    T.op(ACT, lambda: nc.scalar.activation(out=s_.t[0:rows, 1:2], in_=s_.t[0:rows, 0:1], func=AF.Ln, scale=1.0 / D, bias=EPS),
                         reads=[s_.b], writes=[s_.b])
                    T.op(ACT, lambda: nc.scalar.activation(out=s_.t[0:rows, 1:2], in_=s_.t[0:rows, 1:2], func=AF.Exp, scale=-0.5),
                         reads=[s_.b], writes=[s_.b])
                    T.op(DVE, lambda: nc.vector.scalar_tensor_tensor(out=n_.t[0:rows, :], in0=x_.t[0:rows, :], scalar=s_.t[0:rows, 1:2],
                                                                     in1=gbc.t[0:rows, :], op0=ALU.mult, op1=ALU.mult),
                         reads=[x_.b, s_.b, gbc.b], writes=[n_.b])
                    for half in range(2):
                        bk = banks[(2 * i + half) % 4]
                        bkb = bk.t[:].bitcast(BF16)
                        for kk in range(8):
                            k = half * 8 + kk
                            T.op(PE, lambda: nc.tensor.transpose(out=bkb[:, kk * 128:kk * 128 + rows], in_=n_.t[0:rows, k * 128:(k + 1) * 128],
                                                                 identity=identb[0:rows, 0:rows]), reads=[n_.b, cb.b], writes=[bk.b])
                        src = bkb.rearrange("p (k t) -> p k t", k=8)[:, :, 0:rows]
                        d_ap = dst.t[:, half * 8:half * 8 + 8, dtok0 + i * 128:dtok0 + i * 128 + rows]
                        if half == 0:
                            T.op(ACT, lambda: nc.scalar.copy(out=d_ap, in_=src), reads=[bk.b], writes=[dst.b])
                        else:
                            T.op(DVE, lambda: nc.vector.tensor_copy(out=d_ap, in_=src), reads=[bk.b], writes=[dst.b])


            L_hT = Phase(A, T)
            hT = L_hT.sb("hT", [128, KD, NTP], BF16)
            T.op(DVE, lambda: nc.vector.memset(hT.t[:, :, NT:NTP], 0.0), writes=[hT.b])
            L_cT = Phase(A, T)
            cT = L_cT.sb("cT", [128, 4, NT], BF16)
            P2 = Phase(A, T)
            norm_tiles(P2, x_own, TP, 0, hT, 0, "o")
            norm_tiles(P2, x_s, TS, 0, hT, TP, "s")
            P2.close()

            checkpoint("P_A")
            P3 = Phase(A, T)
            hTpl = P3.sb("hTpl", [128, KD, 128], BF16)
            PN = Phase(A, T)
            norm_tiles(PN, x_prev[TP - 128:TP, :], 128, 0, hTpl, 0, "l")
            PN.close()
            WS = WStream(P3, 2, KD)
            for c0 in (0, 512):
                WS.add(w_in, 0, KD, c0)
            uxp = P3.sb("uxp", [128, 4, 30 + TP], F32)
            uxs = P3.sb("uxs", [128, 4, NSAMP, 34], F32)
            tmp = [P3.sb(f"ctmp{i}", [128, 512], F32) for i in range(2)]
            sct = [P3.sb(f"sct{i}", [128, CW], F32) for i in range(2)]
            for i in range(4):
                s_ = sct[i % 2]
                T.dma(SP, lambda: nc.sync.dma_start(out=s_.t[0:120, :], in_=sconv[i * 120:(i + 1) * 120, :]), writes=[s_.b])
                bk = banks[i % 2]
                for m in range(4):
                    T.op(PE, lambda: nc.tensor.transpose(out=bk.t[:, m * 128:m * 128 + 120], in_=s_.t[0:120, m * 128:(m + 1) * 128],
                                                         identity=identf[0:120, 0:120]), reads=[s_.b, cst.b], writes=[bk.b])
                T.op(ACT, lambda: nc.scalar.copy(out=uxs.t[:, :, 4 * i:4 * i + 4, 0:30],
                                                 in_=bk.t[:].rearrange("p (m x) -> p m x", m=4)[:, :, 0:120].rearrange("p m (b r) -> p m b r", b=4)),
                     reads=[bk.b], writes=[uxs.b])

            CG = [(hTpl, 0, 128)] + [(hT, t0, n) for (t0, n) in GROUPS]

            def u_dst(m, gi):
                if gi == 0:
                    return uxp.t[:, m, 0:30], uxp.b
                t0, n = GROUPS[gi - 1]
                if gi < 3:
                    return uxp.t[:, m, 30 + t0:30 + t0 + n], uxp.b
                return uxs.t[:, m, :, 30:34], uxs.b

            def u_src(t_ap, gi, n):
                if gi == 0:
                    return t_ap[:, 98:128]
                if gi < 3:
                    return t_ap[:, 0:n]
                return t_ap[:, 0:n].rearrange("p (b t) -> p b t", t=4)

            wt = WS.get()
            it = 0
            for m in range(4):
                for gi, (src_h, t0, n) in enumerate(CG):
                    bk = banks[it % 4]
                    mm_fm(bk, n, 0, wt, KD, m, src_h, t0)
                    d_ap, d_b = u_dst(m, gi)
                    T.op(ACT, lambda: nc.scalar.copy(out=d_ap, in_=u_src(bk.t, gi, n)), reads=[bk.b], writes=[d_b])
                    it += 1
            wt = WS.get()
            for m in range(4):
                for gi, (src_h, t0, n) in enumerate(CG):
                    bk = banks[it % 4]
                    mm_fm(bk, n, 0, wt, KD, m, src_h, t0)
                    t_ = tmp[it % 2]
                    T.op(ACT, lambda: nc.scalar.activation(out=t_.t[:, 0:n], in_=bk.t[:, 0:n], func=AF.Sigmoid), reads=[bk.b], writes=[t_.b])
                    d_ap, d_b = u_dst(m, gi)
                    if gi == 0:
                        T.op(DVE, lambda: nc.vector.scalar_tensor_tensor(out=d_ap, in0=d_ap, scalar=pm, in1=u_src(t_.t, gi, n), op0=ALU.mult, op1=ALU.mult),
                             reads=[d_b, vecs.b, t_.b], writes=[d_b])
                    else:
                        T.op(DVE, lambda: nc.vector.tensor_tensor(out=d_ap, in0=d_ap, in1=u_src(t_.t, gi, n), op=ALU.mult), reads=[d_b, t_.b], writes=[d_b])
                    it += 1
            cst_o = P3.sb("cst_o", [128, CW], F32)
            bk = banks[4]
            for m in range(4):
                T.op(PE, lambda: nc.tensor.transpose(out=bk.t[0:30, m * 128:(m + 1) * 128], in_=uxp.t[:, m, TP:TP + 30], identity=identf),
                     reads=[uxp.b, cst.b], writes=[bk.b])
            T.op(ACT, lambda: nc.scalar.copy(out=cst_o.t[0:30, :], in_=bk.t[0:30, :]), reads=[bk.b], writes=[cst_o.b])
            T.dma(SP, lambda: nc.sync.dma_start(out=convp_o[:, :], in_=cst_o.t[0:30, :]), reads=[cst_o.b])
            cst_s = P3.sb("cst_s", [128, CW], F32)
            ucp = P3.sb("ucp", [128, 4, TS], F32)
            T.op(DVE, lambda: nc.vector.tensor_copy(out=ucp.t[:].rearrange("p m (b t) -> p m b t", t=4), in_=uxs.t[:, :, :, 30:34]),
                 reads=[uxs.b], writes=[ucp.b])
            bk = banks[5]
            for m in range(4):
                T.op(PE, lambda: nc.tensor.transpose(out=bk.t[0:TS, m * 128:(m + 1) * 128], in_=ucp.t[:, m, :], identity=identf),
                     reads=[ucp.b, cst.b], writes=[bk.b])
            T.op(ACT, lambda: nc.scalar.copy(out=cst_s.t[0:TS, :], in_=bk.t[0:TS, :]), reads=[bk.b], writes=[cst_s.b])
            for b in range(NSAMP):
                T.dma(SP, lambda: nc.sync.dma_start(out=convs_o[b, 26:30, :], in_=cst_s.t[4 * b:4 * b + 4, :]), reads=[cst_s.b])
            u_bufp, u_bufs = T.newbuf("u_scr_p"), T.newbuf("u_scr_s")
            T.dma(SP, lambda: nc.sync.dma_start(out=u_scr_p[:, :], in_=uxp.t[:].rearrange("p m t -> p (m t)")), reads=[uxp.b], writes=[u_bufp])
            T.dma(SP, lambda: nc.sync.dma_start(out=u_scr_s[:, :], in_=uxs.t[:].rearrange("p m b r -> p (m b r)")), reads=[uxs.b], writes=[u_bufs])
            P3.close()

            checkpoint("P_B")
            L_kv = Phase(A, T)
            kT = L_kv.sb("kT", [128, NH, 2 * TP], BF16)
            Vtok = L_kv.sb("Vtok", [128, 16, NH * HD], BF16)
            L_q = Phase(A, T)
            kTs = L_q.sb("kTs", [128, NH, TS], BF16)
            Vs = L_q.sb("Vs", [128, NH * HD], BF16)
            P1 = Phase(A, T)
            hTp = P1.sb("hTp", [128, KD, TP], BF16)
            PN = Phase(A, T)
            norm_tiles(PN, x_prev, TP, 0, hTp, 0, "p")
            PN.close()
            checkpoint("PC0")
            WS = WStream(P1, 2, KD)
            for c0 in (2048, 2560, 3072, 3584):
                WS.add(w_in, 0, KD, c0)
            wk_sq = [P1.sb(f"sq4{i}", [128, 512], F32) for i in range(2)]
            wk_rb = [P1.sb(f"rb4{i}", [128, 512], F32) for i in range(2)]
            kn = [P1.sb(f"kn{i}", [128, 512], F32) for i in range(2)]
            kst = [P1.sb(f"kst{i}", [128, 512], F32) for i in range(1)]
            vst = kst
            it = 0
            for blk in range(2):
                wt = WS.get()
                for hh in range(4):
                    h = blk * 4 + hh
                    for g in range(2):
                        bk, bk2 = banks[it % 2], banks[2 + it % 2]
                        mm_fm(bk, 512, 0, wt, KD, hh, hTp, g * 512)
                        headnorm(bk, 512, V_GKSB, wk_sq[it % 2], wk_rb[it % 2], bk2, kT.t[:, h, g * 512:(g + 1) * 512], kT.b)
                        it += 1
                    for gi, (t0, n) in enumerate(GROUPS):
                        bk, bk2, bk3 = banks[it % 2], banks[2 + it % 2], banks[4 + it % 2]
                        k_ = kn[it % 2]
                        mm_fm(bk, n, 0, wt, KD, hh, hT, t0)
                        headnorm(bk, n, V_GKSB, wk_sq[it % 2], wk_rb[it % 2], bk2, k_.t[:, 0:n], k_.b)
                        if gi < 2:
                            T.op(ACT, lambda: nc.scalar.copy(out=kT.t[:, h, TP + t0:TP + t0 + n], in_=k_.t[:, 0:n]), reads=[k_.b], writes=[kT.b])
                        else:
                            T.op(ACT, lambda: nc.scalar.copy(out=kTs.t[:, h, :], in_=k_.t[:, 0:n]), reads=[k_.b], writes=[kTs.b])
                        s_ = kst[0]
                        if gi < 2:
                            for j in range(4):
                                T.op(PE, lambda: nc.tensor.transpose(out=bk3.t[:, j * 128:(j + 1) * 128], in_=k_.t[:, j * 128:(j + 1) * 128], identity=identf),
                                     reads=[k_.b, cst.b], writes=[bk3.b])
                            T.op(DVE, lambda: nc.vector.tensor_copy(out=s_.t[:], in_=bk3.t[:]), reads=[bk3.b], writes=[s_.b])
                            T.dma(SP, lambda: nc.sync.dma_start(out=kp_o[t0:t0 + n, h * 128:(h + 1) * 128].rearrange("(j p) d -> p j d", p=128),
                                                                in_=s_.t[:].rearrange("p (j d) -> p j d", j=4)), reads=[s_.b])
                        else:
                            T.op(PE, lambda: nc.tensor.transpose(out=bk3.t[0:TS, 0:128], in_=k_.t[:, 0:TS], identity=identf),
                                 reads=[k_.b, cst.b], writes=[bk3.b])
                            T.op(DVE, lambda: nc.vector.tensor_copy(out=s_.t[0:TS, 0:128], in_=bk3.t[0:TS, 0:128]), reads=[bk3.b], writes=[s_.b])
                            T.dma(SP, lambda: nc.sync.dma_start(out=ks_o[:, h * 128:(h + 1) * 128], in_=s_.t[0:TS, 0:128]), reads=[s_.b])
                        it += 1
            checkpoint("PC1")
            for blk in range(2):
                wt = WS.get()
                for tile in range(8):
                    bk = banks[it % 4]
                    mm_tm(bk, 128, wt, KD, hTp, tile * 128)
                    o_ap = Vtok.t[:, tile, blk * 512:(blk + 1) * 512]
                    if it % 2 == 0:
                        T.op(ACT, lambda: nc.scalar.copy(out=o_ap, in_=bk.t[:, :]), reads=[bk.b], writes=[Vtok.b])
                    else:
                        T.op(DVE, lambda: nc.vector.tensor_copy(out=o_ap, in_=bk.t[:, :]), reads=[bk.b], writes=[Vtok.b])
                    it += 1
                if blk == 0:
                    checkpoint("PC2")
                for tile in range(9):
                    if blk == 0 and tile == 8:
                        checkpoint("PC3")
                    rows = 128 if tile < 8 else TS
                    bk = banks[it % 4]
                    s_ = vst[0]
                    mm_tm(bk, rows, wt, KD, hT, tile * 128)
                    T.op(ACT, lambda: nc.scalar.copy(out=s_.t[0:rows, :], in_=bk.t[0:rows, :]), reads=[bk.b], writes=[s_.b])
                    if tile < 8:
                        T.op(DVE, lambda: nc.vector.tensor_copy(out=Vtok.t[:, 8 + tile, blk * 512:(blk + 1) * 512], in_=s_.t[:, :]),
                             reads=[s_.b], writes=[Vtok.b])
                        T.dma(SP, lambda: nc.sync.dma_start(out=vp_o[tile * 128:(tile + 1) * 128, blk * 512:(blk + 1) * 512], in_=s_.t[:, :]), reads=[s_.b])
                    else:
                        T.op(DVE, lambda: nc.vector.tensor_copy(out=Vs.t[0:TS, blk * 512:(blk + 1) * 512], in_=s_.t[0:TS, :]),
                             reads=[s_.b], writes=[Vs.b])
                        T.dma(SP, lambda: nc.sync.dma_start(out=vs_o[:, blk * 512:(blk + 1) * 512], in_=s_.t[0:TS, :]), reads=[s_.b])
                    it += 1
            P1.close()

            checkpoint("P_C")
            L_qm = Phase(A, T)
            qT = L_q.sb("qT", [128, NH, NT], BF16)
            qmT = L_qm.sb("qmT", [128, MH, NT], BF16)
            P4 = Phase(A, T)
            WS = WStream(P4, 2, KD)
            for c0 in (1024, 1536, 4096):
                WS.add(w_in, 0, KD, c0)
            wk_sq = [P4.sb(f"sqd{i}", [128, 512], F32) for i in range(2)]
            wk_rb = [P4.sb(f"rbd{i}", [128, 512], F32) for i in range(2)]
            it = 0
            for blk in range(2):
                wt = WS.get()
                for hh in range(4):
                    h = blk * 4 + hh
                    for gi, (t0, n) in enumerate(GROUPS):
                        bk, bk2 = banks[it % 2], banks[2 + it % 2]
                        mm_fm(bk, n, 0, wt, KD, hh, hT, t0)
                        headnorm(bk, n, V_GQSB, wk_sq[it % 2], wk_rb[it % 2], bk2, qT.t[:, h, t0:t0 + n], qT.b)
                        it += 1
            wt = WS.get()
            for h in range(MH):
                for gi, (t0, n) in enumerate(GROUPS):
                    bk, bk2 = banks[it % 2], banks[2 + it % 2]
                    mm_fm(bk, n, 0, wt, KD, h, hT, t0)
                    headnorm(bk, n, V_GQM, wk_sq[it % 2], wk_rb[it % 2], bk2, qmT.t[:, h, t0:t0 + n], qmT.b)
                    it += 1
            P4.close()
            dump(T, "qT", qT, "p k t -> p (k t)")
            dump(T, "qmT", qmT, "p k t -> p (k t)")
            dump(T, "kT", kT, "p k t -> p (k t)")
            hT_buf = T.newbuf("hT_dram")
            T.dma(SP, lambda: nc.sync.dma_start(out=hT_scr[:, :], in_=hT.t[:].rearrange("p k t -> p (k t)")), reads=[hT.b], writes=[hT_buf])
            L_hT.close()

            L_o = Phase(A, T)
            oT_sb = L_o.sb("oT_sb", [128, NH, NT], BF16)
            oT_mem = L_o.sb("oT_mem", [128, MH, NT], BF16)

            checkpoint("P_D")
            LC = Phase(A, T)
            uxp = LC.sb("uxp2", [128, 4, 30 + TP], F32)
            uxs = LC.sb("uxs2", [128, 4, NSAMP, 34], F32)
            acc = LC.sb("cacc", [128, 4, NT], F32)
            T.dma(SP, lambda: nc.sync.dma_start(out=uxp.t[:].rearrange("p m t -> p (m t)"), in_=u_scr_p[:, :]), reads=[u_bufp], writes=[uxp.b])
            T.dma(SP, lambda: nc.sync.dma_start(out=uxs.t[:].rearrange("p m b r -> p (m b r)"), in_=u_scr_s[:, :]), reads=[u_bufs], writes=[uxs.b])
            wdw = lambda m, j: vecs.t[:, V_WDW + m * 31 + j:V_WDW + m * 31 + j + 1]
            conv_ops = []

            def conv_init(m, sample):
                o_ = acc.t[:, m, TP:NT].rearrange("p (b t) -> p b t", t=4) if sample else acc.t[:, m, 0:TP]
                i_, ib = (uxs.t[:, m, :, 0:4], uxs.b) if sample else (uxp.t[:, m, 0:TP], uxp.b)
                T.op(DVE, lambda: nc.vector.tensor_scalar(out=o_, in0=i_, scalar1=wdw(m, 0), scalar2=vecs.t[:, V_BDW + m:V_BDW + m + 1],
                                                          op0=ALU.mult, op1=ALU.add), reads=[ib, vecs.b], writes=[acc.b])

            def conv_tap(m, j, sample):
                o_ = acc.t[:, m, TP:NT].rearrange("p (b t) -> p b t", t=4) if sample else acc.t[:, m, 0:TP]
                i_, ib = (uxs.t[:, m, :, j:j + 4], uxs.b) if sample else (uxp.t[:, m, j:j + TP], uxp.b)
                T.op(DVE, lambda: nc.vector.scalar_tensor_tensor(out=o_, in0=i_, scalar=wdw(m, j), in1=o_, op0=ALU.mult, op1=ALU.add),
                     reads=[ib, vecs.b, acc.b], writes=[acc.b])

            for m in range(4):
                for sample in (False, True):
                    conv_ops.append((lambda m=m, sample=sample: conv_init(m, sample)))
                for j in range(1, 31):
                    for sample in (False, True):
                        conv_ops.append((lambda m=m, j=j, sample=sample: conv_tap(m, j, sample)))
            P5 = Phase(A, T)
            NE = 4
            maskd = P5.sb("maskd", [128, 4 * 512], F32)
            T.dma(SP, lambda: nc.sync.dma_start(out=maskd.t[:], in_=cst_d[:, C_MASKD:C_MASKD + 2048]), writes=[maskd.b])
            Eb = [P5.sb(f"E{i}", [128, 512], F32) for i in range(NE)]
            SPb = [P5.sb(f"SP{i}", [128, 512], F32) for i in range(NE)]
            Xb = [P5.sb(f"X{i}", [128, 512], F32) for i in range(2)]
            ab = [P5.sb(f"a{i}", [128, 512], BF16) for i in range(2)]
            zbanks = banks[0:3]
            tiles = []
            for hp in range(0, NH, 2):
                for qg in range(2):
                    lists = []
                    for s in range(2):
                        kbs = [8 + i for i in range(4 * qg + 3, -1, -1)] + list(range(7, -1, -1))
                        lists.append([(hp + s, qg, kb, s) for kb in kbs])
                    for a_, b_ in zip(*lists):
                        tiles.append(a_)
                        tiles.append(b_)
            state = {}
            for ti, (h, qg, kb, s) in enumerate(tiles):
                first = kb == 8 + 4 * qg + 3
                state[ti] = dict(h=h, qg=qg, kb=kb, s=s, first=first, last=(kb == 0), z=zbanks[ti % 3], E=Eb[ti % NE], SP=SPb[ti % NE],
                                 X=Xb[ti % 2], a=ab[ti % 2], C=banks[3 + s], O=banks[5 + s], prevSP=(SPb[(ti - 2) % NE] if not first else None))

            def stA(d):
                T.op(PE, lambda: nc.tensor.matmul(d["z"].t[:], lhsT=kT.t[:, d["h"], d["kb"] * 128:(d["kb"] + 1) * 128],
                                                  rhs=qT.t[:, d["h"], d["qg"] * 512:(d["qg"] + 1) * 512], start=True, stop=True),
                     reads=[kT.b, qT.b], writes=[d["z"].b])

            def stB(d):
                h, kb, qg = d["h"], d["kb"], d["qg"]
                if kb >= 8:
                    T.op(ACT, lambda: nc.scalar.activation(out=d["E"].t[:], in_=d["z"].t[:], func=AF.Exp, scale=SCALE, bias=bsb.t[:, h:h + 1]),
                         reads=[d["z"].b, bsb.b], writes=[d["E"].b])
                    r = kb - 8 - 4 * qg
                    if r >= 0:
                        T.op(DVE, lambda: nc.vector.tensor_tensor(out=d["E"].t[:], in0=d["E"].t[:], in1=maskd.t[:, r * 512:(r + 1) * 512],
                                                                  op=ALU.mult), reads=[d["E"].b, maskd.b], writes=[d["E"].b])
                else:
                    T.op(ACT, lambda: nc.scalar.activation(out=d["E"].t[:], in_=d["z"].t[:], func=AF.Exp, scale=sm.t[:, 0:1], bias=sm.t[:, 2 + h:3 + h]),
                         reads=[d["z"].b, sm.b], writes=[d["E"].b])
                T.op(ACT, lambda: nc.scalar.activation(out=d["SP"].t[:], in_=d["E"].t[:], func=AF.Ln, bias=1.0), reads=[d["E"].b], writes=[d["SP"].b])
                if not d["first"]:
                    T.op(PE, lambda: nc.tensor.matmul(d["C"].t[:], lhsT=lstr, rhs=d["prevSP"].t[:], start=False, stop=False, skip_group_check=True),
                         reads=[cst.b, d["prevSP"].b], writes=[d["C"].b])
                T.op(PE, lambda: nc.tensor.matmul(d["C"].t[:], lhsT=tinc, rhs=d["SP"].t[:], start=d["first"], stop=True, skip_group_check=True),
                     reads=[cst.b, d["SP"].b], writes=[d["C"].b])

            def stC(d):
                h, kb, qg = d["h"], d["kb"], d["qg"]
                T.op(ACT, lambda: nc.scalar.activation(out=d["X"].t[:], in_=d["C"].t[:], func=AF.Exp, scale=-1.0), reads=[d["C"].b], writes=[d["X"].b])
                T.op(DVE, lambda: nc.vector.tensor_tensor(out=d["a"].t[:], in0=d["E"].t[:], in1=d["X"].t[:], op=ALU.mult),
                     reads=[d["E"].b, d["X"].b], writes=[d["a"].b])
                T.op(PE, lambda: nc.tensor.matmul(d["O"].t[:], lhsT=Vtok.t[:, kb, h * 128:(h + 1) * 128], rhs=d["a"].t[:],
                                                  start=d["first"], stop=d["last"], skip_group_check=True),
                     reads=[Vtok.b, d["a"].b], writes=[d["O"].b])
                if d["last"]:
                    T.op(ACT, lambda: nc.scalar.copy(out=oT_sb.t[:, h, qg * 512:(qg + 1) * 512], in_=d["O"].t[:]), reads=[d["O"].b], writes=[oT_sb.b])

            n_t = len(tiles)
            ci = 0
            for step in range(n_t + 2):
                if step < n_t:
                    stA(state[step])
                if 0 <= step - 1 < n_t:
                    stB(state[step - 1])
                if 0 <= step - 2 < n_t:
                    stC(state[step - 2])
                left = n_t + 2 - step
                for _ in range((len(conv_ops) - ci + left - 1) // left):
                    conv_ops[ci]()
                    ci += 1
            while ci < len(conv_ops):
                conv_ops[ci]()
                ci += 1
            P5.close()
            L_kv.close()
            rbc = LC.sb("rbc", [128, 512], F32)
            tmp = [LC.sb(f"lntmp{i}", [128, 512], F32) for i in range(2)]
            for gi, (t0, n) in enumerate(GROUPS):
                b1, b2 = banks[6], banks[7]
                for m in range(4):
                    T.op(PE, lambda: nc.tensor.matmul(b1.t[:, 0:n], lhsT=onesf, rhs=acc.t[:, m, t0:t0 + n], start=(m == 0), stop=(m == 3)),
                         reads=[cst.b, acc.b], writes=[b1.b])
                for m in range(4):
                    a_ = acc.t[:, m, t0:t0 + n]
                    T.op(DVE, lambda: nc.vector.scalar_tensor_tensor(out=a_, in0=b1.t[:, 0:n], scalar=-1.0 / CW, in1=a_, op0=ALU.mult, op1=ALU.add),
                         reads=[b1.b, acc.b], writes=[acc.b])
                    t_ = tmp[m % 2]
                    T.op(ACT, lambda: nc.scalar.activation(out=t_.t[:, 0:n], in_=a_, func=AF.Square), reads=[acc.b], writes=[t_.b])
                    T.op(PE, lambda: nc.tensor.matmul(b2.t[:, 0:n], lhsT=onesf, rhs=t_.t[:, 0:n], start=(m == 0), stop=(m == 3)),
                         reads=[cst.b, t_.b], writes=[b2.b])
                rstd_from(b2, n, 1.0 / CW, rbc)
                for m in range(4):
                    a_ = acc.t[:, m, t0:t0 + n]
                    T.op(DVE, lambda: nc.vector.tensor_tensor(out=a_, in0=a_, in1=rbc.t[:, 0:n], op=ALU.mult), reads=[acc.b, rbc.b], writes=[acc.b])
                    T.op(ACT, lambda: nc.scalar.activation(out=cT.t[:, m, t0:t0 + n], in_=a_, func=AF.Silu,
                                                           scale=vecs.t[:, V_GCLN + m:V_GCLN + m + 1], bias=vecs.t[:, V_BCLN + m:V_BCLN + m + 1]),
                         reads=[acc.b, vecs.b], writes=[cT.b])
            LC.close()
            dump(T, "cT", cT, "p k t -> p (k t)")

            checkpoint("A1")
            P6 = Phase(A, T)
            Vb = P6.sb("Vb", [128, NPG, NH * HD], BF16)
            KT = P6.sb("KT", [128, NPG, NH, 128], BF16)
            biasN = P6.sb("biasN", [128, 512], F32)
            T.op(DVE, lambda: nc.vector.tensor_copy(out=biasN.t[:].rearrange("p (h x) -> p h x", h=NH),
                                                    in_=bsb.t[:].unsqueeze(2).to_broadcast([128, NH, 64])), reads=[bsb.b], writes=[biasN.b])
            biasP = P6.sb("biasP", [128, 512], F32)
            T.op(DVE, lambda: nc.vector.tensor_copy(out=biasP.t[:].rearrange("p (g h t) -> p g h t", g=NPG, h=NH),
                                                    in_=bsb.t[:].unsqueeze(1).unsqueeze(3).to_broadcast([128, NPG, NH, 4])),
                 reads=[bsb.b], writes=[biasP.b])
            P6n = Phase(A, T)
            En = P6n.sb("En", [128, 512], F32)
            SPn = P6n.sb("SPn", [128, 512], F32)
            Xn = P6n.sb("Xn", [128, 512], F32)
            an = P6n.sb("an", [128, 512], BF16)
            m64t = P6n.sb("m64", [128, 512], F32)
            T.dma(SP, lambda: nc.sync.dma_start(out=m64t.t[:], in_=cst_d[:, C_M64:C_M64 + 512]), writes=[m64t.b])
            m64 = m64t.t[0:TS, :]
            XN = P6.sb("XN", [128, 512], F32)
            Onew = P6.sb("Onew", [128, 512], F32)
            bz = banks[0]
            for h in range(NH):
                T.op(PE, lambda: nc.tensor.matmul(bz.t[0:TS, h * 64:(h + 1) * 64], lhsT=kTs.t[:, h, :], rhs=qT.t[:, h, TP:NT], start=True, stop=True),
                     reads=[kTs.b, qT.b], writes=[bz.b])
            T.op(DVE, lambda: nc.vector.scalar_tensor_tensor(out=En.t[0:TS, :], in0=bz.t[0:TS, :], scalar=SCALE, in1=biasN.t[0:TS, :],
                                                             op0=ALU.mult, op1=ALU.add), reads=[bz.b, biasN.b], writes=[En.b])
            T.op(ACT, lambda: nc.scalar.activation(out=En.t[0:TS, :], in_=En.t[0:TS, :], func=AF.Exp), reads=[En.b], writes=[En.b])
            T.op(DVE, lambda: nc.vector.tensor_tensor(out=En.t[0:TS, :], in0=En.t[0:TS, :], in1=m64, op=ALU.mult), reads=[En.b, m64t.b], writes=[En.b])
            T.op(ACT, lambda: nc.scalar.activation(out=SPn.t[0:TS, :], in_=En.t[0:TS, :], func=AF.Ln, bias=1.0), reads=[En.b], writes=[SPn.b])
            b1, b2, b3 = banks[1], banks[2], banks[3]
            T.op(PE, lambda: nc.tensor.matmul(b1.t[:, :], lhsT=onesf[0:TS, :], rhs=SPn.t[0:TS, :], start=True, stop=True),
                 reads=[cst.b, SPn.b], writes=[b1.b])
            T.op(ACT, lambda: nc.scalar.activation(out=XN.t[:], in_=b1.t[:], func=AF.Exp, scale=-1.0), reads=[b1.b], writes=[XN.b])
            T.op(PE, lambda: nc.tensor.matmul(b2.t[0:TS, :], lhsT=tinc[0:TS, 0:TS], rhs=SPn.t[0:TS, :], start=True, stop=True),
                 reads=[cst.b, SPn.b], writes=[b2.b])
            T.op(ACT, lambda: nc.scalar.activation(out=Xn.t[0:TS, :], in_=b2.t[0:TS, :], func=AF.Exp, scale=-1.0), reads=[b2.b], writes=[Xn.b])
            T.op(DVE, lambda: nc.vector.tensor_tensor(out=an.t[0:TS, :], in0=En.t[0:TS, :], in1=Xn.t[0:TS, :], op=ALU.mult),
                 reads=[En.b, Xn.b], writes=[an.b])
            for h in range(NH):
                T.op(PE, lambda: nc.tensor.matmul(b3.t[:, h * 64:(h + 1) * 64], lhsT=Vs.t[0:TS, h * 128:(h + 1) * 128], rhs=an.t[0:TS, h * 64:(h + 1) * 64],
                                                  start=True, stop=True), reads=[Vs.b, an.b], writes=[b3.b])
            T.op(DVE, lambda: nc.vector.tensor_copy(out=Onew.t[:], in_=b3.t[:]), reads=[b3.b], writes=[Onew.b])
            P6n.close()
            NKS, NVS = 6, 6
            Kst = [P6.sb(f"Kst{i}", [128, NH * HD], F32) for i in range(NKS)]
            Vst = [P6.sb(f"Vst{i}", [128, NH * HD], F32) for i in range(NVS)]
            zb_ = P6.sb("zb", [128, 512], F32)
            Es = P6.sb("Es", [128, 512], F32)
            SPs = P6.sb("SPs", [128, 512], F32)
            Xs = P6.sb("Xs", [128, 512], F32)
            a1 = P6.sb("a1", [128, 512], F32)
            as_ = P6.sb("as", [128, 512], BF16)
            kcnt = vcnt = 0
            ev = 0
            KTb = [P6.buf(f"KTpg{i}") for i in range(NPG)]

            def z_page(b, pg):
                bz = banks[0]
                for h in range(NH):
                    c0 = (pg * NH + h) * 4
                    T.op(PE, lambda: nc.tensor.matmul(bz.t[:, c0:c0 + 4], lhsT=KT.t[:, pg, h, :], rhs=qT.t[:, h, TP + 4 * b:TP + 4 * b + 4],
                                                      start=True, stop=True), reads=[KTb[pg], qT.b], writes=[bz.b])

            for b in range(NSAMP):
                for pg in range(NPG):
                    col = b * NPG + pg
                    ks_ = Kst[kcnt % NKS]
                    kcnt += 1
                    T.dma(POOL, lambda: nc.gpsimd.indirect_dma_start(out=ks_.t[:, :], out_offset=None, in_=poolk[:, :],
                                                                     in_offset=bass.IndirectOffsetOnAxis(ap=idx.t[:, col:col + 1], axis=0)),
                          reads=[idx.b], writes=[ks_.b])
                    vs_ = Vst[vcnt % NVS]
                    vcnt += 1
                    T.dma(POOL, lambda: nc.gpsimd.indirect_dma_start(out=vs_.t[:, :], out_offset=None, in_=poolv[:, :],
                                                                     in_offset=bass.IndirectOffsetOnAxis(ap=idx.t[:, col:col + 1], axis=0)),
                          reads=[idx.b], writes=[vs_.b])
                    for hb in range(2):
                        bk = banks[4 + ev % 4]
                        for hh in range(4):
                            h = hb * 4 + hh
                            T.op(PE, lambda: nc.tensor.transpose(out=bk.t[:, hh * 128:(hh + 1) * 128], in_=ks_.t[:, h * 128:(h + 1) * 128], identity=identf),
                                 reads=[ks_.b, cst.b], writes=[bk.b])
                        o_ap = KT.t[:, pg, hb * 4:hb * 4 + 4, :]
                        i_ap = bk.t[:, :].rearrange("p (h s) -> p h s", h=4)
                        if ev % 2 == 0:
                            T.op(ACT, lambda: nc.scalar.copy(out=o_ap, in_=i_ap), reads=[bk.b], writes=[KTb[pg]])
                        else:
                            T.op(DVE, lambda: nc.vector.tensor_copy(out=o_ap, in_=i_ap), reads=[bk.b], writes=[KTb[pg]])
                        ev += 1
                    if pg >= 1:
                        z_page(b, pg - 1)
                    if pg % 2 == 0:
                        T.op(DVE, lambda: nc.vector.tensor_copy(out=Vb.t[:, pg, :], in_=vs_.t[:, :]), reads=[vs_.b], writes=[Vb.b])
                    else:
                        T.op(ACT, lambda: nc.scalar.copy(out=Vb.t[:, pg, :], in_=vs_.t[:, :]), reads=[vs_.b], writes=[Vb.b])
                bz, bc, bo = banks[0], banks[1], banks[2 + b % 2]
                z_page(b, NPG - 1)
                T.op(DVE, lambda: nc.vector.scalar_tensor_tensor(out=zb_.t[:], in0=bz.t[:], scalar=SCALE, in1=biasP.t[:], op0=ALU.mult, op1=ALU.add),
                     reads=[bz.b, biasP.b], writes=[zb_.b])
                T.op(ACT, lambda: nc.scalar.activation(out=Es.t[:], in_=zb_.t[:], func=AF.Exp), reads=[zb_.b], writes=[Es.b])
                T.op(ACT, lambda: nc.scalar.activation(out=SPs.t[:], in_=Es.t[:], func=AF.Ln, bias=1.0), reads=[Es.b], writes=[SPs.b])
                T.op(PE, lambda: nc.tensor.matmul(bc.t[:], lhsT=tinc, rhs=SPs.t[:], start=True, stop=False, skip_group_check=True),
                     reads=[cst.b, SPs.b], writes=[bc.b])
                for k in range(1, NPG):
                    w_ = 32 * (NPG - k)
                    T.op(PE, lambda: nc.tensor.matmul(bc.t[:, 0:w_], lhsT=onesf, rhs=SPs.t[:, 32 * k:512], start=False, stop=(k == NPG - 1),
                                                      skip_group_check=True), reads=[cst.b, SPs.b], writes=[bc.b])
                T.op(ACT, lambda: nc.scalar.activation(out=Xs.t[:], in_=bc.t[:], func=AF.Exp, scale=-1.0), reads=[bc.b], writes=[Xs.b])
                T.op(DVE, lambda: nc.vector.tensor_tensor(out=a1.t[:], in0=Es.t[:], in1=Xs.t[:], op=ALU.mult), reads=[Es.b, Xs.b], writes=[a1.b])
                xn_b = XN.t[:].rearrange("p (h b t) -> p h b t", h=NH, b=NSAMP)[:, :, b, :].unsqueeze(1).to_broadcast([128, NPG, NH, 4])
                T.op(DVE, lambda: nc.vector.tensor_tensor(out=as_.t[:].rearrange("p (g h t) -> p g h t", g=NPG, h=NH),
                                                          in0=a1.t[:].rearrange("p (g h t) -> p g h t", g=NPG, h=NH), in1=xn_b, op=ALU.mult),
                     reads=[a1.b, XN.b], writes=[as_.b])
                for h in range(NH):
                    for pg in range(NPG):
                        c0 = (pg * NH + h) * 4
                        T.op(PE, lambda: nc.tensor.matmul(bo.t[:, h * 4:h * 4 + 4], lhsT=Vb.t[:, pg, h * 128:(h + 1) * 128], rhs=as_.t[:, c0:c0 + 4],
                                                          start=(pg == 0), stop=(pg == NPG - 1), skip_group_check=True),
                             reads=[Vb.b, as_.b], writes=[bo.b])
                T.op(DVE, lambda: nc.vector.tensor_tensor(out=oT_sb.t[:, :, TP + 4 * b:TP + 4 * b + 4],
                                                          in0=bo.t[:, 0:32].rearrange("p (h t) -> p h t", h=NH),
                                                          in1=Onew.t[:].rearrange("p (h b t) -> p h b t", h=NH, b=NSAMP)[:, :, b, :], op=ALU.add),
                     reads=[bo.b, Onew.b], writes=[oT_sb.b])
            P6.close()
            L_q.close()

            checkpoint("A2")
            P7 = Phase(A, T)
            hTm = P7.sb("hTm", [128, KD, ML], BF16)
            PN = Phase(A, T)
            norm_tiles(PN, mem, ML, 2, hTm, 0, "m")
            PN.close()
            WS = WStream(P7, 2, KD)
            WS.add(w_mkv, 0, KD, 0)
            WS.add(w_mkv, 0, KD, 512)
            mkT = P7.sb("mkT", [128, MH, ML], BF16)
            mvt = P7.sb("mvt", [128, 2, MH * HD], BF16)
            wk_sq = [P7.sb(f"sq7{i}", [128, 512], F32) for i in range(2)]
            wk_rb = [P7.sb(f"rb7{i}", [128, 512], F32) for i in range(2)]
            kn = [P7.sb(f"kn7{i}", [128, ML], F32) for i in range(2)]
            mst = [P7.sb(f"mst{i}", [128, 512], F32) for i in range(2)]
            wt = WS.get()
            it = 0
            for h in range(MH):
                bk, bk2, bk3 = banks[it % 2], banks[2 + it % 2], banks[4 + it % 2]
                k_ = kn[it % 2]
                mm_fm(bk, ML, 0, wt, KD, h, hTm, 0)
                headnorm(bk, ML, V_GKM, wk_sq[it % 2], wk_rb[it % 2], bk2, k_.t[:, :], k_.b)
                T.op(ACT, lambda: nc.scalar.copy(out=mkT.t[:, h, :], in_=k_.t[:, :]), reads=[k_.b], writes=[mkT.b])
                s_ = mst[it % 2]
                for j in range(2):
                    T.op(PE, lambda: nc.tensor.transpose(out=bk3.t[:, j * 128:(j + 1) * 128], in_=k_.t[:, j * 128:(j + 1) * 128], identity=identf),
                         reads=[k_.b, cst.b], writes=[bk3.b])
                T.op(DVE, lambda: nc.vector.tensor_copy(out=s_.t[:, 0:256], in_=bk3.t[:, 0:256]), reads=[bk3.b], writes=[s_.b])
                T.dma(SP, lambda: nc.sync.dma_start(out=memk_o[:, h * 128:(h + 1) * 128].rearrange("(j p) d -> p j d", p=128),
                                                    in_=s_.t[:, 0:256].rearrange("p (j d) -> p j d", j=2)), reads=[s_.b])
                it += 1
            wt = WS.get()
            for tile in range(2):
                bk = banks[it % 2]
                s_ = mst[it % 2]
                mm_tm(bk, 128, wt, KD, hTm, tile * 128)
                T.op(ACT, lambda: nc.scalar.copy(out=s_.t[:, :], in_=bk.t[:, :]), reads=[bk.b], writes=[s_.b])
                T.op(DVE, lambda: nc.vector.tensor_copy(out=mvt.t[:, tile, :], in_=bk.t[:, :]), reads=[bk.b], writes=[mvt.b])
                T.dma(SP, lambda: nc.sync.dma_start(out=memv_o[tile * 128:(tile + 1) * 128, :], in_=s_.t[:, :]), reads=[s_.b])
                it += 1
            Pb = [P7.sb(f"Pm{i}", [128, 512], BF16) for i in range(4)]
            rd = P7.sb("rden", [128, 512], F32)
            it = 0
            for h in range(MH):
                for qg in range(2):
                    bd, bo = banks[4 + it % 2], banks[6 + it % 2]
                    for mb in range(2):
                        bs = banks[it % 4]
                        p_ = Pb[it % 4]
                        T.op(PE, lambda: nc.tensor.matmul(bs.t[:], lhsT=mkT.t[:, h, mb * 128:(mb + 1) * 128], rhs=qmT.t[:, h, qg * 512:(qg + 1) * 512],
                                                          start=True, stop=True), reads=[mkT.b, qmT.b], writes=[bs.b])
                        T.op(ACT, lambda: nc.scalar.activation(out=p_.t[:], in_=bs.t[:], func=AF.Exp, scale=SCALE), reads=[bs.b], writes=[p_.b])
                        T.op(PE, lambda: nc.tensor.matmul(bd.t[:], lhsT=onesb, rhs=p_.t[:], start=(mb == 0), stop=(mb == 1)),
                             reads=[cb.b, p_.b], writes=[bd.b])
                        T.op(PE, lambda: nc.tensor.matmul(bo.t[:], lhsT=mvt.t[:, mb, h * 128:(h + 1) * 128], rhs=p_.t[:], start=(mb == 0), stop=(mb == 1)),
                             reads=[mvt.b, p_.b], writes=[bo.b])
                        it += 1
                    T.op(DVE, lambda: nc.vector.reciprocal(out=rd.t[:], in_=bd.t[:]), reads=[bd.b], writes=[rd.b])
                    T.op(DVE, lambda: nc.vector.tensor_tensor(out=oT_mem.t[:, h, qg * 512:(qg + 1) * 512], in0=bo.t[:], in1=rd.t[:], op=ALU.mult),
                         reads=[bo.b, rd.b], writes=[oT_mem.b])
            HS = NSAMP // 2
            cmkb = P7.sb("cmkb", [128, HS * 2, MH * HD], BF16)
            cmvb = P7.sb("cmvb", [128, HS * 2, MH * HD], BF16)
            mkTs = P7.sb("mkTs", [128, HS * 2, MH, 128], BF16)
            Ps = P7.sb("Ps", [128, 256], BF16)
            rds = P7.sb("rds", [128, 128], F32)
            ev = 0
            for hf in range(2):
                r0 = hf * HS * ML
                T.dma(POOL, lambda: nc.gpsimd.dma_start(out=cmkb.t[:], in_=cmk[r0:r0 + HS * ML, :].rearrange("(x p) c -> p x c", p=128)), writes=[cmkb.b])
                T.dma(POOL, lambda: nc.gpsimd.dma_start(out=cmvb.t[:], in_=cmv[r0:r0 + HS * ML, :].rearrange("(x p) c -> p x c", p=128)), writes=[cmvb.b])
                for x in range(HS * 2):
                    bk = banks[ev % 2]
                    bkb = bk.t[:].bitcast(BF16)
                    for h in range(MH):
                        T.op(PE, lambda: nc.tensor.transpose(out=bkb[:, h * 128:(h + 1) * 128], in_=cmkb.t[:, x, h * 128:(h + 1) * 128], identity=identb),
                             reads=[cmkb.b, cb.b], writes=[bk.b])
                    i_ap = bkb[:, 0:512].rearrange("p (h s) -> p h s", h=4)
                    if ev % 2 == 0:
                        T.op(ACT, lambda: nc.scalar.copy(out=mkTs.t[:, x, :, :], in_=i_ap), reads=[bk.b], writes=[mkTs.b])
                    else:
                        T.op(DVE, lambda: nc.vector.tensor_copy(out=mkTs.t[:, x, :, :], in_=i_ap), reads=[bk.b], writes=[mkTs.b])
                    ev += 1
                bs, bd, bo = banks[2], banks[3], banks[4]
                for bb in range(HS):
                    b = hf * HS + bb
                    for mb in range(2):
                        for h in range(MH):
                            c0 = ((bb * 2 + mb) * MH + h) * 4
                            T.op(PE, lambda: nc.tensor.matmul(bs.t[:, c0:c0 + 4], lhsT=mkTs.t[:, bb * 2 + mb, h, :], rhs=qmT.t[:, h, TP + 4 * b:TP + 4 * b + 4],
                                                              start=True, stop=True), reads=[mkTs.b, qmT.b], writes=[bs.b])
                T.op(ACT, lambda: nc.scalar.activation(out=Ps.t[:], in_=bs.t[:, 0:256], func=AF.Exp, scale=SCALE), reads=[bs.b], writes=[Ps.b])
                p4 = Ps.t[:].rearrange("p (b m x) -> p b m x", b=HS, m=2)
                for mb in range(2):
                    T.op(PE, lambda: nc.tensor.matmul(bd.t[:, 0:128].rearrange("p (b x) -> p b x", b=HS), lhsT=onesb, rhs=p4[:, :, mb, :],
                                                      start=(mb == 0), stop=(mb == 1)), reads=[cb.b, Ps.b], writes=[bd.b])
                for bb in range(HS):
                    for h in range(MH):
                        for mb in range(2):
                            c0 = ((bb * 2 + mb) * MH + h) * 4
                            o0 = (bb * MH + h) * 4
                            T.op(PE, lambda: nc.tensor.matmul(bo.t[:, o0:o0 + 4], lhsT=cmvb.t[:, bb * 2 + mb, h * 128:(h + 1) * 128], rhs=Ps.t[:, c0:c0 + 4],
                                                              start=(mb == 0), stop=(mb == 1), skip_group_check=True),
                                 reads=[cmvb.b, Ps.b], writes=[bo.b])
                T.op(DVE, lambda: nc.vector.reciprocal(out=rds.t[:], in_=bd.t[:, 0:128]), reads=[bd.b], writes=[rds.b])
                T.op(DVE, lambda: nc.vector.tensor_tensor(
                    out=oT_mem.t[:, :, TP + hf * HS * 4:TP + (hf + 1) * HS * 4].rearrange("p h (b t) -> p h b t", t=4),
                    in0=bo.t[:, 0:128].rearrange("p (b h t) -> p h b t", b=HS, h=MH),
                    in1=rds.t[:].rearrange("p (b h t) -> p h b t", b=HS, h=MH), op=ALU.mult), reads=[bo.b, rds.b], writes=[oT_mem.b])
            P7.close()
            L_qm.close()
            dump(T, "oT_sb", oT_sb, "p k t -> p (k t)")
            dump(T, "oT_mem", oT_mem, "p k t -> p (k t)")

            checkpoint("A3")
            L_hT = Phase(A, T)
            hT = L_hT.sb("hT", [128, KD, NTP], BF16)
            T.dma(SP, lambda: nc.sync.dma_start(out=hT.t[:].rearrange("p k t -> p (k t)"), in_=hT_scr[:, :]), reads=[hT_buf], writes=[hT.b])
            L_mrg = Phase(A, T)
            mrg = L_mrg.sb("merged", [128, KD, NTP], BF16)
            T.op(DVE, lambda: nc.vector.memset(mrg.t[:, :, NT:NTP], 0.0), writes=[mrg.b])
            P8 = Phase(A, T)
            WS = WStream(P8, 3, KD)
            for cg in range(4):
                WS.add(w_co, 0, 4, cg * 512)
                WS.add(w_in, 0, KD, 4608 + cg * 512)
                WS.add(w_so, 0, 8, cg * 512)
                WS.add(w_in, 0, KD, 4608 + 2048 + cg * 512)
                WS.add(w_mo, 0, 4, cg * 512)
                WS.add(w_in, 0, KD, 4608 + 4096 + cg * 512)
            sgb = [P8.sb(f"sg{i}", [128, 512], F32) for i in range(2)]
            macc = [[P8.sb(f"macc{mm}_{gi}", [128, 512], F32) for gi in range(3)] for mm in range(4)]
            it = 0
            for cg in range(4):
                for br, (src, kc, boff) in enumerate(((cT, 4, 0), (oT_sb, 8, 16), (oT_mem, 4, 32))):
                    wy = WS.get()
                    wg = WS.get(hold=1)
                    for mm in range(4):
                        m = cg * 4 + mm
                        for gi, (t0, n) in enumerate(GROUPS):
                            by, bg = banks[(2 * it) % 8], banks[(2 * it + 1) % 8]
                            s_ = sgb[it % 2]
                            mm_fm(by, n, 0, wy, kc, mm, src, t0)
                            mm_fm(bg, n, 0, wg, KD, mm, hT, t0)
                            T.op(ACT, lambda: nc.scalar.activation(out=s_.t[:, 0:n], in_=bg.t[:, 0:n], func=AF.Sigmoid,
                                                                   bias=vecs.t[:, V_BGATE + boff + m:V_BGATE + boff + m + 1]),
                                 reads=[bg.b, vecs.b], writes=[s_.b])
                            a_ = macc[mm][gi]
                            if br == 0:
                                T.op(DVE, lambda: nc.vector.tensor_tensor(out=a_.t[:, 0:n], in0=by.t[:, 0:n], in1=s_.t[:, 0:n], op=ALU.mult),
                                     reads=[by.b, s_.b], writes=[a_.b])
                            else:
                                T.op(DVE, lambda: nc.vector.tensor_tensor(out=s_.t[:, 0:n], in0=by.t[:, 0:n], in1=s_.t[:, 0:n], op=ALU.mult),
                                     reads=[by.b, s_.b], writes=[s_.b])
                                if br == 1:
                                    T.op(DVE, lambda: nc.vector.tensor_tensor(out=a_.t[:, 0:n], in0=a_.t[:, 0:n], in1=s_.t[:, 0:n], op=ALU.add),
                                         reads=[a_.b, s_.b], writes=[a_.b])
                                else:
                                    T.op(DVE, lambda: nc.vector.tensor_tensor(out=mrg.t[:, m, t0:t0 + n], in0=a_.t[:, 0:n], in1=s_.t[:, 0:n], op=ALU.add),
                                         reads=[a_.b, s_.b], writes=[mrg.b])
                            it += 1
            P8.close()
            dump(T, "mrg", mrg, "p k t -> p (k t)")
            L_hT.close()
            L_cT.close()
            L_o.close()

            checkpoint("M")
            L_x1 = Phase(A, T)
            x1 = L_x1.sb("x1", [128, 9, D], F32)
            x1b = [L_x1.buf(f"x1_{i}") for i in range(9)]
            P9 = Phase(A, T)
            WS = WStream(P9, 2, KD)
            for cg in range(4):
                WS.add(w_o, 0, KD, cg * 512)
            xr = [P9.sb(f"xr{i}", [128, 512], F32) for i in range(3)]
            it = 0
            for cg in range(4):
                wt = WS.get()
                for tile in range(9):
                    rows = 128 if tile < 8 else TS
                    bk = banks[it % 4]
                    x_ = xr[it % 3]
                    xsrc = x_own[tile * 128:(tile + 1) * 128, cg * 512:(cg + 1) * 512] if tile < 8 else x_s[:, cg * 512:(cg + 1) * 512]
                    T.dma(SP, lambda: nc.sync.dma_start(out=x_.t[0:rows, :], in_=xsrc), writes=[x_.b])
                    mm_tm(bk, rows, wt, KD, mrg, tile * 128)
                    T.op(DVE, lambda: nc.vector.tensor_tensor(out=x1.t[0:rows, tile, cg * 512:(cg + 1) * 512], in0=bk.t[0:rows, :], in1=x_.t[0:rows, :], op=ALU.add),
                         reads=[bk.b, x_.b], writes=[x1b[tile]])
                    it += 1
            P9.close()
            dump(T, "x1", x1, "p k t -> p (k t)", F32, reads=x1b)
            L_mrg.close()

            checkpoint("O")
            P10 = Phase(A, T)
            h2T = P10.sb("h2T", [128, KD, NT], BF16)
            PN = Phase(A, T)
            gbc = PN.sb("gbc2", [128, D], F32)
            T.dma(SP, lambda: nc.sync.dma_start(out=gbc.t[:], in_=grow[1:2, :].partition_broadcast(128)), writes=[gbc.b])
            xn2 = [PN.sb(f"xn2{i}", [128, D], BF16) for i in range(2)]
            junk = PN.sb("junk2", [128, D], BF16)
            ssr = [PN.sb(f"ss2{i}", [128, 2], F32) for i in range(2)]
            for tile in range(9):
                rows = 128 if tile < 8 else TS
                n_, s_ = xn2[tile % 2], ssr[tile % 2]
                T.op(ACT, lambda: nc.scalar.activation(out=junk.t[0:rows, :], in_=x1.t[0:rows, tile, :], func=AF.Square, accum_out=s_.t[0:rows, 0:1]),
                     reads=[x1b[tile]], writes=[junk.b, s_.b])
                T.op(ACT, lambda: nc.scalar.activation(out=s_.t[0:rows, 1:2], in_=s_.t[0:rows, 0:1], func=AF.Ln, scale=1.0 / D, bias=EPS), reads=[s_.b], writes=[s_.b])
                T.op(ACT, lambda: nc.scalar.activation(out=s_.t[0:rows, 1:2], in_=s_.t[0:rows, 1:2], func=AF.Exp, scale=-0.5), reads=[s_.b], writes=[s_.b])
                T.op(DVE, lambda: nc.vector.scalar_tensor_tensor(out=n_.t[0:rows, :], in0=x1.t[0:rows, tile, :], scalar=s_.t[0:rows, 1:2], in1=gbc.t[0:rows, :],
                                                                 op0=ALU.mult, op1=ALU.mult), reads=[x1b[tile], s_.b, gbc.b], writes=[n_.b])
                for half in range(2):
                    bk = banks[(2 * tile + half) % 4]
                    bkb = bk.t[:].bitcast(BF16)
                    for kk in range(8):
                        k = half * 8 + kk
                        T.op(PE, lambda: nc.tensor.transpose(out=bkb[:, kk * 128:kk * 128 + rows], in_=n_.t[0:rows, k * 128:(k + 1) * 128],
                                                             identity=identb[0:rows, 0:rows]), reads=[n_.b, cb.b], writes=[bk.b])
                    src = bkb.rearrange("p (k t) -> p k t", k=8)[:, :, 0:rows]
                    d_ap = h2T.t[:, half * 8:half * 8 + 8, tile * 128:tile * 128 + rows]
                    if half == 0:
                        T.op(ACT, lambda: nc.scalar.copy(out=d_ap, in_=src), reads=[bk.b], writes=[h2T.b])
                    else:
                        T.op(DVE, lambda: nc.vector.tensor_copy(out=d_ap, in_=src), reads=[bk.b], writes=[h2T.b])
            PN.close()
            WS = WStream(P10, 3, KD)
            for e in range(8):
                WS.add(w_up, 0, KD, e * 1024)
                WS.add(w_up, 0, KD, e * 1024 + 512)
                for cg in range(4):
                    WS.add(w_dn, e * 1024, 8, cg * 512)
            ffq = P10.sb("ffq", [128, 8, NTP], BF16)
            T.op(DVE, lambda: nc.vector.memset(ffq.t[:, :, NT:NTP], 0.0), writes=[ffq.b])
            sqf = [P10.sb(f"sqf{i}", [128, 512], F32) for i in range(2)]
            it = 0
            for e in range(8):
                for ub in range(2):
                    wt = WS.get()
                    for mm in range(4):
                        for gi, (t0, n) in enumerate(GROUPS):
                            bk = banks[it % 4]
                            s_ = sqf[it % 2]
                            mm_fm(bk, n, 0, wt, KD, mm, h2T, t0)
                            T.op(ACT, lambda: nc.scalar.activation(out=s_.t[:, 0:n], in_=bk.t[:, 0:n], func=AF.Square), reads=[bk.b], writes=[s_.b])
                            T.op(DVE, lambda: nc.vector.scalar_tensor_tensor(out=ffq.t[:, ub * 4 + mm, t0:t0 + n], in0=bk.t[:, 0:n], scalar=0.0, in1=s_.t[:, 0:n],
                                                                             op0=ALU.is_gt, op1=ALU.mult), reads=[bk.b, s_.b], writes=[ffq.b])
                            it += 1
                for cg in range(4):
                    wt = WS.get()
                    for tile in range(9):
                        rows = 128 if tile < 8 else TS
                        bk = banks[4 + it % 4]
                        mm_tm(bk, rows, wt, 8, ffq, tile * 128)
                        xs_ = x1.t[0:rows, tile, cg * 512:(cg + 1) * 512]
                        T.op(DVE, lambda: nc.vector.tensor_tensor(out=xs_, in0=bk.t[0:rows, :], in1=xs_, op=ALU.add), reads=[bk.b, x1b[tile]], writes=[x1b[tile]])
                        it += 1
            for tile in range(9):
                if tile < 8:
                    T.dma(SP, lambda: nc.sync.dma_start(out=y_own[tile * 128:(tile + 1) * 128, :], in_=x1.t[:, tile, :]), reads=[x1b[tile]])
                else:
                    T.dma(SP, lambda: nc.sync.dma_start(out=y_s[:, :], in_=x1.t[0:TS, tile, :]), reads=[x1b[tile]])
            P10.close()
            T.finish()
            print(f"[kernel] arena peak {A.peak / 1024:.1f} KiB/partition; instr pe={PE.cnt} dve={DVE.cnt} act={ACT.cnt}")
    except _Stop:
        pass
    return nc


def _consts():
    c = np.zeros((128, C_END), np.float32)
    j = np.arange(128)[:, None]
    s = np.arange(128)[None, :]
    c[:, C_IDENT:C_IDENT + 128] = np.eye(128, dtype=np.float32)
    c[:, C_TINC:C_TINC + 128] = (j >= s)
    c[:, C_LSTR:C_LSTR + 128] = (j < s)
    c[:, C_ONES:C_ONES + 128] = 1.0
    n = np.arange(512)[None, :]
    for r in range(4):
        c[:, C_MASKD + r * 512:C_MASKD + (r + 1) * 512] = (128 * r + j < n)
    rows = np.arange(64)[:, None]
    cols = np.arange(512)[None, :]
    bt = cols % 64
    c[0:64, C_M64:C_M64 + 512] = ((rows // 4) == (bt // 4)) & ((rows % 4) < (bt % 4))
    return c


def _fm(v, k):
    return np.ascontiguousarray(np.asarray(v, np.float32).reshape(k, 128).T)


def make_in_maps(inp, npool=None, core_list=range(8)):
    f = lambda a: np.ascontiguousarray(np.asarray(a, np.float32))
    cst = _consts()
    x_prompt, x_sample, mem_prompt = f(inp["x_prompt"]), f(inp["x_sample"]), f(inp["mem_prompt"])
    poolk = f(inp["cache_sb_k"])[0].reshape(-1, NH * HD)
    poolv = f(inp["cache_sb_v"])[0].reshape(-1, NH * HD)
    pt = np.asarray(inp["page_table"], np.int32)
    sconv = f(inp["state_conv"])[0]
    cmk, cmv = f(inp["cache_mem_k"])[0], f(inp["cache_mem_v"])[0]
    shared = {
        "poolk": poolk, "poolv": poolv,
        "w_in": f(inp["w_in"])[0], "w_co": f(inp["w_conv_out"])[0], "w_so": f(inp["w_sb_out"])[0], "w_mo": f(inp["w_mem_out"])[0],
        "w_o": f(inp["w_o"])[0], "w_up": f(inp["w_up"])[0], "w_dn": f(inp["w_down"])[0], "w_mkv": f(inp["w_mem_kv"])[0],
        "grow": np.ascontiguousarray(np.stack([f(inp["g_mix"])[0], f(inp["g_mlp"])[0], f(inp["g_mem"])[0]])),
        "bsb": f(inp["b_sb"]).reshape(1, NH), "cst": cst,
    }
    vbase = np.zeros((128, V_END), np.float32)
    vbase[:, V_BGATE:V_BGATE + 48] = _fm(f(inp["b_gate"])[0], 48)
    wdw = f(inp["w_dw"])[0]
    vbase[:, V_WDW:V_WDW + 124] = wdw.T.reshape(4, 128, 31).transpose(1, 0, 2).reshape(128, 124)
    vbase[:, V_BDW:V_BDW + 4] = _fm(f(inp["b_dw"])[0], 4)
    vbase[:, V_GCLN:V_GCLN + 4] = _fm(f(inp["g_conv_ln"])[0], 4)
    vbase[:, V_BCLN:V_BCLN + 4] = _fm(f(inp["b_conv_ln"])[0], 4)
    vbase[:, V_GQSB] = f(inp["g_q_sb"])[0]
    vbase[:, V_GKSB] = f(inp["g_k_sb"])[0]
    vbase[:, V_GQM] = f(inp["g_q_mem"])[0]
    vbase[:, V_GKM] = f(inp["g_k_mem"])[0]
    vbase[:, V_ROWID] = np.arange(128, dtype=np.float32)
    maps = []
    for c in core_list:
        b, half = c // 2, c % 2
        v = vbase.copy()
        v[:, V_PM] = float(half)
        m = dict(shared)
        m.update({
            "x_prev": x_prompt[b, 0:TP], "x_own": x_prompt[b, half * TP:(half + 1) * TP],
            "x_s": x_sample[c * NSAMP:(c + 1) * NSAMP].reshape(TS, D), "mem": mem_prompt[b],
            "pt": np.ascontiguousarray(pt[c * NSAMP:(c + 1) * NSAMP].reshape(1, NSAMP * NPG)),
            "sconv": sconv[c * NSAMP:(c + 1) * NSAMP].reshape(NSAMP * 30, CW),
            "cmk": cmk[c * NSAMP:(c + 1) * NSAMP].reshape(NSAMP * ML, MH * HD),
            "cmv": cmv[c * NSAMP:(c + 1) * NSAMP].reshape(NSAMP * ML, MH * HD),
            "vecs": v,
        })
        maps.append(m)
    return maps


def assemble(res):
    B = 4
    yp = np.zeros((B, 2 * TP, D), np.float32)
    ys = np.zeros((128, 4, D), np.float32)
    kp = np.zeros((1, B, 2 * TP, NH, HD), np.float32)
    vp = np.zeros_like(kp)
    ks = np.zeros((1, 128, 4, NH, HD), np.float32)
    vs = np.zeros_like(ks)
    cp = np.zeros((1, B, 30, CW), np.float32)
    cs = np.zeros((1, 128, 30, CW), np.float32)
    mk = np.zeros((1, B, ML, MH, HD), np.float32)
    mv = np.zeros_like(mk)
    for c, r in enumerate(res):
        b, half = c // 2, c % 2
        sl = slice(half * TP, (half + 1) * TP)
        ss = slice(c * NSAMP, (c + 1) * NSAMP)
        yp[b, sl] = r["y_own"]
        ys[ss] = r["y_s"].reshape(NSAMP, 4, D)
        kp[0, b, sl] = r["kp"].reshape(TP, NH, HD)
        vp[0, b, sl] = r["vp"].reshape(TP, NH, HD)
        ks[0, ss] = r["ks"].reshape(NSAMP, 4, NH, HD)
        vs[0, ss] = r["vs"].reshape(NSAMP, 4, NH, HD)
        cs[0, ss] = r["convs"]
        if half == 1:
            cp[0, b] = r["convp"]
        else:
            mk[0, b] = r["memk"].reshape(ML, MH, HD)
            mv[0, b] = r["memv"].reshape(ML, MH, HD)
    return (yp, ys, kp, vp, ks, vs, cp, cs, mk, mv)


def kernel(**inputs):
    npool = int(np.asarray(inputs["cache_sb_k"]).shape[1])
    nc = build_nc(npool)
    in_maps = make_in_maps(inputs)
    res = run_bass_kernel_spmd(nc, in_maps, core_ids=list(range(8)))
    return assemble(res.results)
```

```python
import numpy as np
from contextlib import ExitStack
import concourse.bass as bass
import concourse.mybir as mybir
from concourse.bass_utils import run_bass_kernel_spmd

F32 = mybir.dt.float32
BF16 = mybir.dt.bfloat16
I32 = mybir.dt.int32
AF = mybir.ActivationFunctionType
ALU = mybir.AluOpType

D = 2048
KD = 16
NH = 8
HD = 128
CW = 512
MH = 4
ML = 256
DFF = 8192
TP = 1024
TS = 64
NT = TP + TS
NTP = TP + 128
NSAMP = 16
NPG = 16
INC = 10752
SCALE = HD ** -0.5
EPS = 1e-6
GROUPS = [(0, 512), (512, 512), (1024, 64)]
C_IDENT, C_TINC, C_LSTR, C_ONES, C_MASKD, C_M64, C_END = 0, 128, 256, 384, 512, 2560, 3072
V_BGATE, V_WDW, V_BDW, V_GCLN, V_BCLN, V_GQSB, V_GKSB, V_GQM, V_GKM, V_ROWID, V_PM, V_END = \
    0, 48, 172, 176, 180, 184, 185, 186, 187, 188, 189, 190


class Buf:
    __slots__ = ("name", "w", "r", "xr")

    def __init__(self, name, fence=None, xr=False):
        self.name = name
        self.w = None
        self.r = dict(fence) if fence else {}
        self.xr = xr


class Eng:
    def __init__(self, name, eng, sem, self_sync):
        self.name, self.eng, self.sem, self.self_sync = name, eng, sem, self_sync
        self.cnt = 0
        self.known = {}


class TB:
    __slots__ = ("t", "b")

    def __init__(self, t, b):
        self.t, self.b = t, b


class Trk:
    NDMA = 16

    def __init__(self, nc, stack):
        self.nc = nc
        mk = lambda n: stack.enter_context(nc.semaphore(n))
        self.pe = Eng("pe", nc.tensor, mk("s_pe"), False)
        self.dve = Eng("dve", nc.vector, mk("s_dve"), True)
        self.act = Eng("act", nc.scalar, mk("s_act"), True)
        self.pool = Eng("pool", nc.gpsimd, mk("s_pool"), True)
        self.sp = Eng("sp", nc.sync, mk("s_sp"), False)
        self.dsem = {q: [[mk(f"d_{q}{i}"), 0] for i in range(self.NDMA)] for q in ("sp", "pool")}
        self.dnext = {"sp": 0, "pool": 0}
        self.fence = {}

    def newbuf(self, name, xr=False):
        return Buf(name, self.fence, xr)

    def release(self, bufs):
        for b in bufs:
            evs = list(b.r.values())
            if b.w is not None:
                evs.append(b.w)
            for sem, val in evs:
                k = id(sem)
                if k not in self.fence or self.fence[k][1] < val:
                    self.fence[k] = (sem, val)

    def _wait(self, E, evs):
        best = {}
        for ev in evs:
            if ev is None:
                continue
            sem, val = ev
            k = id(sem)
            if k not in best or best[k][1] < val:
                best[k] = (sem, val)
        for k, (sem, val) in best.items():
            if sem is E.sem and not E.self_sync:
                continue
            if E.known.get(k, 0) >= val:
                continue
            E.eng.wait_ge(sem, val)
            E.known[k] = val

    @staticmethod
    def _deps(reads, writes, own=None):
        evs = []
        for b in reads:
            evs.append(b.w)
            if b.xr:
                evs.extend(e for e in b.r.values() if e[0] is not own)
        for b in writes:
            evs.append(b.w)
            evs.extend(b.r.values())
        return evs

    @staticmethod
    def _mark(ev, reads, writes):
        k = id(ev[0])
        for b in reads:
            o = b.r.get(k)
            if o is None or o[1] < ev[1]:
                b.r[k] = ev
        for b in writes:
            b.w = ev
            b.r = {}

    def op(self, E, fn, reads=(), writes=()):
        self._wait(E, self._deps(reads, writes, E.sem))
        inst = fn()
        E.cnt += 1
        inst.then_inc(E.sem, 1)
        self._mark((E.sem, E.cnt), reads, writes)
        return inst

    def dma(self, E, fn, reads=(), writes=()):
        slots = self.dsem[E.name]
        i = self.dnext[E.name]
        self.dnext[E.name] = (i + 1) % len(slots)
        sem, n = slots[i]
        evs = self._deps(reads, writes)
        if n > 0:
            evs.append((sem, 16 * n))
        self._wait(E, evs)
        inst = fn()
        inst.then_inc(sem, 16)
        slots[i][1] = n + 1
        self._mark((sem, 16 * (n + 1)), reads, writes)

    def finish(self):
        evs = []
        for q in self.dsem:
            for sem, n in self.dsem[q]:
                if n > 0:
                    evs.append((sem, 16 * n))
        self._wait(self.sp, evs)


class Arena:
    def __init__(self, nc, stack, nbytes):
        self.t = stack.enter_context(nc.sbuf_tensor("arena", [128, nbytes // 4], F32))
        self.free_list = [(0, nbytes)]
        self.used = 0
        self.peak = 0

    def alloc(self, nbytes, name):
        nbytes = (nbytes + 31) // 32 * 32
        for i, (off, sz) in enumerate(self.free_list):
            if sz >= nbytes:
                if sz == nbytes:
                    self.free_list.pop(i)
                else:
                    self.free_list[i] = (off + nbytes, sz - nbytes)
                self.used += nbytes
                self.peak = max(self.peak, self.used)
                return off, nbytes
        raise MemoryError(f"arena: cannot fit {name} ({nbytes} B/partition); used={self.used} free={self.free_list}")

    def free(self, off, nbytes):
        self.used -= nbytes
        fl = sorted(self.free_list + [(off, nbytes)])
        out = []
        for o, z in fl:
            if out and out[-1][0] + out[-1][1] == o:
                out[-1] = (out[-1][0], out[-1][1] + z)
            else:
                out.append((o, z))
        self.free_list = out

    def view(self, off, shape, dt):
        n = 1
        for d in shape[1:]:
            n *= d
        esz = 2 if dt == BF16 else 4
        words = (n * esz + 3) // 4
        ap = self.t[:, off // 4:off // 4 + words]
        if dt != F32:
            ap = ap.bitcast(dt)
        ap = ap[:, 0:n]
        fr = shape[1:]
        if len(fr) == 2:
            ap = ap.rearrange("p (a b) -> p a b", a=fr[0])
        elif len(fr) == 3:
            ap = ap.rearrange("p (a b c) -> p a b c", a=fr[0], b=fr[1])
        return ap


class Phase:
    def __init__(self, A, T):
        self.A, self.T = A, T
        self.bufs = []
        self.blocks = []

    def buf(self, name):
        b = self.T.newbuf(name)
        self.bufs.append(b)
        return b

    def sb(self, name, shape, dt):
        n = 1
        for d in shape[1:]:
            n *= d
        off, nb = self.A.alloc(n * (2 if dt == BF16 else 4), name)
        self.blocks.append((off, nb))
        return TB(self.A.view(off, shape, dt), self.buf(name))

    def close(self):
        self.T.release(self.bufs)
        for off, nb in self.blocks:
            self.A.free(off, nb)
        self.bufs, self.blocks = [], []


class _Stop(Exception):
    pass


def build_nc(npool, dbg=False, stop_after=None):
    nc = bass.Bass("TRN2", target_bir_lowering=False)
    din = lambda n, s, d=F32: nc.dram_tensor(n, s, d, kind="ExternalInput").ap()
    dout = lambda n, s: nc.dram_tensor(n, s, F32, kind="ExternalOutput").ap()
    x_prev, x_own, x_s, mem = din("x_prev", [TP, D]), din("x_own", [TP, D]), din("x_s", [TS, D]), din("mem", [ML, D])
    poolk, poolv = din("poolk", [npool * 128, NH * HD]), din("poolv", [npool * 128, NH * HD])
    pt_d = din("pt", [1, NSAMP * NPG], I32)
    sconv = din("sconv", [NSAMP * 30, CW])
    cmk, cmv = din("cmk", [NSAMP * ML, MH * HD]), din("cmv", [NSAMP * ML, MH * HD])
    w_in, w_co, w_so, w_mo = din("w_in", [D, INC]), din("w_co", [CW, D]), din("w_so", [NH * HD, D]), din("w_mo", [MH * HD, D])
    w_o, w_up, w_dn, w_mkv = din("w_o", [D, D]), din("w_up", [D, DFF]), din("w_dn", [DFF, D]), din("w_mkv", [D, 2 * MH * HD])
    grow = din("grow", [3, D])
    vecs_d = din("vecs", [128, V_END])
    bsb_d = din("bsb", [1, NH])
    cst_d = din("cst", [128, C_END])
    y_own, y_s = dout("y_own", [TP, D]), dout("y_s", [TS, D])
    kp_o, vp_o = dout("kp", [TP, NH * HD]), dout("vp", [TP, NH * HD])
    ks_o, vs_o = dout("ks", [TS, NH * HD]), dout("vs", [TS, NH * HD])
    convp_o, convs_o = dout("convp", [30, CW]), dout("convs", [NSAMP, 30, CW])
    memk_o, memv_o = dout("memk", [ML, MH * HD]), dout("memv", [ML, MH * HD])
    hT_scr = nc.dram_tensor("hT_scr", [128, KD * NTP], BF16, kind="Internal").ap()
    u_scr_p = nc.dram_tensor("u_scr_p", [128, 4 * (30 + TP)], F32, kind="Internal").ap()
    u_scr_s = nc.dram_tensor("u_scr_s", [128, 4 * NSAMP * 34], F32, kind="Internal").ap()

    def dump(T, name, tb, flat, dt=BF16, reads=None):
        if not dbg:
            return
        n = 1
        for d_ in tb.t.shape[1:]:
            n *= d_
        o = nc.dram_tensor("dbg_" + name, [128, n], dt, kind="ExternalOutput").ap()
        T.dma(T.sp, lambda: nc.sync.dma_start(out=o[:, :], in_=tb.t[:].rearrange(flat)), reads=reads or [tb.b])

    try:
        with ExitStack() as st:
            T = Trk(nc, st)
            PE, DVE, ACT, POOL, SP = T.pe, T.dve, T.act, T.pool, T.sp

            def checkpoint(name):
                if dbg or stop_after == "print":
                    print(f"[ckpt] {name}: pe={PE.cnt} dve={DVE.cnt} act={ACT.cnt} arena_used={A.used / 1024:.1f} peak={A.peak / 1024:.1f}")
                    A.peak = A.used
                if stop_after == name:
                    T.finish()
                    raise _Stop()
            A = Arena(nc, st, 206 * 1024)
            G = Phase(A, T)

            banks = [TB(st.enter_context(nc.psum_tensor(f"pb{i}", [128, 512], F32)), T.newbuf(f"pb{i}", xr=True)) for i in range(8)]

            def ld(dst, src, reads=(), q=None):
                E = q or SP
                T.dma(E, lambda: E.eng.dma_start(out=dst_ap(dst), in_=src), reads=list(reads), writes=[dst.b])

            def dst_ap(x):
                return x.t[:] if isinstance(x, TB) else x

            cst = G.sb("cst", [128, C_MASKD], F32)
            vecs = G.sb("vecs", [128, V_END], F32)
            bsb = G.sb("bsb", [128, NH], F32)
            T.dma(SP, lambda: nc.sync.dma_start(out=cst.t[:], in_=cst_d[:, 0:C_MASKD]), writes=[cst.b])
            T.dma(SP, lambda: nc.sync.dma_start(out=vecs.t[:], in_=vecs_d[:, :]), writes=[vecs.b])
            T.dma(SP, lambda: nc.sync.dma_start(out=bsb.t[:], in_=bsb_d.partition_broadcast(128)), writes=[bsb.b])
            identf = cst.t[:, C_IDENT:C_IDENT + 128]
            tinc = cst.t[:, C_TINC:C_TINC + 128]
            lstr = cst.t[:, C_LSTR:C_LSTR + 128]
            onesf = cst.t[:, C_ONES:C_ONES + 128]
            cb = G.sb("cstb", [128, 256], BF16)
            T.op(DVE, lambda: nc.vector.tensor_copy(out=cb.t[:, 0:128], in_=identf), reads=[cst.b], writes=[cb.b])
            T.op(DVE, lambda: nc.vector.tensor_copy(out=cb.t[:, 128:256], in_=onesf), reads=[cst.b], writes=[cb.b])
            identb = cb.t[:, 0:128]
            onesb = cb.t[:, 128:256]
            pm = vecs.t[:, V_PM:V_PM + 1]
            sm = G.sb("small", [128, 16], F32)
            T.op(DVE, lambda: nc.vector.tensor_scalar(out=sm.t[:, 0:1], in0=pm, scalar1=SCALE, scalar2=None, op0=ALU.mult),
                 reads=[vecs.b], writes=[sm.b])
            T.op(DVE, lambda: nc.vector.tensor_scalar(out=sm.t[:, 1:2], in0=pm, scalar1=-1.0, scalar2=30.0, op0=ALU.add, op1=ALU.mult),
                 reads=[vecs.b], writes=[sm.b])
            T.op(DVE, lambda: nc.vector.tensor_scalar(out=sm.t[:, 2:10], in0=bsb.t[:], scalar1=pm, scalar2=sm.t[:, 1:2],
                                                      op0=ALU.mult, op1=ALU.add), reads=[vecs.b, bsb.b, sm.b], writes=[sm.b])
            ptb = G.sb("ptb", [128, NSAMP * NPG], I32)
            idx = G.sb("idx", [128, NSAMP * NPG], I32)
            T.dma(SP, lambda: nc.sync.dma_start(out=ptb.t[:], in_=pt_d.partition_broadcast(128)), writes=[ptb.b])
            T.op(DVE, lambda: nc.vector.tensor_scalar(out=idx.t[:], in0=ptb.t[:], scalar1=128.0, scalar2=vecs.t[:, V_ROWID:V_ROWID + 1],
                                                      op0=ALU.mult, op1=ALU.add), reads=[ptb.b, vecs.b], writes=[idx.b])
            dd = T.newbuf("convs_rows")
            T.dma(SP, lambda: nc.sync.dma_start(out=convs_o[:, 0:26, :], in_=sconv.rearrange("(b r) c -> b r c", r=30)[:, 4:30, :]),
                  writes=[dd])

            class WStream:
                def __init__(self, ph, nslots, kc):
                    self.slots = [ph.sb(f"w{i}", [128, kc, 512], BF16) for i in range(nslots)]
                    self.i = 0
                    self.plan = []
                    self.issued = 0
                    self.taken = 0

                def add(self, W, r0, kc, c0):
                    self.plan.append((W, r0, kc, c0))

                def _issue(self):
                    W, r0, kc, c0 = self.plan[self.issued]
                    s = self.slots[self.issued % len(self.slots)]
                    T.dma(POOL, lambda: nc.gpsimd.dma_start(
                        out=s.t[:, 0:kc, :], in_=W[r0:r0 + kc * 128, c0:c0 + 512].rearrange("(k p) c -> p k c", p=128)),
                        writes=[s.b])
                    self.issued += 1

                def get(self, hold=0):
                    while self.issued < len(self.plan) and self.issued < self.taken + len(self.slots) - hold:
                        self._issue()
                    s = self.slots[self.taken % len(self.slots)]
                    self.taken += 1
                    return s

                def prefetch(self):
                    while self.issued < len(self.plan) and self.issued < self.taken + len(self.slots):
                        self._issue()

            def mm_fm(bank, n, col0, wt, kc, mcol, src, tok0):
                for k in range(kc):
                    T.op(PE, lambda: nc.tensor.matmul(bank.t[:, col0:col0 + n], lhsT=wt.t[:, k, mcol * 128:(mcol + 1) * 128],
                                                      rhs=src.t[:, k, tok0:tok0 + n], start=(k == 0), stop=(k == kc - 1)),
                         reads=[wt.b, src.b], writes=[bank.b])

            def mm_tm(bank, rows, wt, kc, src, tok0, koff=0):
                for k in range(kc):
                    T.op(PE, lambda: nc.tensor.matmul(bank.t[:, :], lhsT=src.t[:, koff + k, tok0:tok0 + 128], rhs=wt.t[:, k, :],
                                                      start=(k == 0), stop=(k == kc - 1)), reads=[wt.b, src.b], writes=[bank.b])

            def rstd_from(bank, n, inv, out):
                T.op(ACT, lambda: nc.scalar.activation(out=out.t[:, 0:n], in_=bank.t[:, 0:n], func=AF.Ln, scale=inv, bias=EPS),
                     reads=[bank.b], writes=[out.b])
                T.op(ACT, lambda: nc.scalar.activation(out=out.t[:, 0:n], in_=out.t[:, 0:n], func=AF.Exp, scale=-0.5),
                     reads=[out.b], writes=[out.b])

            def headnorm(bank, n, gcol, sq, rb, bank2, out_ap, out_b):
                T.op(ACT, lambda: nc.scalar.activation(out=sq.t[:, 0:n], in_=bank.t[:, 0:n], func=AF.Square), reads=[bank.b], writes=[sq.b])
                T.op(PE, lambda: nc.tensor.matmul(bank2.t[:, 0:n], lhsT=onesf, rhs=sq.t[:, 0:n], start=True, stop=True),
                     reads=[cst.b, sq.b], writes=[bank2.b])
                rstd_from(bank2, n, 1.0 / HD, rb)
                T.op(DVE, lambda: nc.vector.scalar_tensor_tensor(out=out_ap, in0=bank.t[:, 0:n], scalar=vecs.t[:, gcol:gcol + 1],
                                                                 in1=rb.t[:, 0:n], op0=ALU.mult, op1=ALU.mult),
                     reads=[bank.b, vecs.b, rb.b], writes=[out_b])

            def norm_tiles(ph, xd, ntok, grow_i, dst, dtok0, tag):
                gbc = ph.sb(f"gbc{tag}", [128, D], F32)
                T.dma(SP, lambda: nc.sync.dma_start(out=gbc.t[:], in_=grow[grow_i:grow_i + 1, :].partition_broadcast(128)), writes=[gbc.b])
                xs = [ph.sb(f"xt{tag}{i}", [128, D], F32) for i in range(2)]
                xn = [ph.sb(f"xn{tag}{i}", [128, D], BF16) for i in range(2)]
                junk = ph.sb(f"junk{tag}", [128, D], BF16)
                ssr = [ph.sb(f"ss{tag}{i}", [128, 2], F32) for i in range(2)]
                ntile = (ntok + 127) // 128
                for i in range(ntile):
                    rows = min(128, ntok - i * 128)
                    x_, n_, s_ = xs[i % 2], xn[i % 2], ssr[i % 2]
                    T.dma(SP, lambda: nc.sync.dma_start(out=x_.t[0:rows, :], in_=xd[i * 128:i * 128 + rows, :]), writes=[x_.b])
                    T.op(ACT, lambda: nc.scalar.activation(out=junk.t[0:rows, :], in_=x_.t[0:rows, :], func=AF.Square, accum_out=s_.t[0:rows, 0:1]),
                         reads=[x_.b], writes=[junk.b, s_.b])
                    T.op(ACT, lambda: nc.scalar.activation(out=s_.t[0:rows, 1:2], in_=s_.t[0:rows, 0:1], func=AF.Ln, scale=1.0 / D, bias=EPS),
                         reads=[s_.b], writes=[s_.b])
                    T.op(ACT, lambda: nc.scalar.activation(out=s_.t[0:rows, 1:2], in_=s_.t[0:rows, 1:2], func=AF.Exp, scale=-0.5),
                         reads=[s_.b], writes=[s_.b])
                    T.op(DVE, lambda: nc.vector.scalar_tensor_tensor(out=n_.t[0:rows, :], in0=x_.t[0:rows, :], scalar=s_.t[0:rows, 1:2],
                                                                     in1=gbc.t[0:rows, :], op0=ALU.mult, op1=ALU.mult),
                         reads=[x_.b, s_.b, gbc.b], writes=[n_.b])
                    for half in range(2):
                        bk = banks[(2 * i + half) % 4]
                        bkb = bk.t[:].bitcast(BF16)
                        for kk in range(8):
                            k = half * 8 + kk
                            T.op(PE, lambda: nc.tensor.transpose(out=bkb[:, kk * 128:kk * 128 + rows], in_=n_.t[0:rows, k * 128:(k + 1) * 128],
                                                                 identity=identb[0:rows, 0:rows]), reads=[n_.b, cb.b], writes=[bk.b])
                        src = bkb.rearrange("p (k t) -> p k t", k=8)[:, :, 0:rows]
                        d_ap = dst.t[:, half * 8:half * 8 + 8, dtok0 + i * 128:dtok0 + i * 128 + rows]
                        if half == 0:
                            T.op(ACT, lambda: nc.scalar.copy(out=d_ap, in_=src), reads=[bk.b], writes=[dst.b])
                        else:
                            T.op(DVE, lambda: nc.vector.tensor_copy(out=d_ap, in_=src), reads=[bk.b], writes=[dst.b])


            L_hT = Phase(A, T)
            hT = L_hT.sb("hT", [128, KD, NTP], BF16)
            T.op(DVE, lambda: nc.vector.memset(hT.t[:, :, NT:NTP], 0.0), writes=[hT.b])
            L_cT = Phase(A, T)
            cT = L_cT.sb("cT", [128, 4, NT], BF16)
            P2 = Phase(A, T)
            norm_tiles(P2, x_own, TP, 0, hT, 0, "o")
            norm_tiles(P2, x_s, TS, 0, hT, TP, "s")
            P2.close()

            checkpoint("P_A")
            P3 = Phase(A, T)
            hTpl = P3.sb("hTpl", [128, KD, 128], BF16)
            PN = Phase(A, T)
            norm_tiles(PN, x_prev[TP - 128:TP, :], 128, 0, hTpl, 0, "l")
            PN.close()
            WS = WStream(P3, 2, KD)
            for c0 in (0, 512):
                WS.add(w_in, 0, KD, c0)
            uxp = P3.sb("uxp", [128, 4, 30 + TP], F32)
            uxs = P3.sb("uxs", [128, 4, NSAMP, 34], F32)
            tmp = [P3.sb(f"ctmp{i}", [128, 512], F32) for i in range(2)]
            sct = [P3.sb(f"sct{i}", [128, CW], F32) for i in range(2)]
            for i in range(4):
                s_ = sct[i % 2]
                T.dma(SP, lambda: nc.sync.dma_start(out=s_.t[0:120, :], in_=sconv[i * 120:(i + 1) * 120, :]), writes=[s_.b])
                bk = banks[i % 2]
                for m in range(4):
                    T.op(PE, lambda: nc.tensor.transpose(out=bk.t[:, m * 128:m * 128 + 120], in_=s_.t[0:120, m * 128:(m + 1) * 128],
                                                         identity=identf[0:120, 0:120]), reads=[s_.b, cst.b], writes=[bk.b])
                T.op(ACT, lambda: nc.scalar.copy(out=uxs.t[:, :, 4 * i:4 * i + 4, 0:30],
                                                 in_=bk.t[:].rearrange("p (m x) -> p m x", m=4)[:, :, 0:120].rearrange("p m (b r) -> p m b r", b=4)),
                     reads=[bk.b], writes=[uxs.b])

            CG = [(hTpl, 0, 128)] + [(hT, t0, n) for (t0, n) in GROUPS]

            def u_dst(m, gi):
                if gi == 0:
                    return uxp.t[:, m, 0:30], uxp.b
                t0, n = GROUPS[gi - 1]
                if gi < 3:
                    return uxp.t[:, m, 30 + t0:30 + t0 + n], uxp.b
                return uxs.t[:, m, :, 30:34], uxs.b

            def u_src(t_ap, gi, n):
                if gi == 0:
                    return t_ap[:, 98:128]
                if gi < 3:
                    return t_ap[:, 0:n]
                return t_ap[:, 0:n].rearrange("p (b t) -> p b t", t=4)

            wt = WS.get()
            it = 0
            for m in range(4):
                for gi, (src_h, t0, n) in enumerate(CG):
                    bk = banks[it % 4]
                    mm_fm(bk, n, 0, wt, KD, m, src_h, t0)
                    d_ap, d_b = u_dst(m, gi)
                    T.op(ACT, lambda: nc.scalar.copy(out=d_ap, in_=u_src(bk.t, gi, n)), reads=[bk.b], writes=[d_b])
                    it += 1
            wt = WS.get()
            for m in range(4):
                for gi, (src_h, t0, n) in enumerate(CG):
                    bk = banks[it % 4]
                    mm_fm(bk, n, 0, wt, KD, m, src_h, t0)
                    t_ = tmp[it % 2]
                    T.op(ACT, lambda: nc.scalar.activation(out=t_.t[:, 0:n], in_=bk.t[:, 0:n], func=AF.Sigmoid), reads=[bk.b], writes=[t_.b])
                    d_ap, d_b = u_dst(m, gi)
                    if gi == 0:
                        T.op(DVE, lambda: nc.vector.scalar_tensor_tensor(out=d_ap, in0=d_ap, scalar=pm, in1=u_src(t_.t, gi, n), op0=ALU.mult, op1=ALU.mult),
                             reads=[d_b, vecs.b, t_.b], writes=[d_b])
                    else:
                        T.op(DVE, lambda: nc.vector.tensor_tensor(out=d_ap, in0=d_ap, in1=u_src(t_.t, gi, n), op=ALU.mult), reads=[d_b, t_.b], writes=[d_b])
                    it += 1
            cst_o = P3.sb("cst_o", [128, CW], F32)
            bk = banks[4]
            for m in range(4):
                T.op(PE, lambda: nc.tensor.transpose(out=bk.t[0:30, m * 128:(m + 1) * 128], in_=uxp.t[:, m, TP:TP + 30], identity=identf),
                     reads=[uxp.b, cst.b], writes=[bk.b])
            T.op(ACT, lambda: nc.scalar.copy(out=cst_o.t[0:30, :], in_=bk.t[0:30, :]), reads=[bk.b], writes=[cst_o.b])
            T.dma(SP, lambda: nc.sync.dma_start(out=convp_o[:, :], in_=cst_o.t[0:30, :]), reads=[cst_o.b])
            cst_s = P3.sb("cst_s", [128, CW], F32)
            ucp = P3.sb("ucp", [128, 4, TS], F32)
            T.op(DVE, lambda: nc.vector.tensor_copy(out=ucp.t[:].rearrange("p m (b t) -> p m b t", t=4), in_=uxs.t[:, :, :, 30:34]),
                 reads=[uxs.b], writes=[ucp.b])
            bk = banks[5]
            for m in range(4):
                T.op(PE, lambda: nc.tensor.transpose(out=bk.t[0:TS, m * 128:(m + 1) * 128], in_=ucp.t[:, m, :], identity=identf),
                     reads=[ucp.b, cst.b], writes=[bk.b])
            T.op(ACT, lambda: nc.scalar.copy(out=cst_s.t[0:TS, :], in_=bk.t[0:TS, :]), reads=[bk.b], writes=[cst_s.b])
            for b in range(NSAMP):
                T.dma(SP, lambda: nc.sync.dma_start(out=convs_o[b, 26:30, :], in_=cst_s.t[4 * b:4 * b + 4, :]), reads=[cst_s.b])
            u_bufp, u_bufs = T.newbuf("u_scr_p"), T.newbuf("u_scr_s")
            T.dma(SP, lambda: nc.sync.dma_start(out=u_scr_p[:, :], in_=uxp.t[:].rearrange("p m t -> p (m t)")), reads=[uxp.b], writes=[u_bufp])
            T.dma(SP, lambda: nc.sync.dma_start(out=u_scr_s[:, :], in_=uxs.t[:].rearrange("p m b r -> p (m b r)")), reads=[uxs.b], writes=[u_bufs])
            P3.close()

            checkpoint("P_B")
            L_kv = Phase(A, T)
            kT = L_kv.sb("kT", [128, NH, 2 * TP], BF16)
            Vtok = L_kv.sb("Vtok", [128, 16, NH * HD], BF16)
            L_q = Phase(A, T)
            kTs = L_q.sb("kTs", [128, NH, TS], BF16)
            Vs = L_q.sb("Vs", [128, NH * HD], BF16)
            P1 = Phase(A, T)
            hTp = P1.sb("hTp", [128, KD, TP], BF16)
            PN = Phase(A, T)
            norm_tiles(PN, x_prev, TP, 0, hTp, 0, "p")
            PN.close()
            checkpoint("PC0")
            WS = WStream(P1, 2, KD)
            for c0 in (2048, 2560, 3072, 3584):
                WS.add(w_in, 0, KD, c0)
            wk_sq = [P1.sb(f"sq4{i}", [128, 512], F32) for i in range(2)]
            wk_rb = [P1.sb(f"rb4{i}", [128, 512], F32) for i in range(2)]
            kn = [P1.sb(f"kn{i}", [128, 512], F32) for i in range(2)]
            kst = [P1.sb(f"kst{i}", [128, 512], F32) for i in range(1)]
            vst = kst
            it = 0
            bkA = [banks[0], banks[1], banks[6]]
            gix = [0]
            for blk in range(2):
                wt = WS.get()
                items = []
                for hh in range(4):
                    h = blk * 4 + hh
                    for g in range(2):
                        items.append(("prev", hh, h, g * 512, 512, g))
                    for gi, (t0, n) in enumerate(GROUPS):
                        items.append(("own", hh, h, t0, n, gi))
                base = gix[0]

                def kA(ix, item):
                    kind, hh, h, t0, n, gi = item
                    mm_fm(bkA[ix % 3], n, 0, wt, KD, hh, hTp if kind == "prev" else hT, t0)

                def kB(ix, item):
                    kind, hh, h, t0, n, gi = item
                    bk, bk2 = bkA[ix % 3], banks[2 + ix % 2]
                    if kind == "prev":
                        headnorm(bk, 512, V_GKSB, wk_sq[ix % 2], wk_rb[ix % 2], bk2, kT.t[:, h, t0:t0 + 512], kT.b)
                    else:
                        k_ = kn[ix % 2]
                        headnorm(bk, n, V_GKSB, wk_sq[ix % 2], wk_rb[ix % 2], bk2, k_.t[:, 0:n], k_.b)

                def kC(ix, item):
                    kind, hh, h, t0, n, gi = item
                    if kind == "prev":
                        return
                    k_, bk3, s_ = kn[ix % 2], banks[4 + ix % 2], kst[0]
                    if gi < 2:
                        T.op(ACT, lambda: nc.scalar.copy(out=kT.t[:, h, TP + t0:TP + t0 + n], in_=k_.t[:, 0:n]), reads=[k_.b], writes=[kT.b])
                        for j in range(4):
                            T.op(PE, lambda: nc.tensor.transpose(out=bk3.t[:, j * 128:(j + 1) * 128], in_=k_.t[:, j * 128:(j + 1) * 128], identity=identf),
                                 reads=[k_.b, cst.b], writes=[bk3.b])
                        T.op(DVE, lambda: nc.vector.tensor_copy(out=s_.t[:], in_=bk3.t[:]), reads=[bk3.b], writes=[s_.b])
                        T.dma(SP, lambda: nc.sync.dma_start(out=kp_o[t0:t0 + n, h * 128:(h + 1) * 128].rearrange("(j p) d -> p j d", p=128),
                                                            in_=s_.t[:].rearrange("p (j d) -> p j d", j=4)), reads=[s_.b])
                    else:
                        T.op(ACT, lambda: nc.scalar.copy(out=kTs.t[:, h, :], in_=k_.t[:, 0:n]), reads=[k_.b], writes=[kTs.b])
                        T.op(PE, lambda: nc.tensor.transpose(out=bk3.t[0:TS, 0:128], in_=k_.t[:, 0:TS], identity=identf),
                             reads=[k_.b, cst.b], writes=[bk3.b])
                        T.op(DVE, lambda: nc.vector.tensor_copy(out=s_.t[0:TS, 0:128], in_=bk3.t[0:TS, 0:128]), reads=[bk3.b], writes=[s_.b])
                        T.dma(SP, lambda: nc.sync.dma_start(out=ks_o[:, h * 128:(h + 1) * 128], in_=s_.t[0:TS, 0:128]), reads=[s_.b])

                ni = len(items)
                for st_ in range(ni + 2):
                    if st_ < ni:
                        kA(base + st_, items[st_])
                    if 0 <= st_ - 1 < ni:
                        kB(base + st_ - 1, items[st_ - 1])
                    if 0 <= st_ - 2 < ni:
                        kC(base + st_ - 2, items[st_ - 2])
                gix[0] += ni
            it = gix[0]
            checkpoint("PC1")
            for blk in range(2):
                wt = WS.get()
                for tile in range(8):
                    bk = banks[it % 4]
                    mm_tm(bk, 128, wt, KD, hTp, tile * 128)
                    o_ap = Vtok.t[:, tile, blk * 512:(blk + 1) * 512]
                    if it % 2 == 0:
                        T.op(ACT, lambda: nc.scalar.copy(out=o_ap, in_=bk.t[:, :]), reads=[bk.b], writes=[Vtok.b])
                    else:
                        T.op(DVE, lambda: nc.vector.tensor_copy(out=o_ap, in_=bk.t[:, :]), reads=[bk.b], writes=[Vtok.b])
                    it += 1
                if blk == 0:
                    checkpoint("PC2")
                for tile in range(9):
                    if blk == 0 and tile == 8:
                        checkpoint("PC3")
                    rows = 128 if tile < 8 else TS
                    bk = banks[it % 4]
                    s_ = vst[0]
                    mm_tm(bk, rows, wt, KD, hT, tile * 128)
                    T.op(ACT, lambda: nc.scalar.copy(out=s_.t[0:rows, :], in_=bk.t[0:rows, :]), reads=[bk.b], writes=[s_.b])
                    if tile < 8:
                        T.op(DVE, lambda: nc.vector.tensor_copy(out=Vtok.t[:, 8 + tile, blk * 512:(blk + 1) * 512], in_=s_.t[:, :]),
                             reads=[s_.b], writes=[Vtok.b])
                        T.dma(SP, lambda: nc.sync.dma_start(out=vp_o[tile * 128:(tile + 1) * 128, blk * 512:(blk + 1) * 512], in_=s_.t[:, :]), reads=[s_.b])
                    else:
                        T.op(DVE, lambda: nc.vector.tensor_copy(out=Vs.t[0:TS, blk * 512:(blk + 1) * 512], in_=s_.t[0:TS, :]),
                             reads=[s_.b], writes=[Vs.b])
                        T.dma(SP, lambda: nc.sync.dma_start(out=vs_o[:, blk * 512:(blk + 1) * 512], in_=s_.t[0:TS, :]), reads=[s_.b])
                    it += 1
            P1.close()

            checkpoint("P_C")
            L_qm = Phase(A, T)
            qT = L_q.sb("qT", [128, NH, NT], BF16)
            qmT = L_qm.sb("qmT", [128, MH, NT], BF16)
            P4 = Phase(A, T)
            WS = WStream(P4, 2, KD)
            for c0 in (1024, 1536, 4096):
                WS.add(w_in, 0, KD, c0)
            wk_sq = [P4.sb(f"sqd{i}", [128, 512], F32) for i in range(2)]
            wk_rb = [P4.sb(f"rbd{i}", [128, 512], F32) for i in range(2)]
            it = 0
            bkA = [banks[0], banks[1], banks[6]]
            qix = 0
            for blk in range(3):
                wt = WS.get()
                items = []
                for hh in range(4):
                    for gi, (t0, n) in enumerate(GROUPS):
                        items.append((hh, t0, n))
                ni = len(items)
                for st_ in range(ni + 1):
                    if st_ < ni:
                        hh, t0, n = items[st_]
                        mm_fm(bkA[(qix + st_) % 3], n, 0, wt, KD, hh, hT, t0)
                    if 0 <= st_ - 1 < ni:
                        ix = qix + st_ - 1
                        hh, t0, n = items[st_ - 1]
                        if blk < 2:
                            headnorm(bkA[ix % 3], n, V_GQSB, wk_sq[ix % 2], wk_rb[ix % 2], banks[2 + ix % 2], qT.t[:, blk * 4 + hh, t0:t0 + n], qT.b)
                        else:
                            headnorm(bkA[ix % 3], n, V_GQM, wk_sq[ix % 2], wk_rb[ix % 2], banks[2 + ix % 2], qmT.t[:, hh, t0:t0 + n], qmT.b)
                qix += ni
            P4.close()
            dump(T, "qT", qT, "p k t -> p (k t)")
            dump(T, "qmT", qmT, "p k t -> p (k t)")
            dump(T, "kT", kT, "p k t -> p (k t)")
            hT_buf = T.newbuf("hT_dram")
            T.dma(SP, lambda: nc.sync.dma_start(out=hT_scr[:, :], in_=hT.t[:].rearrange("p k t -> p (k t)")), reads=[hT.b], writes=[hT_buf])
            L_hT.close()

            L_o = Phase(A, T)
            oT_sb = L_o.sb("oT_sb", [128, NH, NT], BF16)
            oT_mem = L_o.sb("oT_mem", [128, MH, NT], BF16)

            checkpoint("P_D")
            LC = Phase(A, T)
            uxp = LC.sb("uxp2", [128, 4, 30 + TP], F32)
            uxs = LC.sb("uxs2", [128, 4, NSAMP, 34], F32)
            acc = LC.sb("cacc", [128, 4, NT], F32)
            T.dma(SP, lambda: nc.sync.dma_start(out=uxp.t[:].rearrange("p m t -> p (m t)"), in_=u_scr_p[:, :]), reads=[u_bufp], writes=[uxp.b])
            T.dma(SP, lambda: nc.sync.dma_start(out=uxs.t[:].rearrange("p m b r -> p (m b r)"), in_=u_scr_s[:, :]), reads=[u_bufs], writes=[uxs.b])
            wdw = lambda m, j: vecs.t[:, V_WDW + m * 31 + j:V_WDW + m * 31 + j + 1]
            conv_ops = []

            def conv_init(m, sample):
                o_ = acc.t[:, m, TP:NT].rearrange("p (b t) -> p b t", t=4) if sample else acc.t[:, m, 0:TP]
                i_, ib = (uxs.t[:, m, :, 0:4], uxs.b) if sample else (uxp.t[:, m, 0:TP], uxp.b)
                T.op(DVE, lambda: nc.vector.tensor_scalar(out=o_, in0=i_, scalar1=wdw(m, 0), scalar2=vecs.t[:, V_BDW + m:V_BDW + m + 1],
                                                          op0=ALU.mult, op1=ALU.add), reads=[ib, vecs.b], writes=[acc.b])

            def conv_tap(m, j, sample):
                o_ = acc.t[:, m, TP:NT].rearrange("p (b t) -> p b t", t=4) if sample else acc.t[:, m, 0:TP]
                i_, ib = (uxs.t[:, m, :, j:j + 4], uxs.b) if sample else (uxp.t[:, m, j:j + TP], uxp.b)
                T.op(DVE, lambda: nc.vector.scalar_tensor_tensor(out=o_, in0=i_, scalar=wdw(m, j), in1=o_, op0=ALU.mult, op1=ALU.add),
                     reads=[ib, vecs.b, acc.b], writes=[acc.b])

            for m in range(4):
                for sample in (False, True):
                    conv_ops.append((lambda m=m, sample=sample: conv_init(m, sample)))
                for j in range(1, 31):
                    for sample in (False, True):
                        conv_ops.append((lambda m=m, j=j, sample=sample: conv_tap(m, j, sample)))
            P5 = Phase(A, T)
            NE = 4
            maskd = P5.sb("maskd", [128, 4 * 512], F32)
            T.dma(SP, lambda: nc.sync.dma_start(out=maskd.t[:], in_=cst_d[:, C_MASKD:C_MASKD + 2048]), writes=[maskd.b])
            Eb = [P5.sb(f"E{i}", [128, 512], F32) for i in range(NE)]
            SPb = [P5.sb(f"SP{i}", [128, 512], F32) for i in range(NE)]
            Xb = [P5.sb(f"X{i}", [128, 512], F32) for i in range(2)]
            ab = [P5.sb(f"a{i}", [128, 512], BF16) for i in range(2)]
            zbanks = banks[0:3]
            tiles = []
            for hp in range(0, NH, 2):
                for qg in range(2):
                    lists = []
                    for s in range(2):
                        kbs = [8 + i for i in range(4 * qg + 3, -1, -1)] + list(range(7, -1, -1))
                        lists.append([(hp + s, qg, kb, s) for kb in kbs])
                    for a_, b_ in zip(*lists):
                        tiles.append(a_)
                        tiles.append(b_)
            state = {}
            for ti, (h, qg, kb, s) in enumerate(tiles):
                first = kb == 8 + 4 * qg + 3
                state[ti] = dict(h=h, qg=qg, kb=kb, s=s, first=first, last=(kb == 0), z=zbanks[ti % 3], E=Eb[ti % NE], SP=SPb[ti % NE],
                                 X=Xb[ti % 2], a=ab[ti % 2], C=banks[3 + s], O=banks[5 + s], prevSP=(SPb[(ti - 2) % NE] if not first else None))

            def stA(d):
                T.op(PE, lambda: nc.tensor.matmul(d["z"].t[:], lhsT=kT.t[:, d["h"], d["kb"] * 128:(d["kb"] + 1) * 128],
                                                  rhs=qT.t[:, d["h"], d["qg"] * 512:(d["qg"] + 1) * 512], start=True, stop=True),
                     reads=[kT.b, qT.b], writes=[d["z"].b])

            def stB(d):
                h, kb, qg = d["h"], d["kb"], d["qg"]
                if kb >= 8:
                    T.op(ACT, lambda: nc.scalar.activation(out=d["E"].t[:], in_=d["z"].t[:], func=AF.Exp, scale=SCALE, bias=bsb.t[:, h:h + 1]),
                         reads=[d["z"].b, bsb.b], writes=[d["E"].b])
                    r = kb - 8 - 4 * qg
                    if r >= 0:
                        T.op(DVE, lambda: nc.vector.tensor_tensor(out=d["E"].t[:], in0=d["E"].t[:], in1=maskd.t[:, r * 512:(r + 1) * 512],
                                                                  op=ALU.mult), reads=[d["E"].b, maskd.b], writes=[d["E"].b])
                else:
                    T.op(ACT, lambda: nc.scalar.activation(out=d["E"].t[:], in_=d["z"].t[:], func=AF.Exp, scale=sm.t[:, 0:1], bias=sm.t[:, 2 + h:3 + h]),
                         reads=[d["z"].b, sm.b], writes=[d["E"].b])
                T.op(ACT, lambda: nc.scalar.activation(out=d["SP"].t[:], in_=d["E"].t[:], func=AF.Ln, bias=1.0), reads=[d["E"].b], writes=[d["SP"].b])
                if not d["first"]:
                    T.op(PE, lambda: nc.tensor.matmul(d["C"].t[:], lhsT=lstr, rhs=d["prevSP"].t[:], start=False, stop=False, skip_group_check=True),
                         reads=[cst.b, d["prevSP"].b], writes=[d["C"].b])
                T.op(PE, lambda: nc.tensor.matmul(d["C"].t[:], lhsT=tinc, rhs=d["SP"].t[:], start=d["first"], stop=True, skip_group_check=True),
                     reads=[cst.b, d["SP"].b], writes=[d["C"].b])

            def stC(d):
                h, kb, qg = d["h"], d["kb"], d["qg"]
                T.op(ACT, lambda: nc.scalar.activation(out=d["X"].t[:], in_=d["C"].t[:], func=AF.Exp, scale=-1.0), reads=[d["C"].b], writes=[d["X"].b])
                T.op(DVE, lambda: nc.vector.tensor_tensor(out=d["a"].t[:], in0=d["E"].t[:], in1=d["X"].t[:], op=ALU.mult),
                     reads=[d["E"].b, d["X"].b], writes=[d["a"].b])
                T.op(PE, lambda: nc.tensor.matmul(d["O"].t[:], lhsT=Vtok.t[:, kb, h * 128:(h + 1) * 128], rhs=d["a"].t[:],
                                                  start=d["first"], stop=d["last"], skip_group_check=True),
                     reads=[Vtok.b, d["a"].b], writes=[d["O"].b])
                if d["last"]:
                    T.op(ACT, lambda: nc.scalar.copy(out=oT_sb.t[:, h, qg * 512:(qg + 1) * 512], in_=d["O"].t[:]), reads=[d["O"].b], writes=[oT_sb.b])

            n_t = len(tiles)
            ci = 0
            for step in range(n_t + 2):
                if step < n_t:
                    stA(state[step])
                if 0 <= step - 1 < n_t:
                    stB(state[step - 1])
                if 0 <= step - 2 < n_t:
                    stC(state[step - 2])
                left = n_t + 2 - step
                for _ in range((len(conv_ops) - ci + left - 1) // left):
                    conv_ops[ci]()
                    ci += 1
            while ci < len(conv_ops):
                conv_ops[ci]()
                ci += 1
            P5.close()
            L_kv.close()
            rbc = LC.sb("rbc", [128, 512], F32)
            tmp = [LC.sb(f"lntmp{i}", [128, 512], F32) for i in range(2)]
            for gi, (t0, n) in enumerate(GROUPS):
                b1, b2 = banks[6], banks[7]
                for m in range(4):
                    T.op(PE, lambda: nc.tensor.matmul(b1.t[:, 0:n], lhsT=onesf, rhs=acc.t[:, m, t0:t0 + n], start=(m == 0), stop=(m == 3)),
                         reads=[cst.b, acc.b], writes=[b1.b])
                for m in range(4):
                    a_ = acc.t[:, m, t0:t0 + n]
                    T.op(DVE, lambda: nc.vector.scalar_tensor_tensor(out=a_, in0=b1.t[:, 0:n], scalar=-1.0 / CW, in1=a_, op0=ALU.mult, op1=ALU.add),
                         reads=[b1.b, acc.b], writes=[acc.b])
                    t_ = tmp[m % 2]
                    T.op(ACT, lambda: nc.scalar.activation(out=t_.t[:, 0:n], in_=a_, func=AF.Square), reads=[acc.b], writes=[t_.b])
                    T.op(PE, lambda: nc.tensor.matmul(b2.t[:, 0:n], lhsT=onesf, rhs=t_.t[:, 0:n], start=(m == 0), stop=(m == 3)),
                         reads=[cst.b, t_.b], writes=[b2.b])
                rstd_from(b2, n, 1.0 / CW, rbc)
                for m in range(4):
                    a_ = acc.t[:, m, t0:t0 + n]
                    T.op(DVE, lambda: nc.vector.tensor_tensor(out=a_, in0=a_, in1=rbc.t[:, 0:n], op=ALU.mult), reads=[acc.b, rbc.b], writes=[acc.b])
                    T.op(ACT, lambda: nc.scalar.activation(out=cT.t[:, m, t0:t0 + n], in_=a_, func=AF.Silu,
                                                           scale=vecs.t[:, V_GCLN + m:V_GCLN + m + 1], bias=vecs.t[:, V_BCLN + m:V_BCLN + m + 1]),
                         reads=[acc.b, vecs.b], writes=[cT.b])
            LC.close()
            dump(T, "cT", cT, "p k t -> p (k t)")

            checkpoint("A1")
            P6 = Phase(A, T)
            Vb = P6.sb("Vb", [128, NPG, NH * HD], BF16)
            KT = P6.sb("KT", [128, NPG, NH, 128], BF16)
            biasN = P6.sb("biasN", [128, 512], F32)
            T.op(DVE, lambda: nc.vector.tensor_copy(out=biasN.t[:].rearrange("p (h x) -> p h x", h=NH),
                                                    in_=bsb.t[:].unsqueeze(2).to_broadcast([128, NH, 64])), reads=[bsb.b], writes=[biasN.b])
            biasP = P6.sb("biasP", [128, 512], F32)
            T.op(DVE, lambda: nc.vector.tensor_copy(out=biasP.t[:].rearrange("p (g h t) -> p g h t", g=NPG, h=NH),
                                                    in_=bsb.t[:].unsqueeze(1).unsqueeze(3).to_broadcast([128, NPG, NH, 4])),
                 reads=[bsb.b], writes=[biasP.b])
            P6n = Phase(A, T)
            En = P6n.sb("En", [128, 512], F32)
            SPn = P6n.sb("SPn", [128, 512], F32)
            Xn = P6n.sb("Xn", [128, 512], F32)
            an = P6n.sb("an", [128, 512], BF16)
            m64t = P6n.sb("m64", [128, 512], F32)
            T.dma(SP, lambda: nc.sync.dma_start(out=m64t.t[:], in_=cst_d[:, C_M64:C_M64 + 512]), writes=[m64t.b])
            m64 = m64t.t[0:TS, :]
            XN = P6.sb("XN", [128, 512], F32)
            Onew = P6.sb("Onew", [128, 512], F32)
            bz = banks[0]
            for h in range(NH):
                T.op(PE, lambda: nc.tensor.matmul(bz.t[0:TS, h * 64:(h + 1) * 64], lhsT=kTs.t[:, h, :], rhs=qT.t[:, h, TP:NT], start=True, stop=True),
                     reads=[kTs.b, qT.b], writes=[bz.b])
            T.op(DVE, lambda: nc.vector.scalar_tensor_tensor(out=En.t[0:TS, :], in0=bz.t[0:TS, :], scalar=SCALE, in1=biasN.t[0:TS, :],
                                                             op0=ALU.mult, op1=ALU.add), reads=[bz.b, biasN.b], writes=[En.b])
            T.op(ACT, lambda: nc.scalar.activation(out=En.t[0:TS, :], in_=En.t[0:TS, :], func=AF.Exp), reads=[En.b], writes=[En.b])
            T.op(DVE, lambda: nc.vector.tensor_tensor(out=En.t[0:TS, :], in0=En.t[0:TS, :], in1=m64, op=ALU.mult), reads=[En.b, m64t.b], writes=[En.b])
            T.op(ACT, lambda: nc.scalar.activation(out=SPn.t[0:TS, :], in_=En.t[0:TS, :], func=AF.Ln, bias=1.0), reads=[En.b], writes=[SPn.b])
            b1, b2, b3 = banks[1], banks[2], banks[3]
            T.op(PE, lambda: nc.tensor.matmul(b1.t[:, :], lhsT=onesf[0:TS, :], rhs=SPn.t[0:TS, :], start=True, stop=True),
                 reads=[cst.b, SPn.b], writes=[b1.b])
            T.op(ACT, lambda: nc.scalar.activation(out=XN.t[:], in_=b1.t[:], func=AF.Exp, scale=-1.0), reads=[b1.b], writes=[XN.b])
            T.op(PE, lambda: nc.tensor.matmul(b2.t[0:TS, :], lhsT=tinc[0:TS, 0:TS], rhs=SPn.t[0:TS, :], start=True, stop=True),
                 reads=[cst.b, SPn.b], writes=[b2.b])
            T.op(ACT, lambda: nc.scalar.activation(out=Xn.t[0:TS, :], in_=b2.t[0:TS, :], func=AF.Exp, scale=-1.0), reads=[b2.b], writes=[Xn.b])
            T.op(DVE, lambda: nc.vector.tensor_tensor(out=an.t[0:TS, :], in0=En.t[0:TS, :], in1=Xn.t[0:TS, :], op=ALU.mult),
                 reads=[En.b, Xn.b], writes=[an.b])
            for h in range(NH):
                T.op(PE, lambda: nc.tensor.matmul(b3.t[:, h * 64:(h + 1) * 64], lhsT=Vs.t[0:TS, h * 128:(h + 1) * 128], rhs=an.t[0:TS, h * 64:(h + 1) * 64],
                                                  start=True, stop=True), reads=[Vs.b, an.b], writes=[b3.b])
            T.op(DVE, lambda: nc.vector.tensor_copy(out=Onew.t[:], in_=b3.t[:]), reads=[b3.b], writes=[Onew.b])
            P6n.close()
            NKS, NVS = 6, 6
            Kst = [P6.sb(f"Kst{i}", [128, NH * HD], F32) for i in range(NKS)]
            Vst = [P6.sb(f"Vst{i}", [128, NH * HD], F32) for i in range(NVS)]
            zb_ = P6.sb("zb", [128, 512], F32)
            Es = P6.sb("Es", [128, 512], F32)
            SPs = P6.sb("SPs", [128, 512], F32)
            Xs = P6.sb("Xs", [128, 512], F32)
            a1 = P6.sb("a1", [128, 512], F32)
            as_ = P6.sb("as", [128, 512], BF16)
            kcnt = vcnt = 0
            ev = 0
            KTb = [P6.buf(f"KTpg{i}") for i in range(NPG)]

            def z_page(b, pg):
                bz = banks[0]
                for h in range(NH):
                    c0 = (pg * NH + h) * 4
                    T.op(PE, lambda: nc.tensor.matmul(bz.t[:, c0:c0 + 4], lhsT=KT.t[:, pg, h, :], rhs=qT.t[:, h, TP + 4 * b:TP + 4 * b + 4],
                                                      start=True, stop=True), reads=[KTb[pg], qT.b], writes=[bz.b])

            for b in range(NSAMP):
                for pg in range(NPG):
                    col = b * NPG + pg
                    ks_ = Kst[kcnt % NKS]
                    kcnt += 1
                    T.dma(POOL, lambda: nc.gpsimd.indirect_dma_start(out=ks_.t[:, :], out_offset=None, in_=poolk[:, :],
                                                                     in_offset=bass.IndirectOffsetOnAxis(ap=idx.t[:, col:col + 1], axis=0)),
                          reads=[idx.b], writes=[ks_.b])
                    vs_ = Vst[vcnt % NVS]
                    vcnt += 1
                    T.dma(POOL, lambda: nc.gpsimd.indirect_dma_start(out=vs_.t[:, :], out_offset=None, in_=poolv[:, :],
                                                                     in_offset=bass.IndirectOffsetOnAxis(ap=idx.t[:, col:col + 1], axis=0)),
                          reads=[idx.b], writes=[vs_.b])
                    for hb in range(2):
                        bk = banks[4 + ev % 4]
                        for hh in range(4):
                            h = hb * 4 + hh
                            T.op(PE, lambda: nc.tensor.transpose(out=bk.t[:, hh * 128:(hh + 1) * 128], in_=ks_.t[:, h * 128:(h + 1) * 128], identity=identf),
                                 reads=[ks_.b, cst.b], writes=[bk.b])
                        o_ap = KT.t[:, pg, hb * 4:hb * 4 + 4, :]
                        i_ap = bk.t[:, :].rearrange("p (h s) -> p h s", h=4)
                        if ev % 2 == 0:
                            T.op(ACT, lambda: nc.scalar.copy(out=o_ap, in_=i_ap), reads=[bk.b], writes=[KTb[pg]])
                        else:
                            T.op(DVE, lambda: nc.vector.tensor_copy(out=o_ap, in_=i_ap), reads=[bk.b], writes=[KTb[pg]])
                        ev += 1
                    if pg >= 1:
                        z_page(b, pg - 1)
                    if pg % 2 == 0:
                        T.op(DVE, lambda: nc.vector.tensor_copy(out=Vb.t[:, pg, :], in_=vs_.t[:, :]), reads=[vs_.b], writes=[Vb.b])
                    else:
                        T.op(ACT, lambda: nc.scalar.copy(out=Vb.t[:, pg, :], in_=vs_.t[:, :]), reads=[vs_.b], writes=[Vb.b])
                bz, bc, bo = banks[0], banks[1], banks[2 + b % 2]
                z_page(b, NPG - 1)
                T.op(DVE, lambda: nc.vector.scalar_tensor_tensor(out=zb_.t[:], in0=bz.t[:], scalar=SCALE, in1=biasP.t[:], op0=ALU.mult, op1=ALU.add),
                     reads=[bz.b, biasP.b], writes=[zb_.b])
                T.op(ACT, lambda: nc.scalar.activation(out=Es.t[:], in_=zb_.t[:], func=AF.Exp), reads=[zb_.b], writes=[Es.b])
                T.op(ACT, lambda: nc.scalar.activation(out=SPs.t[:], in_=Es.t[:], func=AF.Ln, bias=1.0), reads=[Es.b], writes=[SPs.b])
                T.op(PE, lambda: nc.tensor.matmul(bc.t[:], lhsT=tinc, rhs=SPs.t[:], start=True, stop=False, skip_group_check=True),
                     reads=[cst.b, SPs.b], writes=[bc.b])
                for k in range(1, NPG):
                    w_ = 32 * (NPG - k)
                    T.op(PE, lambda: nc.tensor.matmul(bc.t[:, 0:w_], lhsT=onesf, rhs=SPs.t[:, 32 * k:512], start=False, stop=(k == NPG - 1),
                                                      skip_group_check=True), reads=[cst.b, SPs.b], writes=[bc.b])
                T.op(ACT, lambda: nc.scalar.activation(out=Xs.t[:], in_=bc.t[:], func=AF.Exp, scale=-1.0), reads=[bc.b], writes=[Xs.b])
                T.op(DVE, lambda: nc.vector.tensor_tensor(out=a1.t[:], in0=Es.t[:], in1=Xs.t[:], op=ALU.mult), reads=[Es.b, Xs.b], writes=[a1.b])
                xn_b = XN.t[:].rearrange("p (h b t) -> p h b t", h=NH, b=NSAMP)[:, :, b, :].unsqueeze(1).to_broadcast([128, NPG, NH, 4])
                T.op(DVE, lambda: nc.vector.tensor_tensor(out=as_.t[:].rearrange("p (g h t) -> p g h t", g=NPG, h=NH),
                                                          in0=a1.t[:].rearrange("p (g h t) -> p g h t", g=NPG, h=NH), in1=xn_b, op=ALU.mult),
                     reads=[a1.b, XN.b], writes=[as_.b])
                for h in range(NH):
                    for pg in range(NPG):
                        c0 = (pg * NH + h) * 4
                        T.op(PE, lambda: nc.tensor.matmul(bo.t[:, h * 4:h * 4 + 4], lhsT=Vb.t[:, pg, h * 128:(h + 1) * 128], rhs=as_.t[:, c0:c0 + 4],
                                                          start=(pg == 0), stop=(pg == NPG - 1), skip_group_check=True),
                             reads=[Vb.b, as_.b], writes=[bo.b])
                T.op(DVE, lambda: nc.vector.tensor_tensor(out=oT_sb.t[:, :, TP + 4 * b:TP + 4 * b + 4],
                                                          in0=bo.t[:, 0:32].rearrange("p (h t) -> p h t", h=NH),
                                                          in1=Onew.t[:].rearrange("p (h b t) -> p h b t", h=NH, b=NSAMP)[:, :, b, :], op=ALU.add),
                     reads=[bo.b, Onew.b], writes=[oT_sb.b])
            P6.close()
            L_q.close()

            checkpoint("A2")
            P7 = Phase(A, T)
            hTm = P7.sb("hTm", [128, KD, ML], BF16)
            PN = Phase(A, T)
            norm_tiles(PN, mem, ML, 2, hTm, 0, "m")
            PN.close()
            WS = WStream(P7, 2, KD)
            WS.add(w_mkv, 0, KD, 0)
            WS.add(w_mkv, 0, KD, 512)
            mkT = P7.sb("mkT", [128, MH, ML], BF16)
            mvt = P7.sb("mvt", [128, 2, MH * HD], BF16)
            wk_sq = [P7.sb(f"sq7{i}", [128, 512], F32) for i in range(2)]
            wk_rb = [P7.sb(f"rb7{i}", [128, 512], F32) for i in range(2)]
            kn = [P7.sb(f"kn7{i}", [128, ML], F32) for i in range(2)]
            mst = [P7.sb(f"mst{i}", [128, 512], F32) for i in range(2)]
            wt = WS.get()
            it = 0
            for h in range(MH):
                bk, bk2, bk3 = banks[it % 2], banks[2 + it % 2], banks[4 + it % 2]
                k_ = kn[it % 2]
                mm_fm(bk, ML, 0, wt, KD, h, hTm, 0)
                headnorm(bk, ML, V_GKM, wk_sq[it % 2], wk_rb[it % 2], bk2, k_.t[:, :], k_.b)
                T.op(ACT, lambda: nc.scalar.copy(out=mkT.t[:, h, :], in_=k_.t[:, :]), reads=[k_.b], writes=[mkT.b])
                s_ = mst[it % 2]
                for j in range(2):
                    T.op(PE, lambda: nc.tensor.transpose(out=bk3.t[:, j * 128:(j + 1) * 128], in_=k_.t[:, j * 128:(j + 1) * 128], identity=identf),
                         reads=[k_.b, cst.b], writes=[bk3.b])
                T.op(DVE, lambda: nc.vector.tensor_copy(out=s_.t[:, 0:256], in_=bk3.t[:, 0:256]), reads=[bk3.b], writes=[s_.b])
                T.dma(SP, lambda: nc.sync.dma_start(out=memk_o[:, h * 128:(h + 1) * 128].rearrange("(j p) d -> p j d", p=128),
                                                    in_=s_.t[:, 0:256].rearrange("p (j d) -> p j d", j=2)), reads=[s_.b])
                it += 1
            wt = WS.get()
            for tile in range(2):
                bk = banks[it % 2]
                s_ = mst[it % 2]
                mm_tm(bk, 128, wt, KD, hTm, tile * 128)
                T.op(ACT, lambda: nc.scalar.copy(out=s_.t[:, :], in_=bk.t[:, :]), reads=[bk.b], writes=[s_.b])
                T.op(DVE, lambda: nc.vector.tensor_copy(out=mvt.t[:, tile, :], in_=bk.t[:, :]), reads=[bk.b], writes=[mvt.b])
                T.dma(SP, lambda: nc.sync.dma_start(out=memv_o[tile * 128:(tile + 1) * 128, :], in_=s_.t[:, :]), reads=[s_.b])
                it += 1
            Pb = [P7.sb(f"Pm{i}", [128, 512], BF16) for i in range(4)]
            rd = P7.sb("rden", [128, 512], F32)
            it = 0
            for h in range(MH):
                for qg in range(2):
                    bd, bo = banks[4 + it % 2], banks[6 + it % 2]
                    for mb in range(2):
                        bs = banks[it % 4]
                        p_ = Pb[it % 4]
                        T.op(PE, lambda: nc.tensor.matmul(bs.t[:], lhsT=mkT.t[:, h, mb * 128:(mb + 1) * 128], rhs=qmT.t[:, h, qg * 512:(qg + 1) * 512],
                                                          start=True, stop=True), reads=[mkT.b, qmT.b], writes=[bs.b])
                        T.op(ACT, lambda: nc.scalar.activation(out=p_.t[:], in_=bs.t[:], func=AF.Exp, scale=SCALE), reads=[bs.b], writes=[p_.b])
                        T.op(PE, lambda: nc.tensor.matmul(bd.t[:], lhsT=onesb, rhs=p_.t[:], start=(mb == 0), stop=(mb == 1)),
                             reads=[cb.b, p_.b], writes=[bd.b])
                        T.op(PE, lambda: nc.tensor.matmul(bo.t[:], lhsT=mvt.t[:, mb, h * 128:(h + 1) * 128], rhs=p_.t[:], start=(mb == 0), stop=(mb == 1)),
                             reads=[mvt.b, p_.b], writes=[bo.b])
                        it += 1
                    T.op(DVE, lambda: nc.vector.reciprocal(out=rd.t[:], in_=bd.t[:]), reads=[bd.b], writes=[rd.b])
                    T.op(DVE, lambda: nc.vector.tensor_tensor(out=oT_mem.t[:, h, qg * 512:(qg + 1) * 512], in0=bo.t[:], in1=rd.t[:], op=ALU.mult),
                         reads=[bo.b, rd.b], writes=[oT_mem.b])
            HS = NSAMP // 2
            cmkb = P7.sb("cmkb", [128, HS * 2, MH * HD], BF16)
            cmvb = P7.sb("cmvb", [128, HS * 2, MH * HD], BF16)
            mkTs = P7.sb("mkTs", [128, HS * 2, MH, 128], BF16)
            Ps = P7.sb("Ps", [128, 256], BF16)
            rds = P7.sb("rds", [128, 128], F32)
            ev = 0
            for hf in range(2):
                r0 = hf * HS * ML
                T.dma(POOL, lambda: nc.gpsimd.dma_start(out=cmkb.t[:], in_=cmk[r0:r0 + HS * ML, :].rearrange("(x p) c -> p x c", p=128)), writes=[cmkb.b])
                T.dma(POOL, lambda: nc.gpsimd.dma_start(out=cmvb.t[:], in_=cmv[r0:r0 + HS * ML, :].rearrange("(x p) c -> p x c", p=128)), writes=[cmvb.b])
                for x in range(HS * 2):
                    bk = banks[ev % 2]
                    bkb = bk.t[:].bitcast(BF16)
                    for h in range(MH):
                        T.op(PE, lambda: nc.tensor.transpose(out=bkb[:, h * 128:(h + 1) * 128], in_=cmkb.t[:, x, h * 128:(h + 1) * 128], identity=identb),
                             reads=[cmkb.b, cb.b], writes=[bk.b])
                    i_ap = bkb[:, 0:512].rearrange("p (h s) -> p h s", h=4)
                    if ev % 2 == 0:
                        T.op(ACT, lambda: nc.scalar.copy(out=mkTs.t[:, x, :, :], in_=i_ap), reads=[bk.b], writes=[mkTs.b])
                    else:
                        T.op(DVE, lambda: nc.vector.tensor_copy(out=mkTs.t[:, x, :, :], in_=i_ap), reads=[bk.b], writes=[mkTs.b])
                    ev += 1
                bs, bd, bo = banks[2], banks[3], banks[4]
                for bb in range(HS):
                    b = hf * HS + bb
                    for mb in range(2):
                        for h in range(MH):
                            c0 = ((bb * 2 + mb) * MH + h) * 4
                            T.op(PE, lambda: nc.tensor.matmul(bs.t[:, c0:c0 + 4], lhsT=mkTs.t[:, bb * 2 + mb, h, :], rhs=qmT.t[:, h, TP + 4 * b:TP + 4 * b + 4],
                                                              start=True, stop=True), reads=[mkTs.b, qmT.b], writes=[bs.b])
                T.op(ACT, lambda: nc.scalar.activation(out=Ps.t[:], in_=bs.t[:, 0:256], func=AF.Exp, scale=SCALE), reads=[bs.b], writes=[Ps.b])
                p4 = Ps.t[:].rearrange("p (b m x) -> p b m x", b=HS, m=2)
                for mb in range(2):
                    T.op(PE, lambda: nc.tensor.matmul(bd.t[:, 0:128].rearrange("p (b x) -> p b x", b=HS), lhsT=onesb, rhs=p4[:, :, mb, :],
                                                      start=(mb == 0), stop=(mb == 1)), reads=[cb.b, Ps.b], writes=[bd.b])
                for bb in range(HS):
                    for h in range(MH):
                        for mb in range(2):
                            c0 = ((bb * 2 + mb) * MH + h) * 4
                            o0 = (bb * MH + h) * 4
                            T.op(PE, lambda: nc.tensor.matmul(bo.t[:, o0:o0 + 4], lhsT=cmvb.t[:, bb * 2 + mb, h * 128:(h + 1) * 128], rhs=Ps.t[:, c0:c0 + 4],
                                                              start=(mb == 0), stop=(mb == 1), skip_group_check=True),
                                 reads=[cmvb.b, Ps.b], writes=[bo.b])
                T.op(DVE, lambda: nc.vector.reciprocal(out=rds.t[:], in_=bd.t[:, 0:128]), reads=[bd.b], writes=[rds.b])
                T.op(DVE, lambda: nc.vector.tensor_tensor(
                    out=oT_mem.t[:, :, TP + hf * HS * 4:TP + (hf + 1) * HS * 4].rearrange("p h (b t) -> p h b t", t=4),
                    in0=bo.t[:, 0:128].rearrange("p (b h t) -> p h b t", b=HS, h=MH),
                    in1=rds.t[:].rearrange("p (b h t) -> p h b t", b=HS, h=MH), op=ALU.mult), reads=[bo.b, rds.b], writes=[oT_mem.b])
            P7.close()
            L_qm.close()
            dump(T, "oT_sb", oT_sb, "p k t -> p (k t)")
            dump(T, "oT_mem", oT_mem, "p k t -> p (k t)")

            checkpoint("A3")
            L_hT = Phase(A, T)
            hT = L_hT.sb("hT", [128, KD, NTP], BF16)
            T.dma(SP, lambda: nc.sync.dma_start(out=hT.t[:].rearrange("p k t -> p (k t)"), in_=hT_scr[:, :]), reads=[hT_buf], writes=[hT.b])
            L_mrg = Phase(A, T)
            mrg = L_mrg.sb("merged", [128, KD, NTP], BF16)
            T.op(DVE, lambda: nc.vector.memset(mrg.t[:, :, NT:NTP], 0.0), writes=[mrg.b])
            P8 = Phase(A, T)
            WS = WStream(P8, 3, KD)
            for cg in range(4):
                WS.add(w_co, 0, 4, cg * 512)
                WS.add(w_in, 0, KD, 4608 + cg * 512)
                WS.add(w_so, 0, 8, cg * 512)
                WS.add(w_in, 0, KD, 4608 + 2048 + cg * 512)
                WS.add(w_mo, 0, 4, cg * 512)
                WS.add(w_in, 0, KD, 4608 + 4096 + cg * 512)
            sgb = [P8.sb(f"sg{i}", [128, 512], F32) for i in range(2)]
            macc = [[P8.sb(f"macc{mm}_{gi}", [128, 512], F32) for gi in range(3)] for mm in range(4)]
            it = 0
            for cg in range(4):
                for br, (src, kc, boff) in enumerate(((cT, 4, 0), (oT_sb, 8, 16), (oT_mem, 4, 32))):
                    wy = WS.get()
                    wg = WS.get(hold=1)
                    for mm in range(4):
                        m = cg * 4 + mm
                        for gi, (t0, n) in enumerate(GROUPS):
                            by, bg = banks[(2 * it) % 8], banks[(2 * it + 1) % 8]
                            s_ = sgb[it % 2]
                            mm_fm(by, n, 0, wy, kc, mm, src, t0)
                            mm_fm(bg, n, 0, wg, KD, mm, hT, t0)
                            T.op(ACT, lambda: nc.scalar.activation(out=s_.t[:, 0:n], in_=bg.t[:, 0:n], func=AF.Sigmoid,
                                                                   bias=vecs.t[:, V_BGATE + boff + m:V_BGATE + boff + m + 1]),
                                 reads=[bg.b, vecs.b], writes=[s_.b])
                            a_ = macc[mm][gi]
                            if br == 0:
                                T.op(DVE, lambda: nc.vector.tensor_tensor(out=a_.t[:, 0:n], in0=by.t[:, 0:n], in1=s_.t[:, 0:n], op=ALU.mult),
                                     reads=[by.b, s_.b], writes=[a_.b])
                            else:
                                T.op(DVE, lambda: nc.vector.tensor_tensor(out=s_.t[:, 0:n], in0=by.t[:, 0:n], in1=s_.t[:, 0:n], op=ALU.mult),
                                     reads=[by.b, s_.b], writes=[s_.b])
                                if br == 1:
                                    T.op(DVE, lambda: nc.vector.tensor_tensor(out=a_.t[:, 0:n], in0=a_.t[:, 0:n], in1=s_.t[:, 0:n], op=ALU.add),
                                         reads=[a_.b, s_.b], writes=[a_.b])
                                else:
                                    T.op(DVE, lambda: nc.vector.tensor_tensor(out=mrg.t[:, m, t0:t0 + n], in0=a_.t[:, 0:n], in1=s_.t[:, 0:n], op=ALU.add),
                                         reads=[a_.b, s_.b], writes=[mrg.b])
                            it += 1
            P8.close()
            dump(T, "mrg", mrg, "p k t -> p (k t)")
            L_hT.close()
            L_cT.close()
            L_o.close()

            checkpoint("M")
            L_x1 = Phase(A, T)
            x1 = L_x1.sb("x1", [128, 9, D], F32)
            x1b = [L_x1.buf(f"x1_{i}") for i in range(9)]
            P9 = Phase(A, T)
            WS = WStream(P9, 2, KD)
            for cg in range(4):
                WS.add(w_o, 0, KD, cg * 512)
            xr = [P9.sb(f"xr{i}", [128, 512], F32) for i in range(3)]
            it = 0
            for cg in range(4):
                wt = WS.get()
                for tile in range(9):
                    rows = 128 if tile < 8 else TS
                    bk = banks[it % 4]
                    x_ = xr[it % 3]
                    xsrc = x_own[tile * 128:(tile + 1) * 128, cg * 512:(cg + 1) * 512] if tile < 8 else x_s[:, cg * 512:(cg + 1) * 512]
                    T.dma(SP, lambda: nc.sync.dma_start(out=x_.t[0:rows, :], in_=xsrc), writes=[x_.b])
                    mm_tm(bk, rows, wt, KD, mrg, tile * 128)
                    T.op(DVE, lambda: nc.vector.tensor_tensor(out=x1.t[0:rows, tile, cg * 512:(cg + 1) * 512], in0=bk.t[0:rows, :], in1=x_.t[0:rows, :], op=ALU.add),
                         reads=[bk.b, x_.b], writes=[x1b[tile]])
                    it += 1
            P9.close()
            dump(T, "x1", x1, "p k t -> p (k t)", F32, reads=x1b)
            L_mrg.close()

            checkpoint("O")
            P10 = Phase(A, T)
            h2T = P10.sb("h2T", [128, KD, NT], BF16)
            PN = Phase(A, T)
            gbc = PN.sb("gbc2", [128, D], F32)
            T.dma(SP, lambda: nc.sync.dma_start(out=gbc.t[:], in_=grow[1:2, :].partition_broadcast(128)), writes=[gbc.b])
            xn2 = [PN.sb(f"xn2{i}", [128, D], BF16) for i in range(2)]
            junk = PN.sb("junk2", [128, D], BF16)
            ssr = [PN.sb(f"ss2{i}", [128, 2], F32) for i in range(2)]
            for tile in range(9):
                rows = 128 if tile < 8 else TS
                n_, s_ = xn2[tile % 2], ssr[tile % 2]
                T.op(ACT, lambda: nc.scalar.activation(out=junk.t[0:rows, :], in_=x1.t[0:rows, tile, :], func=AF.Square, accum_out=s_.t[0:rows, 0:1]),
                     reads=[x1b[tile]], writes=[junk.b, s_.b])
                T.op(ACT, lambda: nc.scalar.activation(out=s_.t[0:rows, 1:2], in_=s_.t[0:rows, 0:1], func=AF.Ln, scale=1.0 / D, bias=EPS), reads=[s_.b], writes=[s_.b])
                T.op(ACT, lambda: nc.scalar.activation(out=s_.t[0:rows, 1:2], in_=s_.t[0:rows, 1:2], func=AF.Exp, scale=-0.5), reads=[s_.b], writes=[s_.b])
                T.op(DVE, lambda: nc.vector.scalar_tensor_tensor(out=n_.t[0:rows, :], in0=x1.t[0:rows, tile, :], scalar=s_.t[0:rows, 1:2], in1=gbc.t[0:rows, :],
                                                                 op0=ALU.mult, op1=ALU.mult), reads=[x1b[tile], s_.b, gbc.b], writes=[n_.b])
                for half in range(2):
                    bk = banks[(2 * tile + half) % 4]
                    bkb = bk.t[:].bitcast(BF16)
                    for kk in range(8):
                        k = half * 8 + kk
                        T.op(PE, lambda: nc.tensor.transpose(out=bkb[:, kk * 128:kk * 128 + rows], in_=n_.t[0:rows, k * 128:(k + 1) * 128],
                                                             identity=identb[0:rows, 0:rows]), reads=[n_.b, cb.b], writes=[bk.b])
                    src = bkb.rearrange("p (k t) -> p k t", k=8)[:, :, 0:rows]
                    d_ap = h2T.t[:, half * 8:half * 8 + 8, tile * 128:tile * 128 + rows]
                    if half == 0:
                        T.op(ACT, lambda: nc.scalar.copy(out=d_ap, in_=src), reads=[bk.b], writes=[h2T.b])
                    else:
                        T.op(DVE, lambda: nc.vector.tensor_copy(out=d_ap, in_=src), reads=[bk.b], writes=[h2T.b])
            PN.close()
            WS = WStream(P10, 3, KD)
            for e in range(8):
                WS.add(w_up, 0, KD, e * 1024)
                WS.add(w_up, 0, KD, e * 1024 + 512)
                for cg in range(4):
                    WS.add(w_dn, e * 1024, 8, cg * 512)
            ffq = P10.sb("ffq", [128, 8, NTP], BF16)
            T.op(DVE, lambda: nc.vector.memset(ffq.t[:, :, NT:NTP], 0.0), writes=[ffq.b])
            sqf = [P10.sb(f"sqf{i}", [128, 512], F32) for i in range(2)]
            it = 0
            for e in range(8):
                for ub in range(2):
                    wt = WS.get()
                    for mm in range(4):
                        for gi, (t0, n) in enumerate(GROUPS):
                            bk = banks[it % 4]
                            s_ = sqf[it % 2]
                            mm_fm(bk, n, 0, wt, KD, mm, h2T, t0)
                            T.op(ACT, lambda: nc.scalar.activation(out=s_.t[:, 0:n], in_=bk.t[:, 0:n], func=AF.Square), reads=[bk.b], writes=[s_.b])
                            T.op(DVE, lambda: nc.vector.scalar_tensor_tensor(out=ffq.t[:, ub * 4 + mm, t0:t0 + n], in0=bk.t[:, 0:n], scalar=0.0, in1=s_.t[:, 0:n],
                                                                             op0=ALU.is_gt, op1=ALU.mult), reads=[bk.b, s_.b], writes=[ffq.b])
                            it += 1
                for cg in range(4):
                    wt = WS.get()
                    for tile in range(9):
                        rows = 128 if tile < 8 else TS
                        bk = banks[4 + it % 4]
                        mm_tm(bk, rows, wt, 8, ffq, tile * 128)
                        xs_ = x1.t[0:rows, tile, cg * 512:(cg + 1) * 512]
                        T.op(DVE, lambda: nc.vector.tensor_tensor(out=xs_, in0=bk.t[0:rows, :], in1=xs_, op=ALU.add), reads=[bk.b, x1b[tile]], writes=[x1b[tile]])
                        it += 1
            for tile in range(9):
                if tile < 8:
                    T.dma(SP, lambda: nc.sync.dma_start(out=y_own[tile * 128:(tile + 1) * 128, :], in_=x1.t[:, tile, :]), reads=[x1b[tile]])
                else:
                    T.dma(SP, lambda: nc.sync.dma_start(out=y_s[:, :], in_=x1.t[0:TS, tile, :]), reads=[x1b[tile]])
            P10.close()
            T.finish()
            print(f"[kernel] arena peak {A.peak / 1024:.1f} KiB/partition; instr pe={PE.cnt} dve={DVE.cnt} act={ACT.cnt}")
    except _Stop:
        pass
    return nc


def _consts():
    c = np.zeros((128, C_END), np.float32)
    j = np.arange(128)[:, None]
    s = np.arange(128)[None, :]
    c[:, C_IDENT:C_IDENT + 128] = np.eye(128, dtype=np.float32)
    c[:, C_TINC:C_TINC + 128] = (j >= s)
    c[:, C_LSTR:C_LSTR + 128] = (j < s)
    c[:, C_ONES:C_ONES + 128] = 1.0
    n = np.arange(512)[None, :]
    for r in range(4):
        c[:, C_MASKD + r * 512:C_MASKD + (r + 1) * 512] = (128 * r + j < n)
    rows = np.arange(64)[:, None]
    cols = np.arange(512)[None, :]
    bt = cols % 64
    c[0:64, C_M64:C_M64 + 512] = ((rows // 4) == (bt // 4)) & ((rows % 4) < (bt % 4))
    return c


def _fm(v, k):
    return np.ascontiguousarray(np.asarray(v, np.float32).reshape(k, 128).T)


def make_in_maps(inp, npool=None, core_list=range(8)):
    f = lambda a: np.ascontiguousarray(np.asarray(a, np.float32))
    cst = _consts()
    x_prompt, x_sample, mem_prompt = f(inp["x_prompt"]), f(inp["x_sample"]), f(inp["mem_prompt"])
    poolk = f(inp["cache_sb_k"])[0].reshape(-1, NH * HD)
    poolv = f(inp["cache_sb_v"])[0].reshape(-1, NH * HD)
    pt = np.asarray(inp["page_table"], np.int32)
    sconv = f(inp["state_conv"])[0]
    cmk, cmv = f(inp["cache_mem_k"])[0], f(inp["cache_mem_v"])[0]
    shared = {
        "poolk": poolk, "poolv": poolv,
        "w_in": f(inp["w_in"])[0], "w_co": f(inp["w_conv_out"])[0], "w_so": f(inp["w_sb_out"])[0], "w_mo": f(inp["w_mem_out"])[0],
        "w_o": f(inp["w_o"])[0], "w_up": f(inp["w_up"])[0], "w_dn": f(inp["w_down"])[0], "w_mkv": f(inp["w_mem_kv"])[0],
        "grow": np.ascontiguousarray(np.stack([f(inp["g_mix"])[0], f(inp["g_mlp"])[0], f(inp["g_mem"])[0]])),
        "bsb": f(inp["b_sb"]).reshape(1, NH), "cst": cst,
    }
    vbase = np.zeros((128, V_END), np.float32)
    vbase[:, V_BGATE:V_BGATE + 48] = _fm(f(inp["b_gate"])[0], 48)
    wdw = f(inp["w_dw"])[0]
    vbase[:, V_WDW:V_WDW + 124] = wdw.T.reshape(4, 128, 31).transpose(1, 0, 2).reshape(128, 124)
    vbase[:, V_BDW:V_BDW + 4] = _fm(f(inp["b_dw"])[0], 4)
    vbase[:, V_GCLN:V_GCLN + 4] = _fm(f(inp["g_conv_ln"])[0], 4)
    vbase[:, V_BCLN:V_BCLN + 4] = _fm(f(inp["b_conv_ln"])[0], 4)
    vbase[:, V_GQSB] = f(inp["g_q_sb"])[0]
    vbase[:, V_GKSB] = f(inp["g_k_sb"])[0]
    vbase[:, V_GQM] = f(inp["g_q_mem"])[0]
    vbase[:, V_GKM] = f(inp["g_k_mem"])[0]
    vbase[:, V_ROWID] = np.arange(128, dtype=np.float32)
    maps = []
    for c in core_list:
        b, half = c // 2, c % 2
        v = vbase.copy()
        v[:, V_PM] = float(half)
        m = dict(shared)
        m.update({
            "x_prev": x_prompt[b, 0:TP], "x_own": x_prompt[b, half * TP:(half + 1) * TP],
            "x_s": x_sample[c * NSAMP:(c + 1) * NSAMP].reshape(TS, D), "mem": mem_prompt[b],
            "pt": np.ascontiguousarray(pt[c * NSAMP:(c + 1) * NSAMP].reshape(1, NSAMP * NPG)),
            "sconv": sconv[c * NSAMP:(c + 1) * NSAMP].reshape(NSAMP * 30, CW),
            "cmk": cmk[c * NSAMP:(c + 1) * NSAMP].reshape(NSAMP * ML, MH * HD),
            "cmv": cmv[c * NSAMP:(c + 1) * NSAMP].reshape(NSAMP * ML, MH * HD),
            "vecs": v,
        })
        maps.append(m)
    return maps


def assemble(res):
    B = 4
    yp = np.zeros((B, 2 * TP, D), np.float32)
    ys = np.zeros((128, 4, D), np.float32)
    kp = np.zeros((1, B, 2 * TP, NH, HD), np.float32)
    vp = np.zeros_like(kp)
    ks = np.zeros((1, 128, 4, NH, HD), np.float32)
    vs = np.zeros_like(ks)
    cp = np.zeros((1, B, 30, CW), np.float32)
    cs = np.zeros((1, 128, 30, CW), np.float32)
    mk = np.zeros((1, B, ML, MH, HD), np.float32)
    mv = np.zeros_like(mk)
    for c, r in enumerate(res):
        b, half = c // 2, c % 2
        sl = slice(half * TP, (half + 1) * TP)
        ss = slice(c * NSAMP, (c + 1) * NSAMP)
        yp[b, sl] = r["y_own"]
        ys[ss] = r["y_s"].reshape(NSAMP, 4, D)
        kp[0, b, sl] = r["kp"].reshape(TP, NH, HD)
        vp[0, b, sl] = r["vp"].reshape(TP, NH, HD)
        ks[0, ss] = r["ks"].reshape(NSAMP, 4, NH, HD)
        vs[0, ss] = r["vs"].reshape(NSAMP, 4, NH, HD)
        cs[0, ss] = r["convs"]
        if half == 1:
            cp[0, b] = r["convp"]
        else:
            mk[0, b] = r["memk"].reshape(ML, MH, HD)
            mv[0, b] = r["memv"].reshape(ML, MH, HD)
    return (yp, ys, kp, vp, ks, vs, cp, cs, mk, mv)


def kernel(**inputs):
    npool = int(np.asarray(inputs["cache_sb_k"]).shape[1])
    nc = build_nc(npool)
    in_maps = make_in_maps(inputs)
    res = run_bass_kernel_spmd(nc, in_maps, core_ids=list(range(8)))
    return assemble(res.results)
```
